# Optimizing a Trainium2 kernel written in Bass

```python
import math
import jax, jax.numpy as jnp
from jax import lax
import numpy as np

D_MODEL = 2048
BATCH = 2
SEQ = 8192
DEPTH = 2
DEC_BATCH = 32
DEC_SEQ = 64
PAST_LEN = 1024

CHUNK = 64
D_PLE = 256
EPS = 1e-6

POOL_WIDTH = 1024
POOL_WINDOWS = (2, 4, 8, 16)
POOL_GROUPS = 4
POOL_GROUP_DIM = POOL_WIDTH // POOL_GROUPS
POOL_BUF = 15

ML_HEADS = 4
ML_HEAD_DIM = 256
ML_WIDTH = ML_HEADS * ML_HEAD_DIM

ATT_HEADS = 8
ATT_HEAD_DIM = 128
ATT_WIDTH = ATT_HEADS * ATT_HEAD_DIM
ATT_LEFT_CHUNKS = 8
ATT_WINDOW = ATT_LEFT_CHUNKS * CHUNK
REL_CLIP = 256
KV_LEN = min(ATT_WINDOW, PAST_LEN)

N_BRANCH = 3
BRANCH_WIDTH = 1024

IN_SIZES = (POOL_WIDTH, POOL_WIDTH,
            ML_WIDTH, ML_WIDTH, ML_WIDTH, ML_WIDTH, ML_WIDTH,
            ML_HEADS, ML_HEADS,
            ATT_WIDTH, ATT_WIDTH, ATT_WIDTH, ATT_WIDTH,
            N_BRANCH * D_MODEL)
IN_WIDTH = 17416

kernel_name = 'hybrid_pool_mlstm_chunkattn_step'


def _rmsnorm(x, g):
    xf = x.astype(jnp.float32)
    y = xf * lax.rsqrt(jnp.mean(xf * xf, axis=-1, keepdims=True) + EPS)
    return (y * g.astype(jnp.float32)).astype(x.dtype)


def _split_points():
    pts, acc = [], 0
    for s in IN_SIZES[:-1]:
        acc += s
        pts.append(acc)
    return pts


def _pool_mix(u, hist, pos0, w_group, scale):
    B, T, P = u.shape
    ext = jnp.concatenate([hist.astype(u.dtype), u], axis=1).astype(jnp.float32)
    cs = jnp.cumsum(ext, axis=1)
    cs = jnp.concatenate([jnp.zeros_like(cs[:, :1]), cs], axis=1)
    end = cs[:, POOL_BUF + 1:]
    pos = pos0 + jnp.arange(T)
    means = []
    for gi, w in enumerate(POOL_WINDOWS):
        sl = slice(gi * POOL_GROUP_DIM, (gi + 1) * POOL_GROUP_DIM)
        s = end[..., sl] - cs[:, POOL_BUF + 1 - w:POOL_BUF + 1 - w + T, sl]
        cnt = jnp.minimum(pos + 1, w).astype(jnp.float32)
        means.append(s / cnt[None, :, None])
    m = jnp.concatenate(means, axis=-1).astype(u.dtype) - u
    y = jnp.einsum('btgc,gcd->btgd', m.reshape(B, T, POOL_GROUPS, POOL_GROUP_DIM), w_group)
    return y.reshape(B, T, P) * scale


def _mlstm_chunk(carry, inp):
    c, n, m = carry
    q, k, v, ig, lf = inp
    L = q.shape[2]
    a = jnp.cumsum(lf, axis=-1)
    b = a[..., -1]
    causal = jnp.tril(jnp.ones((L, L), dtype=bool))
    logd = jnp.where(causal, a[..., :, None] - a[..., None, :] + ig[..., None, :], -jnp.inf)
    inter = a + m[..., None]
    m_row = jnp.maximum(inter, jnp.max(logd, axis=-1))
    dmat = jnp.exp(logd - m_row[..., None])
    w_inter = jnp.exp(inter - m_row)
    s = jnp.einsum('bhtd,bhsd->bhts', q, k) * dmat
    num = jnp.einsum('bhts,bhsd->bhtd', s, v) + w_inter[..., None] * jnp.einsum('bhvk,bhtk->bhtv', c, q)
    den = jnp.sum(s, axis=-1) + w_inter * jnp.einsum('bhk,bhtk->bht', n, q)
    h = num / jnp.maximum(jnp.abs(den), jnp.exp(-m_row))[..., None]
    g = b[..., None] - a + ig
    m_new = jnp.maximum(b + m, jnp.max(g, axis=-1))
    wk = jnp.exp(g - m_new[..., None])
    decay = jnp.exp(b + m - m_new)
    c_new = decay[..., None, None] * c + jnp.einsum('bhsv,bhsk->bhvk', v * wk[..., None], k)
    n_new = decay[..., None] * n + jnp.einsum('bhs,bhsk->bhk', wk, k)
    return (c_new, n_new, m_new), h


def _mlstm(q, k, v, ig, fg, c0, n0, m0):
    B, T, _ = q.shape
    L = min(CHUNK, T)
    NC = T // L
    f32 = jnp.float32

    def heads(t):
        return t.astype(f32).reshape(B, NC, L, ML_HEADS, ML_HEAD_DIM).transpose(1, 0, 3, 2, 4)

    def gates(t):
        return t.astype(f32).reshape(B, NC, L, ML_HEADS).transpose(1, 0, 3, 2)

    xs = (heads(q), heads(k) * (ML_HEAD_DIM ** -0.5), heads(v), gates(ig), jax.nn.log_sigmoid(gates(fg)))
    (c1, n1, m1), hs = lax.scan(_mlstm_chunk, (c0.astype(f32), n0.astype(f32), m0.astype(f32)), xs)
    h = hs.transpose(1, 0, 3, 2, 4).reshape(B, T, ML_HEADS, ML_HEAD_DIM)
    return h.astype(q.dtype), (c1, n1, m1)


def _band_attend(q, k, v, q_pos, k_pos, k_valid, rel_bias):
    s = jnp.einsum('bqhd,bkhd->bhqk', q, k).astype(jnp.float32) * (ATT_HEAD_DIM ** -0.5)
    dist = jnp.clip(q_pos[:, None] - k_pos[None, :], -REL_CLIP, REL_CLIP) + REL_CLIP
    s = s + rel_bias[:, dist].astype(jnp.float32)[None]
    s = jnp.where(k_valid[None, None, None, :], s, -jnp.inf)
    p = jax.nn.softmax(s, axis=-1).astype(v.dtype)
    return jnp.einsum('bhqk,bkhd->bqhd', p, v)


def _chunk_band_prompt(q, k, v, rel_bias):
    B, T, H, D = q.shape
    band = ATT_WINDOW + CHUNK
    pad = ((0, 0), (ATT_WINDOW, 0), (0, 0), (0, 0))
    kpad, vpad = jnp.pad(k, pad), jnp.pad(v, pad)

    def one_chunk(ci):
        start = ci * CHUNK
        qc = lax.dynamic_slice_in_dim(q, start, CHUNK, axis=1)
        kc = lax.dynamic_slice_in_dim(kpad, start, band, axis=1)
        vc = lax.dynamic_slice_in_dim(vpad, start, band, axis=1)
        q_pos = start + jnp.arange(CHUNK)
        k_pos = start - ATT_WINDOW + jnp.arange(band)
        return _band_attend(qc, kc, vc, q_pos, k_pos, k_pos >= 0, rel_bias)

    o = lax.map(one_chunk, jnp.arange(T // CHUNK))
    return jnp.moveaxis(o, 0, 1).reshape(B, T, H, D)


def _layer(x, p, pool_hist, c0, n0, m0, kv_cache, pos0,
           norm_g, w_in, w_pool_group, pool_scale, b_ig, b_fg, ml_head_norm,
           att_q_norm, att_k_norm, att_rel_bias, w_branch, w_out, ple_norm, w_ple_gate, w_ple_proj):
    B, T, _ = x.shape
    h = _rmsnorm(x, norm_g)
    proj = jnp.einsum('btd,dn->btn', h, w_in)
    (pu, pz, mq, mk, mv, mo, mz, mi, mf, aq, ak, av, az, gts) = jnp.split(proj, _split_points(), axis=-1)

    y_pool = _pool_mix(pu, pool_hist, pos0, w_pool_group, pool_scale) * jax.nn.silu(pz)
    new_pool = jnp.concatenate([pool_hist.astype(pu.dtype), pu], axis=1)[:, -POOL_BUF:]

    hm, (c1, n1, m1) = _mlstm(mq, mk, mv, mi + b_ig, mf + b_fg, c0, n0, m0)
    hm = hm * jax.nn.sigmoid(mo).reshape(B, T, ML_HEADS, ML_HEAD_DIM)
    hm = _rmsnorm(hm, ml_head_norm.reshape(ML_HEADS, ML_HEAD_DIM)).reshape(B, T, ML_WIDTH)
    y_ml = hm * jax.nn.silu(mz)

    qh = _rmsnorm(aq.reshape(B, T, ATT_HEADS, ATT_HEAD_DIM), att_q_norm)
    kh = _rmsnorm(ak.reshape(B, T, ATT_HEADS, ATT_HEAD_DIM), att_k_norm)
    vh = av.reshape(B, T, ATT_HEADS, ATT_HEAD_DIM)
    if kv_cache is None:
        ya = _chunk_band_prompt(qh, kh, vh, att_rel_bias)
        keep = min(ATT_WINDOW, T)
        new_k, new_v = kh[:, T - keep:], vh[:, T - keep:]
    else:
        ck, cv = kv_cache
        P = ck.shape[1]
        kk = jnp.concatenate([ck.astype(kh.dtype), kh], axis=1)
        vv = jnp.concatenate([cv.astype(vh.dtype), vh], axis=1)
        k_pos = jnp.concatenate([pos0 - P + jnp.arange(P), pos0 + jnp.arange(T)])
        q_pos = pos0 + jnp.arange(T)
        ya = _band_attend(qh, kk, vv, q_pos, k_pos, jnp.ones((P + T,), dtype=bool), att_rel_bias)
        new_k, new_v = kh, vh
    y_att = ya.reshape(B, T, ATT_WIDTH) * jax.nn.silu(az)

    g = jax.nn.sigmoid(gts.reshape(B, T, N_BRANCH, D_MODEL))
    merged = g[:, :, 0] * jnp.einsum('btc,cd->btd', y_pool, w_branch[0])
    merged = merged + g[:, :, 1] * jnp.einsum('btc,cd->btd', y_ml, w_branch[1])
    merged = merged + g[:, :, 2] * jnp.einsum('btc,cd->btd', y_att, w_branch[2])
    x = x + jnp.einsum('btd,de->bte', merged, w_out)

    pg = jax.nn.sigmoid(jnp.einsum('btd,de->bte', _rmsnorm(x, ple_norm), w_ple_gate))
    x = x + pg * jnp.einsum('btp,pd->btd', p, w_ple_proj)
    return x, (new_pool, c1, n1, m1, new_k, new_v)


def setup_inputs(seed: int = 0) -> dict:
    key = jax.random.key(seed)
    ks = jax.random.split(key, 32)
    f32 = jnp.float32

    def nrm(k, shape, s):
        return jax.random.normal(k, shape, f32) * s

    def gain(k, shape):
        return 1.0 + 0.02 * jax.random.normal(k, shape, f32)

    return {
        'x_prompt': nrm(ks[0], (BATCH, SEQ, D_MODEL), 1.0),
        'x_sample': nrm(ks[1], (DEC_BATCH, DEC_SEQ, D_MODEL), 1.0),
        'cache_att_k': nrm(ks[2], (DEPTH, DEC_BATCH, KV_LEN, ATT_HEADS, ATT_HEAD_DIM), 1.0),
        'cache_att_v': nrm(ks[3], (DEPTH, DEC_BATCH, KV_LEN, ATT_HEADS, ATT_HEAD_DIM), 1.0),
        'state_pool': nrm(ks[4], (DEPTH, DEC_BATCH, POOL_BUF, POOL_WIDTH), 1.0),
        'state_mlstm_c': nrm(ks[5], (DEPTH, DEC_BATCH, ML_HEADS, ML_HEAD_DIM, ML_HEAD_DIM), 0.1),
        'state_mlstm_n': nrm(ks[6], (DEPTH, DEC_BATCH, ML_HEADS, ML_HEAD_DIM), 0.1),
        'state_mlstm_m': nrm(ks[7], (DEPTH, DEC_BATCH, ML_HEADS), 0.5),
        'p_prompt': nrm(ks[8], (DEPTH, BATCH, SEQ, D_PLE), 1.0),
        'p_sample': nrm(ks[9], (DEPTH, DEC_BATCH, DEC_SEQ, D_PLE), 1.0),
        'norm_mix': gain(ks[10], (DEPTH, D_MODEL)),
        'w_in': nrm(ks[11], (DEPTH, D_MODEL, IN_WIDTH), D_MODEL ** -0.5),
        'w_pool_group': nrm(ks[12], (DEPTH, POOL_GROUPS, POOL_GROUP_DIM, POOL_GROUP_DIM), POOL_GROUP_DIM ** -0.5),
        'pool_scale': gain(ks[13], (DEPTH, POOL_WIDTH)),
        'b_ig': nrm(ks[14], (DEPTH, ML_HEADS), 0.1),
        'b_fg': jnp.linspace(3.0, 6.0, ML_HEADS, dtype=f32)[None] + nrm(ks[15], (DEPTH, ML_HEADS), 0.1),
        'ml_head_norm': gain(ks[16], (DEPTH, ML_WIDTH)),
        'att_q_norm': gain(ks[17], (DEPTH, ATT_HEAD_DIM)),
        'att_k_norm': gain(ks[18], (DEPTH, ATT_HEAD_DIM)),
        'att_rel_bias': nrm(ks[19], (DEPTH, ATT_HEADS, 2 * REL_CLIP + 1), 0.5),
        'w_branch': nrm(ks[20], (DEPTH, N_BRANCH, BRANCH_WIDTH, D_MODEL), BRANCH_WIDTH ** -0.5),
        'w_out': nrm(ks[21], (DEPTH, D_MODEL, D_MODEL), D_MODEL ** -0.5),
        'ple_norm': gain(ks[22], (DEPTH, D_MODEL)),
        'w_ple_gate': nrm(ks[23], (DEPTH, D_MODEL, D_MODEL), D_MODEL ** -0.5),
        'w_ple_proj': nrm(ks[24], (DEPTH, D_PLE, D_MODEL), D_PLE ** -0.5),
    }


def reference(x_prompt, x_sample, cache_att_k, cache_att_v, state_pool, state_mlstm_c,
              state_mlstm_n, state_mlstm_m, p_prompt, p_sample, norm_mix, w_in, w_pool_group,
              pool_scale, b_ig, b_fg, ml_head_norm, att_q_norm, att_k_norm, att_rel_bias,
              w_branch, w_out, ple_norm, w_ple_gate, w_ple_proj):
    xp, xs = x_prompt, x_sample
    Bp = x_prompt.shape[0]
    sp = [[] for _ in range(6)]
    ss = [[] for _ in range(6)]
    for i in range(DEPTH):
        lw = (norm_mix[i], w_in[i], w_pool_group[i], pool_scale[i], b_ig[i], b_fg[i], ml_head_norm[i],
              att_q_norm[i], att_k_norm[i], att_rel_bias[i], w_branch[i], w_out[i], ple_norm[i],
              w_ple_gate[i], w_ple_proj[i])
        hist0 = jnp.zeros((Bp, POOL_BUF, POOL_WIDTH), xp.dtype)
        c0 = jnp.zeros((Bp, ML_HEADS, ML_HEAD_DIM, ML_HEAD_DIM), jnp.float32)
        n0 = jnp.zeros((Bp, ML_HEADS, ML_HEAD_DIM), jnp.float32)
        m0 = jnp.zeros((Bp, ML_HEADS), jnp.float32)
        xp, st_p = _layer(xp, p_prompt[i], hist0, c0, n0, m0, None, 0, *lw)
        xs, st_s = _layer(xs, p_sample[i], state_pool[i], state_mlstm_c[i], state_mlstm_n[i],
                          state_mlstm_m[i], (cache_att_k[i], cache_att_v[i]), PAST_LEN, *lw)
        for j in range(6):
            sp[j].append(st_p[j])
            ss[j].append(st_s[j])
    pool_p, c_p, n_p, m_p, k_p, v_p = [jnp.stack(a) for a in sp]
    pool_s, c_s, n_s, m_s, k_s, v_s = [jnp.stack(a) for a in ss]
    return (xp, xs, pool_p, pool_s, c_p, c_s, n_p, n_s, m_p, m_s, k_p, k_s, v_p, v_s)
```

```python
import contextlib
import numpy as np
import concourse.bass as bass
import concourse.mybir as mybir
from concourse.bass_utils import run_bass_kernel_spmd

F32 = mybir.dt.float32
BF16 = mybir.dt.bfloat16
ALU = mybir.AluOpType
AF = mybir.ActivationFunctionType
AX = mybir.AxisListType

D = 2048
DEPTH = 2
EPS = 1e-6
NEG = -30000.0
EPOCH = 30000
NTM = 2306
NFM = 768


class Buf:
    __slots__ = ("name", "w", "r")

    def __init__(self, name=""):
        self.name = name
        self.w = None
        self.r = {}

    def reset(self):
        self.w = None
        self.r = {}


class Ctx:
    def __init__(self, nc, stack):
        self.nc = nc
        self.stack = stack
        self.nsem = 0
        self.dsems = []
        self.qs = []

    def new_sem(self, name):
        self.nsem += 1
        return self.stack.enter_context(self.nc.semaphore(f"{name}_{self.nsem}"))


class DmaSem:
    def __init__(self, ctx, name):
        self.ctx = ctx
        self.name = name
        self.sem = ctx.new_sem(name)
        self.val = 0
        self.key = (id(self), 0)
        self.ep = 0
        ctx.dsems.append(self)

    def bump(self):
        if self.val + 16 > EPOCH:
            self.sem = self.ctx.new_sem(self.name)
            self.val = 0
            self.ep += 1
            self.key = (id(self), self.ep)
        self.val += 16
        return (self.key, self.sem, self.val)

    def last(self):
        return (self.key, self.sem, self.val) if self.val else None


class Q:
    def __init__(self, ctx, name):
        self.ctx = ctx
        self.name = name
        self.sem = ctx.new_sem(name)
        self.key = (name, 0)
        self.epoch = 0
        self.count = 0
        self.pending = False
        self.waited = {}
        self.prog = []
        self.log = []
        self.n_ins = 0
        self.n_wait = 0
        ctx.qs.append(self)

    def _need(self, reads, writes):
        need = {}

        def add(tok):
            k, s, v = tok
            if self.waited.get(k, 0) >= v:
                return
            if k not in need or need[k][2] < v:
                need[k] = tok

        for b in reads:
            if b.w is not None:
                add(b.w)
        for b in writes:
            if b.w is not None and (b.w[0] != self.key or self.name != "pe"):
                add(b.w)
            for k, tok in b.r.items():
                if k != self.key:
                    add(tok)
        return list(need.values())

    def _emit_waits(self, need):
        for (k, s, v) in need:
            self.prog.append(lambda e, s=s, v=v: e.wait_ge(s, v))
            self.waited[k] = v
            self.n_wait += 1
            self.log.append(f"  wait {k} >= {v}")

    def _mark(self, tok, reads, writes):
        for b in reads:
            old = b.r.get(tok[0])
            if old is None or old[2] < tok[2]:
                b.r[tok[0]] = tok
        for b in writes:
            b.w = tok
            b.r = {}

    def op(self, meth, reads=(), writes=(), signal=True, **kw):
        self._emit_waits(self._need(reads, writes))
        if self.count + 1 > EPOCH and not self.pending:
            self.epoch += 1
            self.sem = self.ctx.new_sem(self.name)
            self.key = (self.name, self.epoch)
            self.count = 0
        self.n_ins += 1
        self.log.append(f"{meth} W={[b.name for b in writes]} R={[b.name for b in reads]} sig={signal} cnt={self.count + (1 if signal else 0)}")
        if signal:
            self.prog.append(lambda e, meth=meth, kw=kw, sem=self.sem: getattr(e, meth)(**kw).then_inc(sem, 1))
            self.count += 1
            self.pending = False
            tok = (self.key, self.sem, self.count)
        else:
            self.prog.append(lambda e, meth=meth, kw=kw: getattr(e, meth)(**kw))
            self.pending = True
            tok = (self.key, self.sem, self.count + 1)
        self._mark(tok, reads, writes)
        return tok

    def dma(self, dsem, out, in_, reads=(), writes=(), fn=None, **kw):
        need = [t for t in self._need(reads, writes) if t[0] != dsem.key]
        self._emit_waits(need)
        tok = dsem.bump()
        self.log.append(f"DMA {dsem.name} -> {tok[2]} W={[b.name for b in writes]} R={[b.name for b in reads]}")
        if fn is None:
            self.prog.append(lambda e, out=out, in_=in_, kw=kw, sem=tok[1]:
                             e.dma_start(out=out, in_=in_, **kw).then_inc(sem, 16))
        else:
            self.prog.append(lambda e, fn=fn, sem=tok[1]: fn(e).then_inc(sem, 16))
        self.n_ins += 1
        self._mark(tok, reads, writes)
        return tok

    def wait_tok(self, tok):
        if tok is None:
            return
        if self.waited.get(tok[0], 0) < tok[2]:
            self.prog.append(lambda e, s=tok[1], v=tok[2]: e.wait_ge(s, v))
            self.waited[tok[0]] = tok[2]
            self.n_wait += 1

    def last(self):
        assert not self.pending
        return (self.key, self.sem, self.count) if self.count else None


class Arena:
    def __init__(self, nc, nbytes):
        self.nc = nc
        h = nc.alloc_sbuf_tensor("arena", [128, nbytes // 2], BF16)
        self.base = nc.lookup_mloc(h).addr
        self.size = nbytes
        self.off = 0
        self.n = 0

    def alloc(self, name, shape, dt):
        nb = int(np.prod(shape[1:])) * (4 if dt == F32 else 2)
        nb = (nb + 31) // 32 * 32
        assert self.off + nb <= self.size, f"SBUF arena overflow at {name}: {self.off}+{nb} > {self.size}"
        self.n += 1
        t = self.nc.alloc_sbuf_tensor_at(f"{name}_{self.n}", list(shape), dt, offset=self.base + self.off)
        self.off += nb
        return t

    def mark(self):
        return self.off

    def release(self, m):
        self.off = m


class Cfg:
    def __init__(self, NG=2, NPT=16, NSR=4, NH=2):
        self.NG = NG
        self.NPT = NPT
        self.NSR = NSR
        self.NST = NSR // 2
        self.TPR = NPT + self.NST
        self.TR = 128 * self.TPR
        self.GT = 4 * self.TR
        self.NS = 4 * NSR
        self.NPTILES = 4 * NPT
        self.NH = NH
        assert self.TPR % NH == 0
        self.HT = self.TPR // NH
        self.HTOK = self.HT * 128
        for tg in (512, 384, 256, 128):
            if self.HTOK % tg == 0:
                self.TG = tg
                break
        self.NTG = self.HTOK // self.TG
        self.SEQ = self.NPTILES * 128
        self.NCORES = 4 * NG
        self.NKV = 512 + self.NS * 64
        self.stop_after = None
        self.no_cc = False
        self.max_tiles = None
        self.dbg = set()


def build_program(cfg):
    nc = bass.Bass("TRN2", target_bir_lowering=False)
    TR, GT, NS, TPR = cfg.TR, cfg.GT, cfg.NS, cfg.TPR

    def din(name, shape):
        return nc.dram_tensor(name, list(shape), F32, kind="ExternalInput").ap()

    def dout(name, shape):
        return nc.dram_tensor(name, list(shape), F32, kind="ExternalOutput").ap()

    xg = din("xg", [GT, D])
    xown = din("xown", [TR, D])
    pown = din("pown", [DEPTH, TR, 256])
    w_tm = din("w_tm", [DEPTH, D, NTM])
    w_fm = din("w_fm", [DEPTH, D, NFM])
    w_gate = din("w_gate", [DEPTH, 3, 16, D, 128])
    w_br = din("w_br", [DEPTH, 3, 16, 1024, 128])
    w_out = din("w_out", [DEPTH, 4, D, 512])
    w_pg = din("w_pg", [DEPTH, 4, D, 512])
    w_pp = din("w_pp", [DEPTH, 4, 256, 512])
    gn = din("gn", [DEPTH, D])
    gp = din("gp", [DEPTH, D])
    wpool = din("wpool", [DEPTH, 256, 256])
    pscale = din("pscale", [DEPTH, 128, 2])
    psel = din("psel", [128, 4])
    prc = din("prc", [2, 128, 128])
    bgate = din("bgate", [DEPTH, 128, 2])
    mlg = din("mlg", [DEPTH, 256])
    aqg = din("aqg", [DEPTH, 256])
    akg = din("akg", [DEPTH, 256])
    abias = din("abias", [DEPTH, 128, 12, 128])
    cmask = din("cmask", [128, 7, 128])
    spool = din("spool", [DEPTH, 128, NS // 2, 2, 2, 16])
    sC = din("sC", [DEPTH, NS, 2, 128, 256])
    sn = din("sn", [DEPTH, NS, 128, 2])
    sm = din("sm", [DEPTH, 128, NS])
    skT = din("skT", [DEPTH, NS, 2, 128, 512])
    sv = din("sv", [DEPTH, NS, 512, 2, 128])

    y_own = dout("y_own", [TR, D])
    o_pool = dout("o_pool", [DEPTH, 1 + NS, 15, 256])
    o_c = dout("o_c", [DEPTH, 1 + NS, 256, 256])
    o_n = dout("o_n", [DEPTH, 1 + NS, 256])
    o_m = dout("o_m", [DEPTH, 1 + NS])
    o_k = dout("o_k", [DEPTH, cfg.NKV, 256])
    o_v = dout("o_v", [DEPTH, cfg.NKV, 256])

    assert TPR % 3 == 0 and cfg.TG == 384
    NCH = GT // 384
    CPR = TPR // 3
    ysrc = nc.dram_tensor("ysrc", [NCH, 768, 384], BF16).ap()
    ydst = nc.dram_tensor("ydst", [NCH, 4 * 768, 384], BF16).ap()
    xnew = nc.dram_tensor("xnew", [TR, D], F32).ap()
    pscr = nc.dram_tensor("pscr", [2, 128, 256], F32).ap()
    xown1 = nc.dram_tensor("xown1", [TR, D], F32).ap()
    xg1 = nc.dram_tensor("xg1", [TPR, 4 * 128, D], F32).ap()
    rgroups = [[4 * g + i for i in range(4)] for g in range(cfg.NG)]

    with contextlib.ExitStack() as st:
        cx = Ctx(nc, st)
        pe, act, dve, pool, sp = Q(cx, "pe"), Q(cx, "act"), Q(cx, "dve"), Q(cx, "pool"), Q(cx, "sp")
        ccsem = cx.new_sem("cc")
        ccstate = {"n": 0}
        ar = Arena(nc, 212800)
        allbufs = []

        def B(name=""):
            b = Buf(name)
            allbufs.append(b)
            return b

        pb0 = nc.alloc_psum_tensor("pb0", [128, 1024], BF16)
        pb1 = nc.alloc_psum_tensor("pb1", [128, 1024], BF16)
        pbs = [None, None] + [nc.alloc_psum_tensor(f"pb{i}", [128, 512], F32) for i in range(2, 8)]
        b_tr = [B("tr0"), B("tr1")]
        trh = [pb0[:, 0:512].rearrange("p (a b) -> p a b", b=128),
               pb1[:, 0:512].rearrange("p (a b) -> p a b", b=128)]
        trstate = {"i": 0}

        def next_tr():
            i = trstate["i"]
            trstate["i"] = 1 - i
            return trh[i], b_tr[i]

        cm = ar.alloc("cm", [128, 7, 128], F32); b_cm = B("cm")
        identb = ar.alloc("identb", [128, 128], BF16); b_identb = B("identb")
        gnb = ar.alloc("gnb", [128, D], F32); b_gnb = B("gnb")
        xs = [ar.alloc("xs0", [128, D], F32), ar.alloc("xs1", [128, D], F32)]
        b_xs = [B("xs0"), B("xs1")]
        s_xs = [DmaSem(cx, "sxs0"), DmaSem(cx, "sxs1")]
        hb = ar.alloc("hb", [128, D], BF16); b_hb = B("hb")
        junk = hb; b_junk = b_hb
        nst = ar.alloc("nst", [128, 8], F32); b_nst = B("nst")
        s_c = DmaSem(cx, "sconst")
        IDENT, TRIL, TRILBD, ONESBD, SELLO, SELHI, ONES = range(7)

        sp.dma(s_c, cm[:], cmask, writes=[b_cm])
        dve.op("tensor_copy", out=identb[:], in_=cm[:, IDENT, :], reads=[b_cm], writes=[b_identb])

        bsem = cx.new_sem("bar")
        bstate = {"n": 0}

        def issue_collective(src, dst):
            ccstate["n"] += 1
            if not cfg.no_cc:
                pool.prog.append(lambda e, src=src, dst=dst: e.collective_compute(
                    "AllGather", ALU.bypass, replica_groups=rgroups, ins=[src], outs=[dst]).then_inc(ccsem, 1))
            else:
                pool.prog.append(lambda e: e.sem_inc(ccsem, 1))

        def barrier():
            for q in cx.qs:
                if q is not pool:
                    pool.wait_tok(q.last())
            for ds in cx.dsems:
                pool.wait_tok(ds.last())
            if ccstate["n"]:
                pool.wait_tok((("cc", 0), ccsem, ccstate["n"]))
            bstate["n"] += 1
            pool.prog.append(lambda e: e.sem_inc(bsem, 1))
            tok = (("bar", 0), bsem, bstate["n"])
            for q in cx.qs:
                q.wait_tok(tok)
            for b in allbufs:
                b.reset()

        def norm_tile(xt, b_xt, gtile, b_g):
            act.op("activation", out=junk[:], in_=xt[:], func=AF.Square, accum_out=nst[:, 0:1],
                   reads=[b_xt], writes=[b_junk, b_nst])
            act.op("activation", out=nst[:, 1:2], in_=nst[:, 0:1], func=AF.Sqrt, scale=1.0 / D, bias=EPS,
                   reads=[b_nst], writes=[b_nst])
            dve.op("reciprocal", out=nst[:, 2:3], in_=nst[:, 1:2], reads=[b_nst], writes=[b_nst])
            dve.op("scalar_tensor_tensor", out=hb[:], in0=xt[:], scalar=nst[:, 2:3], in1=gtile[:],
                   op0=ALU.mult, op1=ALU.mult, reads=[b_xt, b_nst, b_g], writes=[b_hb])

        evac_flip = {"i": 0}

        def evac_copy(out, in_, reads, writes):
            evac_flip["i"] ^= 1
            if evac_flip["i"]:
                act.op("copy", out=out, in_=in_, reads=reads, writes=writes)
            else:
                dve.op("tensor_copy", out=out, in_=in_, reads=reads, writes=writes)

        def transpose_to(dst3, b_dst, src, b_src, nblk):
            i = 0
            while i < nblk:
                n = min(4, nblk - i)
                tr, btr = next_tr()
                for k in range(n):
                    pe.op("transpose", out=tr[:, k, :], in_=src[:, (i + k) * 128:(i + k + 1) * 128],
                          identity=identb[:], reads=[b_src, b_identb], writes=[btr], signal=(k == n - 1))
                evac_copy(dst3[:, i:i + n, :], tr[:, 0:n, :], [btr], [b_dst])
                i += n

        m_persist = ar.mark()

        for layer in range(DEPTH):
            xin = xg if layer == 0 else xg1
            xsrc_own = xown if layer == 0 else xown1
            xdest = xown1 if layer == 0 else y_own

            ar.release(m_persist)
            Wtm = ar.alloc("Wtm", [128, 16, NTM], BF16); b_Wtm = B("Wtm")
            Wfm = ar.alloc("Wfm", [128, 16, NFM], BF16); b_Wfm = B("Wfm")
            wpl = ar.alloc("wpl", [128, 2, 256], BF16); b_wpl = B("wpl")
            s_w = DmaSem(cx, "sw"); s_w2 = DmaSem(cx, "sw2"); s_w3 = DmaSem(cx, "sw3")
            for kq in range(4):
                pool.dma(s_w, Wtm[:, kq * 4:(kq + 1) * 4, :],
                         w_tm[layer, kq * 512:(kq + 1) * 512, :].rearrange("(k p) c -> p k c", p=128), writes=[b_Wtm])
            for kq in range(2):
                pool.dma(s_w2, Wfm[:, kq * 8:(kq + 1) * 8, :],
                         w_fm[layer, kq * 1024:(kq + 1) * 1024, :].rearrange("(k p) c -> p k c", p=128), writes=[b_Wfm])
            pool.dma(s_w3, wpl[:], wpool[layer].rearrange("(k p) c -> p k c", p=128), writes=[b_wpl])
            s_p = DmaSem(cx, "sparam")
            sp.dma(s_p, gnb[:], gn[layer].partition_broadcast(128), writes=[b_gnb])
            psc = ar.alloc("psc", [128, 2], F32); b_psc = B()
            pse = ar.alloc("pse", [128, 4], F32); b_pse = B()
            prc_t = ar.alloc("prc", [128, 2, 128], F32); b_prc = B()
            bgt = ar.alloc("bgt", [128, 2], F32); b_bgt = B()
            mlgt = ar.alloc("mlgt", [128, 256], F32); b_mlgt = B()
            aqgt = ar.alloc("aqgt", [128, 256], F32); b_aqgt = B()
            akgt = ar.alloc("akgt", [128, 256], F32); b_akgt = B()
            smt = ar.alloc("smt", [128, NS], F32); b_smt = B()
            sp.dma(s_p, psc[:], pscale[layer], writes=[b_psc])
            sp.dma(s_p, pse[:], psel, writes=[b_pse])
            sp.dma(s_p, prc_t[:], prc.rearrange("k p t -> p k t"), writes=[b_prc])
            sp.dma(s_p, bgt[:], bgate[layer], writes=[b_bgt])
            sp.dma(s_p, mlgt[:], mlg[layer].partition_broadcast(128), writes=[b_mlgt])
            sp.dma(s_p, aqgt[:], aqg[layer].partition_broadcast(128), writes=[b_aqgt])
            sp.dma(s_p, akgt[:], akg[layer].partition_broadcast(128), writes=[b_akgt])
            sp.dma(s_p, smt[:], sm[layer], writes=[b_smt])
            for bb_ in (b_gnb, b_psc, b_pse, b_prc, b_bgt, b_mlgt, b_aqgt, b_akgt, b_smt):
                bb_.w = s_p.last()
            bhi = ar.alloc("bhi", [128, 12, 128], BF16); b_bhi = B()
            blo = ar.alloc("blo", [128, 12, 128], BF16); b_blo = B()
            stg = xs[1][:, 0:1536].rearrange("p (a b) -> p a b", b=128)
            sp.dma(s_xs[1], stg, abias[layer], writes=[b_xs[1]])
            dve.op("tensor_copy", out=bhi[:], in_=stg, reads=[b_xs[1]], writes=[b_bhi])
            dve.op("tensor_tensor", out=blo[:], in0=stg, in1=bhi[:], op=ALU.subtract,
                   reads=[b_xs[1], b_bhi], writes=[b_blo])

            hT0 = ar.alloc("hT0", [128, 16, 128], BF16)
            hT = [hT0, hT0]
            b_hT0 = B("hT")
            b_hT = [b_hT0, b_hT0]
            def dbl(name, shape, dt):
                return [ar.alloc(name + "0", shape, dt), ar.alloc(name + "1", shape, dt)], [B(name + "0"), B(name + "1")]
            gat, b_gat = dbl("gat", [128, 2], F32)
            pus, b_pus = dbl("pus", [128, 256], F32)
            puT, b_puT = dbl("puT", [128, 2, 128], F32)
            ktok, b_ktok = dbl("ktok", [128, 256], BF16)
            vsb, b_vsb = dbl("vsb", [128, 256], BF16)
            sgmo, b_sgmo = dbl("sgmo", [128, 256], BF16)
            slmz, b_slmz = dbl("slmz", [128, 256], BF16)
            aqs, b_aqs = dbl("aqs", [128, 256], F32)
            aks, b_aks = dbl("aks", [128, 256], F32)
            avs, b_avs = dbl("avs", [128, 256], F32)
            slaz, b_slaz = dbl("slaz", [128, 256], BF16)
            slpz, b_slpz = dbl("slpz", [128, 2, 128], BF16)
            qTm, b_qTm = dbl("qTm", [128, 2, 128], BF16)
            yst, b_yst = dbl("yst", [128, 2, 3, 128], BF16)
            s_yst = [DmaSem(cx, "syst0"), DmaSem(cx, "syst1")]
            khs, b_khs = dbl("khs", [128, 256], F32)
            s_out = [DmaSem(cx, "sout0"), DmaSem(cx, "sout1")]
            s_pscr = [DmaSem(cx, "spscr0"), DmaSem(cx, "spscr1")]
            qTz = [ar.alloc("qTz0", [128, 2, 128], BF16), ar.alloc("qTz1", [128, 2, 128], BF16)]
            b_qTz = [B(), B()]
            ext = ar.alloc("ext", [128, 2, 144], F32); b_ext = B("ext")
            exs = ar.alloc("exs", [128, 2, 2, 80], F32); b_exs = B("exs")
            s_exs = DmaSem(cx, "sexs")
            psA = ar.alloc("psA", [128, 2, 160], F32); psB = ar.alloc("psB", [128, 2, 160], F32)
            pacc = ar.alloc("pacc", [128, 2, 128], F32); b_pw = B("poolwork")
            hg = pacc[:].rearrange("p h t -> p (h t)"); b_hg = b_pw
            mT = ar.alloc("mT", [128, 2, 128], BF16); b_mT = B("mT")
            gw = ar.alloc("gw", [128, 16], F32); b_gw = B("gw")
            g0 = ar.alloc("g0", [1, 16], F32); b_g0 = B("g0")
            edb = ar.alloc("edb", [128, 1], BF16)
            vpx = ar.alloc("vpx", [128, 257], BF16); b_vpx = B("vpx")
            kTm = ar.alloc("kTm", [128, 2, 128], BF16); b_kTm = B("kTm")
            PTm = ar.alloc("PTm", [128, 128], BF16); b_PTm = B("PTm")
            Ch = ar.alloc("Ch", [128, 2, 257], F32); b_Ch = B("Ch")
            Chb = ar.alloc("Chb", [128, 2, 257], BF16); b_Chb = B("Chb")
            Cs = [ar.alloc("Cs0", [128, 2, 257], F32), ar.alloc("Cs1", [128, 2, 257], F32)]
            b_Cs = [B(), B()]
            s_Cs = [DmaSem(cx, "scs0"), DmaSem(cx, "scs1")]
            Csb = [ar.alloc("Csb0", [128, 2, 257], BF16), ar.alloc("Csb1", [128, 2, 257], BF16)]
            b_Csb = [B(), B()]
            cst = ar.alloc("cst", [128, 2, 257], F32); b_cst = B("cst")
            s_cst = DmaSem(cx, "scst")
            yml = ar.alloc("yml", [128, 256], BF16); b_yml = B("yml")
            yat = ar.alloc("yat", [128, 256], BF16); b_yat = B("yat")
            mrow = ar.alloc("mrow", [1, 1 + NS], F32); b_mrow = B("mrow")
            s_mrow = DmaSem(cx, "smrow")
            qhb = ar.alloc("qhb", [128, 256], BF16); b_qhb = B("qhb")
            khb = ar.alloc("khb", [128, 256], BF16); b_khb = B("khb")
            qTa = ar.alloc("qTa", [128, 2, 128], BF16); b_qTa = B("qTa")
            kring = ar.alloc("kring", [128, 8, 2, 128], BF16); b_kring = [B() for _ in range(8)]
            vring = ar.alloc("vring", [128, 8, 2, 129], BF16); b_vring = [B() for _ in range(8)]
            kTc = ar.alloc("kTc", [128, 2, 2, 512], BF16); b_kTc = [B(), B()]
            Vc = ar.alloc("Vc", [128, 2, 4, 2, 129], BF16); b_Vc = [B(), B()]
            s_kc = [DmaSem(cx, "skc0"), DmaSem(cx, "skc1")]
            PTp = ar.alloc("PTp", [128, 5, 128], BF16); b_PTp = B("PTp")
            PTs = [ar.alloc("PTs0", [128, 4, 128], BF16), ar.alloc("PTs1", [128, 4, 128], BF16)]
            b_PTs = [B(), B()]
            aw = ar.alloc("aw", [128, 16], F32); b_aw = B("aw")

            ptm = [pbs[2], pbs[3], pbs[4]]; b_ptm = [B("ptm0"), B("ptm1"), B("ptm2")]
            ptm_i = {"i": 0}

            def next_tm():
                i = ptm_i["i"]
                ptm_i["i"] = (i + 1) % 3
                return ptm[i], b_ptm[i]
            b_bank5 = B("bank5")
            pN = pbs[5][:, 0:257]; b_pN = b_bank5
            pS = pbs[5][:, 257:385]; b_pS = b_bank5
            pG = pbs[5][:, 385:401]; b_pG = b_bank5
            pAS = pbs[6]; b_pAS = B("pAS")
            b_bank7 = B("bank7")
            pPV = pbs[7][:, 0:258].rearrange("p (h c) -> p h c", c=129); b_pPV = b_bank7
            pA5 = pbs[7][:, 258:386]; b_pA5 = b_bank7

            dve.op("memset", ap=ext[:], constant=0.0, writes=[b_ext])
            dve.op("memset", ap=Ch[:], constant=0.0, writes=[b_Ch])
            dve.op("memset", ap=Chb[:], constant=0.0, writes=[b_Chb])
            dve.op("memset", ap=g0[:], constant=0.0, writes=[b_g0])
            dve.op("memset", ap=vring[:], constant=1.0, writes=b_vring)
            dve.op("memset", ap=Vc[:], constant=1.0, writes=b_Vc)
            for s in range(2):
                dve.op("memset", ap=qTz[s][:], constant=0.0, writes=[b_qTz[s]])
                dve.op("memset", ap=PTs[s][:], constant=0.0, writes=[b_PTs[s]])

            tiles = []
            for r in range(4):
                for l in range(TPR):
                    tiles.append((r, l))

            def load_x(T):
                b = T % 2
                r_, l_ = tiles[T]
                src_ = xg[T * 128:(T + 1) * 128, :] if layer == 0 else xg1[l_, r_ * 128:(r_ + 1) * 128, :]
                sp.dma(s_xs[b], xs[b][:], src_, writes=[b_xs[b]])

            def load_sample_state(u):
                for s in range(2):
                    q = 2 * u + s
                    pool.dma(s_kc[s], kTc[:, s, :, :], skT[layer, q].rearrange("h d k -> d h k"),
                             writes=[b_kTc[s]])
                    for h in range(2):
                        pool.dma(s_kc[s], Vc[:, s, :, h, 0:128],
                                 sv[layer, q, :, h, :].rearrange("(kb p) d -> p kb d", p=128), writes=[b_Vc[s]])
                    b_kTc[s].w = s_kc[s].last(); b_Vc[s].w = s_kc[s].last()
                    sp.dma(s_Cs[s], Cs[s][:, :, 0:256], sC[layer, q].rearrange("k p v -> p k v"), writes=[b_Cs[s]])
                    sp.dma(s_Cs[s], Cs[s][:, :, 256], sn[layer, q], writes=[b_Cs[s]], allow_slow_non_contiguous=True)
                sp.dma(s_exs, exs[:, :, :, 0:16], spool[layer, :, u], writes=[b_exs])

            if "sonly" in cfg.dbg:
                tiles = [t_ for t_ in tiles if t_[1] >= cfg.NPT]
            if "ponly" in cfg.dbg:
                tiles = [t_ for t_ in tiles if t_[1] < cfg.NPT]
            ytoks = []
            ntiles_run = len(tiles) if cfg.max_tiles is None else min(len(tiles), cfg.max_tiles)

            def tinfo(T):
                r, l = tiles[T]
                is_prompt = l < cfg.NPT
                p = r * cfg.NPT + l if is_prompt else None
                u = None if is_prompt else r * cfg.NST + (l - cfg.NPT)
                out_kv = (not is_prompt) or p >= cfg.NPTILES - 4
                out_pool = (not is_prompt) or p == cfg.NPTILES - 1
                if "noout" in cfg.dbg:
                    out_kv = out_pool = False
                if "allpool" in cfg.dbg:
                    out_pool = True
                slot = (p % 6) if is_prompt else 6 + (u % 2)
                return T % 2, is_prompt, p, u, out_kv, out_pool, slot

            def front(T):
                b, is_prompt, p, u, out_kv, out_pool, slot = tinfo(T)
                if T + 1 < ntiles_run:
                    load_x(T + 1)
                norm_tile(xs[b], b_xs[b], gnb, b_gnb)
                transpose_to(hT[b], b_hT[b], hb, b_hb, 16)

                def tm_block(c0, n):
                    ps, bps = next_tm()
                    for k in range(16):
                        pe.op("matmul", out=ps[:, 0:n], lhsT=hT[b][:, k, :], rhs=Wtm[:, k, c0:c0 + n],
                              start=(k == 0), stop=(k == 15), reads=[b_hT[b], b_Wtm], writes=[bps],
                              signal=(k == 15))
                    return ps, bps
                ps, bps = tm_block(2304, 2)
                dve.op("tensor_tensor", out=gat[b][:], in0=ps[:, 0:2], in1=bgt[:], op=ALU.add,
                       reads=[bps, b_bgt], writes=[b_gat[b]])
                ps, bps = tm_block(0, 512)
                if out_pool:
                    act.op("copy", out=pus[b][:], in_=ps[:, 0:256], reads=[bps], writes=[b_pus[b]])
                act.op("activation", out=ktok[b][:], in_=ps[:, 256:512], func=AF.Copy, scale=1.0 / 16.0,
                       reads=[bps], writes=[b_ktok[b]])
                ps, bps = tm_block(512, 512)
                dve.op("tensor_copy", out=vsb[b][:], in_=ps[:, 0:256], reads=[bps], writes=[b_vsb[b]])
                act.op("activation", out=sgmo[b][:], in_=ps[:, 256:512], func=AF.Sigmoid, reads=[bps], writes=[b_sgmo[b]])
                ps, bps = tm_block(1024, 512)
                act.op("activation", out=slmz[b][:], in_=ps[:, 0:256], func=AF.Silu, reads=[bps], writes=[b_slmz[b]])
                dve.op("tensor_copy", out=aqs[b][:], in_=ps[:, 256:512], reads=[bps], writes=[b_aqs[b]])
                ps, bps = tm_block(1536, 512)
                dve.op("tensor_copy", out=aks[b][:], in_=ps[:, 0:256], reads=[bps], writes=[b_aks[b]])
                dve.op("tensor_copy", out=vring[:, slot, :, 0:128], in_=ps[:, 256:512].rearrange("p (h d) -> p h d", d=128),
                       reads=[bps], writes=[b_vring[slot]])
                if out_kv:
                    dve.op("tensor_copy", out=avs[b][:], in_=ps[:, 256:512], reads=[bps], writes=[b_avs[b]])
                ps, bps = tm_block(2048, 256)
                act.op("activation", out=slaz[b][:], in_=ps[:, 0:256], func=AF.Silu, reads=[bps], writes=[b_slaz[b]])

                nseg, Ls = (1, 128) if is_prompt else (2, 64)
                for pair in range(3):
                    psf, bpsf = next_tm()
                    pfm = psf[:, 0:256].rearrange("p (a b) -> p a b", b=128)
                    for i2 in range(2):
                        cbk = pair * 2 + i2
                        for k in range(16):
                            pe.op("matmul", out=pfm[:, i2, :], lhsT=Wfm[:, k, cbk * 128:(cbk + 1) * 128],
                                  rhs=hT[b][:, k, :], start=(k == 0), stop=(k == 15),
                                  reads=[b_hT[b], b_Wfm], writes=[bpsf], signal=(k == 15 and i2 == 1))
                    src = pfm[:, 0:2, :]
                    if pair == 0:
                        dve.op("tensor_copy", out=puT[b][:], in_=src, reads=[bpsf], writes=[b_puT[b]])
                    elif pair == 1:
                        act.op("activation", out=slpz[b][:], in_=src, func=AF.Silu, reads=[bpsf], writes=[b_slpz[b]])
                    else:
                        act.op("copy", out=qTm[b][:], in_=src, reads=[bpsf], writes=[b_qTm[b]])

            def back(T):
                b, is_prompt, p, u, out_kv, out_pool, slot = tinfo(T)
                last_prompt = is_prompt and p == cfg.NPTILES - 1
                MASK = TRIL if is_prompt else TRILBD
                if not is_prompt and "nosload" not in cfg.dbg:
                    load_sample_state(u)
                if is_prompt:
                    dve.op("tensor_copy", out=ext[:, :, 16:144], in_=puT[b][:], reads=[b_puT[b]], writes=[b_ext])
                else:
                    for h in range(2):
                        dve.op("tensor_copy", out=exs[:, h, :, 16:80],
                               in_=puT[b][:, h, :].rearrange("p (s t) -> p s t", t=64),
                               reads=[b_puT[b]], writes=[b_exs])
                    for s in range(2):
                        dve.op("tensor_copy", out=qTz[s][:, :, 64 * s:64 * s + 64],
                               in_=qTm[b][:, :, 64 * s:64 * s + 64],
                               reads=[b_qTm[b]], writes=[b_qTz[s]])

                nseg_, Ls_ = (1, 128) if is_prompt else (2, 64)
                W = 16 + Ls_
                for sg_ in range(nseg_):
                    if is_prompt:
                        def sl(t, a, c, sg_=sg_):
                            return t[:, :, a:c]
                        E, bE = ext, b_ext
                        cur = ext[:, :, 16:144]
                        accv = pacc[:]
                        mTv = mT[:]
                    else:
                        def sl(t, a, c, sg_=sg_):
                            if t is exs:
                                return exs[:, :, sg_, a:c]
                            return t[:, :, sg_ * 80 + a:sg_ * 80 + c]
                        E, bE = exs, b_exs
                        cur = exs[:, :, sg_, 16:80]
                        accv = pacc[:, :, 64 * sg_:64 * sg_ + 64]
                        mTv = mT[:, :, 64 * sg_:64 * sg_ + 64]
                    rd = [bE, b_pw]
                    dve.op("tensor_tensor", out=sl(psA, 1, W), in0=sl(E, 1, W), in1=sl(E, 0, W - 1), op=ALU.add, reads=rd, writes=[b_pw])
                    dve.op("tensor_scalar", out=accv, in0=sl(psA, 16, W), scalar1=pse[:, 0:1], scalar2=None, op0=ALU.mult,
                           reads=[b_pw, b_pse], writes=[b_pw])
                    dve.op("tensor_tensor", out=sl(psB, 3, W), in0=sl(psA, 3, W), in1=sl(psA, 1, W - 2), op=ALU.add, reads=rd, writes=[b_pw])
                    dve.op("scalar_tensor_tensor", out=accv, in0=sl(psB, 16, W), scalar=pse[:, 1:2], in1=accv,
                           op0=ALU.mult, op1=ALU.add, reads=[b_pw, b_pse], writes=[b_pw])
                    dve.op("tensor_tensor", out=sl(psA, 7, W), in0=sl(psB, 7, W), in1=sl(psB, 3, W - 4), op=ALU.add, reads=rd, writes=[b_pw])
                    dve.op("scalar_tensor_tensor", out=accv, in0=sl(psA, 16, W), scalar=pse[:, 2:3], in1=accv,
                           op0=ALU.mult, op1=ALU.add, reads=[b_pw, b_pse], writes=[b_pw])
                    dve.op("tensor_tensor", out=sl(psB, 15, W), in0=sl(psA, 15, W), in1=sl(psA, 7, W - 8), op=ALU.add, reads=rd, writes=[b_pw])
                    dve.op("scalar_tensor_tensor", out=accv, in0=sl(psB, 16, W), scalar=pse[:, 3:4], in1=accv,
                           op0=ALU.mult, op1=ALU.add, reads=[b_pw, b_pse], writes=[b_pw])
                    if is_prompt and p == 0:
                        for h_ in range(2):
                            dve.op("tensor_tensor", out=pacc[:, h_, :], in0=pacc[:, h_, :], in1=prc_t[:, 0, :], op=ALU.mult,
                                   reads=[b_pw, b_prc], writes=[b_pw])
                    else:
                        dve.op("tensor_scalar", out=accv, in0=accv, scalar1=prc_t[:, 1, 0:1], scalar2=None, op0=ALU.mult,
                               reads=[b_pw, b_prc], writes=[b_pw])
                    dve.op("tensor_tensor", out=mTv, in0=accv, in1=cur, op=ALU.subtract, reads=[b_pw, bE], writes=[b_mT])
                if is_prompt:
                    dve.op("tensor_copy", out=ext[:, :, 0:16], in_=ext[:, :, 128:144], reads=[b_ext, b_pw], writes=[b_ext])
                ps, bps = next_tm()
                for ob in range(2):
                    for kc in range(2):
                        pe.op("matmul", out=ps[:, ob * 128:(ob + 1) * 128], lhsT=wpl[:, kc, ob * 128:(ob + 1) * 128],
                              rhs=mT[:, kc, :], start=(kc == 0), stop=(kc == 1), reads=[b_wpl, b_mT], writes=[bps],
                              signal=(ob == 1 and kc == 1))
                for ob in range(2):
                    dve.op("scalar_tensor_tensor", out=yst[b][:, ob, 0, :], in0=ps[:, ob * 128:(ob + 1) * 128],
                           scalar=psc[:, ob:ob + 1], in1=slpz[b][:, ob, :], op0=ALU.mult, op1=ALU.mult,
                           reads=[bps, b_psc, b_slpz[b]], writes=[b_yst[b]])

                act.op("activation", out=gw[:, 0:1], in_=gat[b][:, 1:2], func=AF.Exp, scale=-1.0,
                       reads=[b_gat[b]], writes=[b_gw])
                act.op("activation", out=gw[:, 1:2], in_=gw[:, 0:1], func=AF.Ln, bias=1.0, reads=[b_gw], writes=[b_gw])
                pe.op("matmul", out=pG[:, 0:1], lhsT=cm[:, MASK, :], rhs=gw[:, 1:2], start=True, stop=True,
                      reads=[b_cm, b_gw], writes=[b_pG], signal=False)
                pe.op("matmul", out=pG[:, 1:2], lhsT=cm[:, (ONES if is_prompt else ONESBD), :], rhs=gw[:, 1:2],
                      start=True, stop=True, reads=[b_cm, b_gw], writes=[b_pG], signal=is_prompt)
                if not is_prompt:
                    pe.op("matmul", out=pG[:, 2:3], lhsT=cm[:, SELLO, :], rhs=gw[:, 1:2], start=True, stop=True,
                          reads=[b_cm, b_gw], writes=[b_pG], signal=False)
                    pe.op("matmul", out=pG[:, 3:4], lhsT=cm[:, SELHI, :], rhs=gw[:, 1:2], start=True, stop=True,
                          reads=[b_cm, b_gw], writes=[b_pG])
                dve.op("tensor_tensor", out=gw[:, 2:3], in0=gat[b][:, 0:1], in1=pG[:, 0:1], op=ALU.add,
                       reads=[b_gat[b], b_pG], writes=[b_gw])
                act.op("activation", out=gw[:, 3:4], in_=gw[:, 2:3], func=AF.Exp, reads=[b_gw], writes=[b_gw])
                act.op("activation", out=gw[:, 4:5], in_=pG[:, 0:1], func=AF.Exp, reads=[b_pG], writes=[b_gw])
                if is_prompt:
                    act.op("activation", out=gw[:, 5:6], in_=pG[:, 1:2], func=AF.Exp, scale=-1.0,
                           reads=[b_pG], writes=[b_gw])
                dve.op("tensor_scalar", out=vpx[:, 0:256], in0=vsb[b][:], scalar1=gw[:, 3:4], scalar2=None, op0=ALU.mult,
                       reads=[b_vsb[b], b_gw], writes=[b_vpx])
                dve.op("tensor_copy", out=vpx[:, 256:257], in_=gw[:, 3:4], reads=[b_gw], writes=[b_vpx])
                transpose_to(kTm, b_kTm, ktok[b], b_ktok[b], 2)
                for kc in range(2):
                    pe.op("matmul", out=pS, lhsT=kTm[:, kc, :], rhs=qTm[b][:, kc, :], start=(kc == 0), stop=(kc == 1),
                          reads=[b_kTm, b_qTm[b]], writes=[b_pS], signal=(kc == 1))
                dve.op("tensor_tensor", out=PTm[:], in0=pS, in1=cm[:, MASK, :], op=ALU.mult,
                       reads=[b_pS, b_cm], writes=[b_PTm])
                pe.op("matmul", out=pN, lhsT=PTm[:], rhs=vpx[:], start=True, stop=False,
                      reads=[b_PTm, b_vpx], writes=[b_pN], signal=False)
                if is_prompt:
                    for kc in range(2):
                        pe.op("matmul", out=pN, lhsT=qTm[b][:, kc, :], rhs=Chb[:, kc, :], start=False, stop=(kc == 1),
                              reads=[b_qTm[b], b_Chb], writes=[b_pN], signal=(kc == 1))
                else:
                    for s in range(2):
                        q = 2 * u + s
                        act.op("activation", out=gw[:, 12 + s:13 + s], in_=smt[:, q:q + 1], func=AF.Exp,
                               reads=[b_smt], writes=[b_gw])
                        dve.op("tensor_scalar", out=Cs[s][:], in0=Cs[s][:], scalar1=gw[:, 12 + s:13 + s], scalar2=None,
                               op0=ALU.mult, reads=[b_Cs[s], b_gw], writes=[b_Cs[s]])
                        act.op("copy", out=Csb[s][:], in_=Cs[s][:], reads=[b_Cs[s]], writes=[b_Csb[s]])
                    for s in range(2):
                        for kc in range(2):
                            last = (s == 1 and kc == 1)
                            pe.op("matmul", out=pN, lhsT=qTz[s][:, kc, :], rhs=Csb[s][:, kc, :], start=False, stop=last,
                                  reads=[b_qTz[s], b_Csb[s]], writes=[b_pN], signal=last)
                dve.op("tensor_copy", out=gw[:, 6:7], in_=pN[:, 256:257], reads=[b_pN], writes=[b_gw])
                dve.op("scalar_tensor_tensor", out=gw[:, 6:7], in0=gw[:, 6:7], scalar=-1.0, in1=gw[:, 6:7],
                       op0=ALU.mult, op1=ALU.max, reads=[b_gw], writes=[b_gw])
                dve.op("tensor_tensor", out=gw[:, 6:7], in0=gw[:, 6:7], in1=gw[:, 4:5], op=ALU.max, reads=[b_gw], writes=[b_gw])
                dve.op("reciprocal", out=gw[:, 7:8], in_=gw[:, 6:7], reads=[b_gw], writes=[b_gw])
                dve.op("scalar_tensor_tensor", out=hg[:], in0=pN[:, 0:256], scalar=gw[:, 7:8], in1=sgmo[b][:],
                       op0=ALU.mult, op1=ALU.mult, reads=[b_pN, b_gw, b_sgmo[b]], writes=[b_hg])
                act.op("activation", out=junk[:, 0:256], in_=hg[:], func=AF.Square, accum_out=gw[:, 8:9],
                       reads=[b_hg], writes=[b_junk, b_gw])
                act.op("activation", out=gw[:, 9:10], in_=gw[:, 8:9], func=AF.Sqrt, scale=1.0 / 256.0, bias=EPS,
                       reads=[b_gw], writes=[b_gw])
                dve.op("reciprocal", out=gw[:, 9:10], in_=gw[:, 9:10], reads=[b_gw], writes=[b_gw])
                dve.op("scalar_tensor_tensor", out=hg[:], in0=hg[:], scalar=gw[:, 9:10], in1=mlgt[:],
                       op0=ALU.mult, op1=ALU.mult, reads=[b_hg, b_gw, b_mlgt], writes=[b_hg])
                dve.op("tensor_tensor", out=yml[:], in0=hg[:], in1=slmz[b][:], op=ALU.mult,
                       reads=[b_hg, b_slmz[b]], writes=[b_yml])
                psd, b_pD = next_tm()
                pD = psd[0:1, 0:128]
                pe.op("matmul", out=pD, lhsT=gw[:, 2:3], rhs=cm[:, IDENT, :], start=True, stop=True,
                      reads=[b_gw, b_cm], writes=[b_pD])
                if is_prompt:
                    dve.op("reduce_max", out=g0[:, 2:3], in_=pD, axis=AX.X, reads=[b_pD], writes=[b_g0])
                    dve.op("tensor_tensor", out=g0[:, 4:5], in0=g0[:, 2:3], in1=g0[:, 0:1], op=ALU.add, reads=[b_g0], writes=[b_g0])
                    dve.op("tensor_tensor", out=g0[:, 1:2], in0=g0[:, 1:2], in1=g0[:, 4:5], op=ALU.max, reads=[b_g0], writes=[b_g0])
                    dve.op("tensor_tensor", out=g0[:, 0:1], in0=g0[:, 0:1], in1=pG[0:1, 1:2], op=ALU.add,
                           reads=[b_g0, b_pG], writes=[b_g0])
                elif "nosmb" in cfg.dbg:
                    dve.op("memset", ap=gw[:, 10:12], constant=1.0, writes=[b_gw])
                else:
                    dve.op("reduce_max", out=g0[:, 2:4], in_=pD.rearrange("p (s t) -> p s t", t=64), axis=AX.X,
                           reads=[b_pD], writes=[b_g0])
                    dve.op("tensor_tensor", out=g0[:, 4:6], in0=g0[:, 2:4], in1=smt[0:1, 2 * u:2 * u + 2], op=ALU.max,
                           reads=[b_g0, b_smt], writes=[b_g0])
                    dve.op("tensor_tensor", out=mrow[:, 1 + 2 * u:3 + 2 * u], in0=g0[:, 4:6], in1=pG[0:1, 2:4], op=ALU.subtract,
                           reads=[b_g0, b_pG], writes=[b_mrow])
                    act.op("activation", out=g0[:, 8:10], in_=g0[:, 4:6], func=AF.Exp, scale=-1.0, reads=[b_g0], writes=[b_g0])
                    psb, b_pBc = next_tm()
                    pBc = psb[:, 0:8]
                    if "nopbc" in cfg.dbg:
                        dve.op("memset", ap=gw[:, 10:12], constant=1.0, writes=[b_gw])
                    else:
                        pe.op("matmul", out=pBc[:, 0:2], lhsT=cm[0:1, ONES, :], rhs=g0[:, 8:10], start=True, stop=True,
                              reads=[b_cm, b_g0], writes=[b_pBc])
                        dve.op("tensor_copy", out=gw[:, 10:12], in_=pBc[:, 0:2], reads=[b_pBc], writes=[b_gw])
                if is_prompt:
                    for kc in range(2):
                        ps, bps = next_tm()
                        pe.op("matmul", out=ps[:, 0:257], lhsT=ktok[b][:, kc * 128:(kc + 1) * 128], rhs=vpx[:],
                              start=True, stop=True, reads=[b_ktok[b], b_vpx], writes=[bps])
                        dve.op("tensor_scalar", out=Ch[:, kc, :], in0=Ch[:, kc, :], scalar1=gw[:, 5:6], scalar2=None,
                               op0=ALU.mult, reads=[b_Ch, b_gw], writes=[b_Ch])
                        dve.op("scalar_tensor_tensor", out=Ch[:, kc, :], in0=ps[:, 0:257], scalar=gw[:, 5:6], in1=Ch[:, kc, :],
                               op0=ALU.mult, op1=ALU.add, reads=[bps, b_gw, b_Ch], writes=[b_Ch])
                        act.op("copy", out=Chb[:, kc, :], in_=Ch[:, kc, :], reads=[b_Ch], writes=[b_Chb])
                    if last_prompt:
                        dve.op("tensor_tensor", out=mrow[:, 0:1], in0=g0[:, 1:2], in1=g0[:, 0:1], op=ALU.subtract,
                               reads=[b_g0], writes=[b_mrow])
                        act.op("activation", out=g0[:, 8:9], in_=mrow[:, 0:1], func=AF.Exp, scale=-1.0,
                               reads=[b_mrow], writes=[b_g0])
                        psb, b_pBc = next_tm()
                        pBc = psb[:, 0:8]
                        pe.op("matmul", out=pBc[:, 0:1], lhsT=cm[0:1, ONES, :], rhs=g0[:, 8:9], start=True, stop=True,
                              reads=[b_cm, b_g0], writes=[b_pBc])
                        dve.op("tensor_copy", out=gw[:, 10:11], in_=pBc[:, 0:1], reads=[b_pBc], writes=[b_gw])
                        dve.op("tensor_scalar", out=cst[:], in0=Ch[:], scalar1=gw[:, 10:11], scalar2=None, op0=ALU.mult,
                               reads=[b_Ch, b_gw], writes=[b_cst])
                        sp.dma(s_cst, o_c[layer, 0].rearrange("(k p) v -> p k v", p=128), cst[:, :, 0:256], reads=[b_cst])
                        sp.dma(s_cst, o_n[layer, 0].rearrange("(k p) -> p k", p=128), cst[:, :, 256], reads=[b_cst], allow_slow_non_contiguous=True)
                elif "noK64" not in cfg.dbg:
                    for s in range(2):
                        q = 2 * u + s
                        for kc in range(2):
                            ps, bps = next_tm()
                            pe.op("matmul", out=ps[:, 0:257], lhsT=ktok[b][64 * s:64 * s + 64, kc * 128:(kc + 1) * 128],
                                  rhs=vpx[64 * s:64 * s + 64, :], start=True, stop=True,
                                  reads=[b_ktok[b], b_vpx], writes=[bps])
                            dve.op("tensor_tensor", out=cst[:, kc, :], in0=Cs[s][:, kc, :], in1=ps[:, 0:257], op=ALU.add,
                                   reads=[b_Cs[s], bps], writes=[b_cst])
                        dve.op("tensor_scalar", out=cst[:], in0=cst[:], scalar1=gw[:, 10 + s:11 + s], scalar2=None,
                               op0=ALU.mult, reads=[b_cst, b_gw], writes=[b_cst])
                        sp.dma(s_cst, o_c[layer, 1 + q].rearrange("(k p) v -> p k v", p=128), cst[:, :, 0:256], reads=[b_cst])
                        sp.dma(s_cst, o_n[layer, 1 + q].rearrange("(k p) -> p k", p=128), cst[:, :, 256], reads=[b_cst], allow_slow_non_contiguous=True)

                for h in range(2):
                    act.op("activation", out=junk[:, 0:128], in_=aqs[b][:, h * 128:(h + 1) * 128], func=AF.Square,
                           accum_out=aw[:, h:h + 1], reads=[b_aqs[b]], writes=[b_junk, b_aw])
                    act.op("activation", out=junk[:, 0:128], in_=aks[b][:, h * 128:(h + 1) * 128], func=AF.Square,
                           accum_out=aw[:, 2 + h:3 + h], reads=[b_aks[b]], writes=[b_junk, b_aw])
                act.op("activation", out=aw[:, 4:6], in_=aw[:, 0:2], func=AF.Sqrt, scale=1.0, bias=128.0 * EPS,
                       reads=[b_aw], writes=[b_aw])
                act.op("activation", out=aw[:, 6:8], in_=aw[:, 2:4], func=AF.Sqrt, scale=1.0 / 128.0, bias=EPS,
                       reads=[b_aw], writes=[b_aw])
                dve.op("reciprocal", out=aw[:, 4:8], in_=aw[:, 4:8], reads=[b_aw], writes=[b_aw])
                for h in range(2):
                    hs = slice(h * 128, (h + 1) * 128)
                    dve.op("scalar_tensor_tensor", out=qhb[:, hs], in0=aqs[b][:, hs], scalar=aw[:, 4 + h:5 + h], in1=aqgt[:, hs],
                           op0=ALU.mult, op1=ALU.mult, reads=[b_aqs[b], b_aw, b_aqgt], writes=[b_qhb])
                    dve.op("scalar_tensor_tensor", out=khs[b][:, hs], in0=aks[b][:, hs], scalar=aw[:, 6 + h:7 + h], in1=akgt[:, hs],
                           op0=ALU.mult, op1=ALU.mult, reads=[b_aks[b], b_aw, b_akgt], writes=[b_khs[b]])
                act.op("copy", out=khb[:], in_=khs[b][:], reads=[b_khs[b]], writes=[b_khb])
                transpose_to(qTa, b_qTa, qhb, b_qhb, 2)
                transpose_to(kring[:, slot, :, :], b_kring[slot], khb, b_khb, 2)
                for h in range(2):
                    if is_prompt:
                        kbs = list(range(max(0, 4 - p), 5))
                        nA = len([kb for kb in kbs if kb < 4])
                        pAv = pAS[:, 0:512].rearrange("p (a b) -> p a b", b=128)
                        for kb in kbs:
                            sl_k = (p - 4 + kb) % 6
                            dst = pAv[:, kb, :] if kb < 4 else pA5
                            bd = b_pAS if kb < 4 else b_pA5
                            pe.op("matmul", out=dst, lhsT=kring[:, sl_k, h, :], rhs=qTa[:, h, :], start=True, stop=False,
                                  reads=[b_kring[sl_k], b_qTa], writes=[bd], signal=False)
                            pe.op("matmul", out=dst, lhsT=identb[:], rhs=bhi[:, h * 6 + kb, :], start=False, stop=False,
                                  reads=[b_identb, b_bhi], writes=[bd], signal=False)
                            pe.op("matmul", out=dst, lhsT=identb[:], rhs=blo[:, h * 6 + kb, :], start=False, stop=True,
                                  reads=[b_identb, b_blo], writes=[bd], signal=True)
                        k0 = kbs[0]
                        if k0 < 4:
                            act.op("activation", out=PTp[:, k0:4, :], in_=pAv[:, k0:4, :], func=AF.Exp,
                                   reads=[b_pAS], writes=[b_PTp])
                        act.op("activation", out=PTp[:, 4, :], in_=pA5, func=AF.Exp, reads=[b_pA5], writes=[b_PTp])
                        for i, kb in enumerate(kbs):
                            sl_k = (p - 4 + kb) % 6
                            pe.op("matmul", out=pPV[:, h, :], lhsT=PTp[:, kb, :], rhs=vring[:, sl_k, h, :],
                                  start=(i == 0), stop=(i == len(kbs) - 1), reads=[b_PTp, b_vring[sl_k]], writes=[b_pPV],
                                  signal=(i == len(kbs) - 1))
                    elif "nosatt" in cfg.dbg:
                        pe.op("matmul", out=pPV[:, h, :], lhsT=PTp[:, 4, :], rhs=vring[:, slot, h, :], start=True, stop=True,
                              reads=[b_PTp, b_vring[slot]], writes=[b_pPV])
                    else:
                        pAv = pAS[:, 0:512].rearrange("p (a b) -> p a b", b=64)
                        for s in range(2):
                            for kb in range(4):
                                dst = pAv[:, s * 4 + kb, :]
                                qs = slice(64 * s, 64 * s + 64)
                                pe.op("matmul", out=dst, lhsT=kTc[:, s, h, kb * 128:(kb + 1) * 128], rhs=qTa[:, h, qs],
                                      start=True, stop=False, reads=[b_kTc[s], b_qTa], writes=[b_pAS], signal=False)
                                pe.op("matmul", out=dst, lhsT=identb[:], rhs=bhi[:, h * 6 + kb, 0:64], start=False, stop=False,
                                      reads=[b_identb, b_bhi], writes=[b_pAS], signal=False)
                                pe.op("matmul", out=dst, lhsT=identb[:], rhs=blo[:, h * 6 + kb, 0:64], start=False, stop=True,
                                      reads=[b_identb, b_blo], writes=[b_pAS], signal=(s == 1 and kb == 3))
                        pe.op("matmul", out=pA5, lhsT=kring[:, slot, h, :], rhs=qTa[:, h, :], start=True, stop=False,
                              reads=[b_kring[slot], b_qTa], writes=[b_pA5], signal=False)
                        pe.op("matmul", out=pA5, lhsT=identb[:], rhs=bhi[:, h * 6 + 5, :], start=False, stop=False,
                              reads=[b_identb, b_bhi], writes=[b_pA5], signal=False)
                        pe.op("matmul", out=pA5, lhsT=identb[:], rhs=blo[:, h * 6 + 5, :], start=False, stop=True,
                              reads=[b_identb, b_blo], writes=[b_pA5])
                        for s in range(2):
                            act.op("activation", out=PTs[s][:, :, 64 * s:64 * s + 64], in_=pAv[:, s * 4:s * 4 + 4, :], func=AF.Exp,
                                   reads=[b_pAS], writes=[b_PTs[s]])
                        act.op("activation", out=PTp[:, 4, :], in_=pA5, func=AF.Exp, reads=[b_pA5], writes=[b_PTp])
                        for s in range(2):
                            for kb in range(4):
                                pe.op("matmul", out=pPV[:, h, :], lhsT=PTs[s][:, kb, :], rhs=Vc[:, s, kb, h, :],
                                      start=(s == 0 and kb == 0), stop=False, reads=[b_PTs[s], b_Vc[s]], writes=[b_pPV],
                                      signal=False)
                        pe.op("matmul", out=pPV[:, h, :], lhsT=PTp[:, 4, :], rhs=vring[:, slot, h, :], start=False, stop=True,
                              reads=[b_PTp, b_vring[slot]], writes=[b_pPV])
                dve.op("reciprocal", out=aw[:, 8:10], in_=pPV[:, :, 128], reads=[b_pPV], writes=[b_aw])
                for h in range(2):
                    hs = slice(h * 128, (h + 1) * 128)
                    dve.op("scalar_tensor_tensor", out=yat[:, hs], in0=pPV[:, h, 0:128], scalar=aw[:, 8 + h:9 + h],
                           in1=slaz[b][:, hs], op0=ALU.mult, op1=ALU.mult, reads=[b_pPV, b_aw, b_slaz[b]], writes=[b_yat])

                transpose_to(yst[b][:, :, 1, :], b_yst[b], yml, b_yml, 2)
                transpose_to(yst[b][:, :, 2, :], b_yst[b], yat, b_yat, 2)
                ytok = sp.dma(s_yst[b], ysrc[T // 3].rearrange("(c p) t -> p c t", p=128)[:, :, (T % 3) * 128:(T % 3 + 1) * 128],
                              yst[b][:].rearrange("p h b t -> p (h b) t"), reads=[b_yst[b]])
                ytoks.append(ytok)
                if T % 3 == 2 and cfg.max_tiles is None:
                    pool.wait_tok(ytoks[-1])
                    pool.wait_tok(ytoks[-2])
                    issue_collective(ysrc[T // 3], ydst[T // 3])

                if out_pool and "nosout" not in cfg.dbg:
                    tokp = sp.dma(s_pscr[b], pscr[b], pus[b][:], reads=[b_pus[b]])
                    sp.wait_tok(tokp)
                    if is_prompt:
                        sp.dma(s_out[b], o_pool[layer, 0], pscr[b, 113:128, :])
                    else:
                        for s in range(2):
                            sp.dma(s_out[b], o_pool[layer, 1 + 2 * u + s], pscr[b, 64 * s + 49:64 * s + 64, :])
                    sp.wait_tok(s_out[b].last())
                if out_kv:
                    row0 = (p - (cfg.NPTILES - 4)) * 128 if is_prompt else 512 + u * 128
                    if "nook" not in cfg.dbg:
                        sp.dma(s_out[b], o_k[layer, row0:row0 + 128, :], khs[b][:], reads=[b_khs[b]])
                    if "noov" not in cfg.dbg:
                        sp.dma(s_out[b], o_v[layer, row0:row0 + 128, :], avs[b][:], reads=[b_avs[b]])
                if out_pool or out_kv:
                    for bb_ in (b_pus[b], b_khs[b], b_avs[b]):
                        if s_out[b].key in bb_.r:
                            bb_.r[s_out[b].key] = s_out[b].last()
            load_x(0)
            front(0)
            for T in range(ntiles_run):
                if T + 1 < ntiles_run:
                    front(T + 1)
                back(T)
            sp.dma(s_mrow, o_m[layer:layer + 1, :], mrow[:], reads=[b_mrow])

            barrier()
            if cfg.stop_after == f"A{layer}":
                break

            ar.release(m_persist)
            HT, HTOK, TG, NTG = cfg.HT, cfg.HTOK, cfg.TG, cfg.NTG
            hTB = ar.alloc("hTB", [128, 16, HTOK], BF16); b_hTB = B("hTB")
            mgT = ar.alloc("mgT", [128, 16, HTOK], BF16); b_mgT = B("mgT")
            yTb = [ar.alloc("yTb0", [128, NTG, 8, TG], BF16), ar.alloc("yTb1", [128, NTG, 8, TG], BF16)]
            b_yTb = [B(), B()]
            s_yTb = [DmaSem(cx, "sytb0"), DmaSem(cx, "sytb1")]
            NW = 2
            wslot = [ar.alloc(f"wsl{i}", [128, 16 * 512 + 2 * 512], BF16) for i in range(NW)]
            b_wslot = [B() for _ in range(NW)]
            s_wslot = [DmaSem(cx, f"swsl{i}") for i in range(NW)]
            wsl_i = {"i": 0}
            NWS = 6
            b_wsm = [B() for _ in range(NWS)]
            s_wsm = [DmaSem(cx, f"swsm{i}") for i in range(NWS)]
            wsm_i = {"i": 0}
            sgs = [ar.alloc("sgs0", [128, 512], F32), ar.alloc("sgs1", [128, 512], F32)]
            b_sgs = [B(), B()]
            tmpB = [ar.alloc("tmpB0", [128, 512], F32), ar.alloc("tmpB1", [128, 512], F32)]
            b_tmpB = [B(), B()]
            xsl = [ar.alloc(f"xsl{i}", [128, 512], F32) for i in range(3)]
            b_xsl = [B() for _ in range(3)]
            s_xsl = [DmaSem(cx, f"sxsl{i}") for i in range(3)]
            xos = [ar.alloc(f"xos{i}", [128, 512], F32) for i in range(3)]
            b_xos = [B() for _ in range(3)]
            s_xos = [DmaSem(cx, f"sxos{i}") for i in range(3)]
            pT = ar.alloc("pT", [128, 2, HTOK], BF16); b_pT = B("pT")
            pst = ar.alloc("pst", [128, 256], F32); b_pst = B("pst")
            s_pst = DmaSem(cx, "spst")
            pbf = ar.alloc("pbf", [128, 256], BF16); b_pbf = B("pbf")
            gpb = gnb; b_gpb = b_gnb
            s_gn = DmaSem(cx, "sgn")
            PA = [pbs[2], pbs[3]]; b_PA = [B(), B()]
            PB = [pbs[4], pbs[5]]; b_PB = [B(), B()]
            rank_state = {}

            for hh in range(cfg.NH):
                t0 = hh * HTOK
                sp.dma(s_gn, gnb[:], gn[layer].partition_broadcast(128), writes=[b_gnb])
                for i in range(HT):
                    b = i % 2
                    sp.dma(s_xs[b], xs[b][:], xsrc_own[t0 + i * 128:t0 + (i + 1) * 128, :], writes=[b_xs[b]])
                    norm_tile(xs[b], b_xs[b], gnb, b_gnb)
                    transpose_to(hTB[:, :, i * 128:(i + 1) * 128], b_hTB, hb, b_hb, 16)
                par = 0
                for br in range(3):
                    yb = br % 2
                    for tg in range(NTG):
                        def fn(e, br=br, tg=tg, hh=hh, dstt=yTb[yb], NTG=NTG, CPR=CPR):
                            if "nodyn" in cfg.dbg:
                                rank_state["c"] = 0
                            if "c" not in rank_state:
                                rank_state["c"] = e.partition_id() % 4
                            c = rank_state["c"]
                            y4 = ydst.rearrange("(c k) r t -> c k r t", k=CPR)
                            src = y4[bass.ds(c, 1), hh * NTG + tg, :, :].rearrange("o (rh b p) t -> p (o rh) b t", b=3, p=128)
                            return e.dma_start(out=dstt[:, tg, :, :], in_=src[:, :, br, :])
                        sp.dma(s_yTb[yb], None, None, writes=[b_yTb[yb]], fn=fn)
                    for eb in range(16):
                        wi = wsm_i["i"]
                        wsm_i["i"] = (wi + 1) % NWS
                        wbase = wslot[wi // 3][:, (wi % 3) * 3072:(wi % 3 + 1) * 3072]
                        wg = wbase[:, 0:2048].rearrange("p (k c) -> p k c", c=128)
                        wb = wbase[:, 2048:3072].rearrange("p (k c) -> p k c", c=128)
                        bw_ = b_wsm[wi]
                        pool.dma(s_wsm[wi], wg, w_gate[layer, br, eb].rearrange("(k p) c -> p k c", p=128), writes=[bw_, b_wslot[wi // 3]])
                        pool.dma(s_wsm[wi], wb, w_br[layer, br, eb].rearrange("(k p) c -> p k c", p=128), writes=[bw_, b_wslot[wi // 3]])
                        for tg in range(NTG):
                            ts = slice(tg * TG, (tg + 1) * TG)
                            par ^= 1
                            for k in range(16):
                                pe.op("matmul", out=PA[par][:, 0:TG], lhsT=wg[:, k, :], rhs=hTB[:, k, ts], start=(k == 0), stop=(k == 15),
                                      reads=[bw_, b_hTB], writes=[b_PA[par]], signal=(k == 15))
                            for k in range(8):
                                pe.op("matmul", out=PB[par][:, 0:TG], lhsT=wb[:, k, :], rhs=yTb[yb][:, tg, k, :], start=(k == 0), stop=(k == 7),
                                      reads=[bw_, b_yTb[yb]], writes=[b_PB[par]], signal=(k == 7))
                            act.op("activation", out=sgs[par][:, 0:TG], in_=PA[par][:, 0:TG], func=AF.Sigmoid,
                                   reads=[b_PA[par]], writes=[b_sgs[par]])
                            if br == 0:
                                dve.op("tensor_tensor", out=mgT[:, eb, ts], in0=sgs[par][:, 0:TG], in1=PB[par][:, 0:TG], op=ALU.mult,
                                       reads=[b_sgs[par], b_PB[par]], writes=[b_mgT])
                            else:
                                dve.op("tensor_tensor", out=tmpB[par][:, 0:TG], in0=sgs[par][:, 0:TG], in1=PB[par][:, 0:TG], op=ALU.mult,
                                       reads=[b_sgs[par], b_PB[par]], writes=[b_tmpB[par]])
                                dve.op("tensor_tensor", out=mgT[:, eb, ts], in0=mgT[:, eb, ts], in1=tmpB[par][:, 0:TG], op=ALU.add,
                                       reads=[b_tmpB[par], b_mgT], writes=[b_mgT])
                xi = 0
                for cb in range(4):
                    wi = wsl_i["i"]
                    wsl_i["i"] = (wi + 1) % NW
                    wo = wslot[wi][:, 0:8192].rearrange("p (k c) -> p k c", c=512)
                    for kq in range(2):
                        pool.dma(s_wslot[wi], wo[:, kq * 8:(kq + 1) * 8, :],
                                 w_out[layer, cb, kq * 1024:(kq + 1) * 1024, :].rearrange("(k p) c -> p k c", p=128),
                                 writes=[b_wslot[wi]] + b_wsm[wi * 3:wi * 3 + 3])
                    for i in range(HT):
                        rows = slice(t0 + i * 128, t0 + (i + 1) * 128)
                        cols = slice(cb * 512, (cb + 1) * 512)
                        xi = (xi + 1) % 3
                        par ^= 1
                        sp.dma(s_xsl[xi], xsl[xi][:], xsrc_own[rows, cols], writes=[b_xsl[xi]])
                        for k in range(16):
                            pe.op("matmul", out=PA[par][:], lhsT=mgT[:, k, i * 128:(i + 1) * 128], rhs=wo[:, k, :], start=(k == 0), stop=(k == 15),
                                  reads=[b_mgT, b_wslot[wi]], writes=[b_PA[par]], signal=(k == 15))
                        dve.op("tensor_tensor", out=xos[xi][:], in0=PA[par][:], in1=xsl[xi][:], op=ALU.add,
                               reads=[b_PA[par], b_xsl[xi]], writes=[b_xos[xi]])
                        sp.dma(s_xos[xi], xnew[rows, cols], xos[xi][:], reads=[b_xos[xi]])
                for i3 in range(3):
                    sp.wait_tok(s_xos[i3].last())
                sp.dma(s_gn, gnb[:], gp[layer].partition_broadcast(128), writes=[b_gnb])
                for i in range(HT):
                    b = i % 2
                    rows = slice(t0 + i * 128, t0 + (i + 1) * 128)
                    sp.dma(s_xs[b], xs[b][:], xnew[rows, :], writes=[b_xs[b]])
                    sp.dma(s_pst, pst[:], pown[layer, rows, :], writes=[b_pst])
                    norm_tile(xs[b], b_xs[b], gpb, b_gpb)
                    transpose_to(hTB[:, :, i * 128:(i + 1) * 128], b_hTB, hb, b_hb, 16)
                    dve.op("tensor_copy", out=pbf[:], in_=pst[:], reads=[b_pst], writes=[b_pbf])
                    transpose_to(pT[:, :, i * 128:(i + 1) * 128], b_pT, pbf, b_pbf, 2)
                for cb in range(4):
                    wi = wsl_i["i"]
                    wsl_i["i"] = (wi + 1) % NW
                    wo = wslot[wi][:, 0:8192].rearrange("p (k c) -> p k c", c=512)
                    wp = wslot[wi][:, 8192:9216].rearrange("p (k c) -> p k c", c=512)
                    for kq in range(2):
                        pool.dma(s_wslot[wi], wo[:, kq * 8:(kq + 1) * 8, :],
                                 w_pg[layer, cb, kq * 1024:(kq + 1) * 1024, :].rearrange("(k p) c -> p k c", p=128),
                                 writes=[b_wslot[wi]] + b_wsm[wi * 3:wi * 3 + 3])
                    pool.dma(s_wslot[wi], wp, w_pp[layer, cb].rearrange("(k p) c -> p k c", p=128), writes=[b_wslot[wi]])
                    for i in range(HT):
                        rows = slice(t0 + i * 128, t0 + (i + 1) * 128)
                        cols = slice(cb * 512, (cb + 1) * 512)
                        xi = (xi + 1) % 3
                        par ^= 1
                        sp.dma(s_xsl[xi], xsl[xi][:], xnew[rows, cols], writes=[b_xsl[xi]])
                        for k in range(16):
                            pe.op("matmul", out=PA[par][:], lhsT=hTB[:, k, i * 128:(i + 1) * 128], rhs=wo[:, k, :], start=(k == 0), stop=(k == 15),
                                  reads=[b_hTB, b_wslot[wi]], writes=[b_PA[par]], signal=(k == 15))
                        for k in range(2):
                            pe.op("matmul", out=PB[par][:], lhsT=pT[:, k, i * 128:(i + 1) * 128], rhs=wp[:, k, :], start=(k == 0), stop=(k == 1),
                                  reads=[b_pT, b_wslot[wi]], writes=[b_PB[par]], signal=(k == 1))
                        act.op("activation", out=sgs[par][:], in_=PA[par][:], func=AF.Sigmoid, reads=[b_PA[par]], writes=[b_sgs[par]])
                        dve.op("tensor_tensor", out=tmpB[par][:], in0=sgs[par][:], in1=PB[par][:], op=ALU.mult,
                               reads=[b_sgs[par], b_PB[par]], writes=[b_tmpB[par]])
                        dve.op("tensor_tensor", out=xos[xi][:], in0=tmpB[par][:], in1=xsl[xi][:], op=ALU.add,
                               reads=[b_tmpB[par], b_xsl[xi]], writes=[b_xos[xi]])
                        sp.dma(s_xos[xi], xdest[rows, cols], xos[xi][:], reads=[b_xos[xi]])
                for i3 in range(3):
                    sp.wait_tok(s_xos[i3].last())

            if layer == 0:
                barrier()
                xo3 = xown1.rearrange("(i p) d -> i p d", p=128)
                for i in range(TPR):
                    issue_collective(xo3[i], xg1[i])
            barrier()

        with nc.Block() as block:
            @block.tensor
            def _(e):
                for f in pe.prog:
                    f(e)

            @block.scalar
            def _(e):
                for f in act.prog:
                    f(e)

            @block.vector
            def _(e):
                for f in dve.prog:
                    f(e)

            @block.gpsimd
            def _(e):
                for f in pool.prog:
                    f(e)

            @block.sync
            def _(e):
                for f in sp.prog:
                    f(e)
        stats = {q.name: (q.n_ins, q.n_wait) for q in cx.qs}
        stats["nsem"] = cx.nsem
        stats["logs"] = {q.name: q.log for q in cx.qs}
        stats["sbuf_peak"] = ar.off
    return nc, stats


POOL_WINDOWS = (2, 4, 8, 16)
OFF = {}
_acc = 0
for _n, _s in (("pu", 1024), ("pz", 1024), ("mq", 1024), ("mk", 1024), ("mv", 1024), ("mo", 1024), ("mz", 1024),
               ("mi", 4), ("mf", 4), ("aq", 1024), ("ak", 1024), ("av", 1024), ("az", 1024), ("gts", 3 * D)):
    OFF[_n] = _acc
    _acc += _s


def _consts():
    cm = np.zeros((128, 7, 128), np.float32)
    s = np.arange(128)[:, None]
    t = np.arange(128)[None, :]
    same = (s // 64) == (t // 64)
    cm[:, 0, :] = np.eye(128)
    cm[:, 1, :] = (s <= t)
    cm[:, 2, :] = (s <= t) & same
    cm[:, 3, :] = same
    cm[:, 4, :] = (s < 64) & (t >= 0)
    cm[:, 5, :] = (s >= 64) & (t >= 0)
    cm[:, 6, :] = 1.0
    return cm


def _bias_index():
    idx = np.full((6, 128, 128), -1, np.int64)
    jj = np.arange(128)[:, None]
    ii = np.arange(128)[None, :]
    ci = ii // 64
    for kb in range(5):
        relk = 128 * (kb - 4) + jj
        ok = (relk >= 64 * ci - 512) & (relk <= 64 * ci + 63)
        dist = np.clip(ii - relk, -256, 256) + 256
        idx[kb] = np.where(ok, dist, -1)
    same = (jj // 64) == (ii // 64)
    dist = np.clip((ii % 64) - (jj % 64), -256, 256) + 256
    idx[5] = np.where(same, dist, -1)
    return idx


def prepare_inputs(cfg, inp):
    f = np.float32
    NPT, NSR, NS, TR = cfg.NPT, cfg.NSR, cfg.NS, cfg.TR
    w_in = inp["w_in"]
    cm = _consts()
    bidx = _bias_index()
    maps = []
    for core in range(cfg.NCORES):
        g, j = divmod(core, 4)

        def rank_rows(arr_p, arr_s, r):
            a = arr_p[r * NPT * 128:(r + 1) * NPT * 128]
            bq = arr_s[g * NS + r * NSR: g * NS + (r + 1) * NSR]
            return np.concatenate([a, bq.reshape((-1,) + bq.shape[2:])], 0)

        m = {}
        m["xg"] = np.ascontiguousarray(np.concatenate(
            [rank_rows(inp["x_prompt"][g], inp["x_sample"], r) for r in range(4)], 0), f)
        m["xown"] = np.ascontiguousarray(rank_rows(inp["x_prompt"][g], inp["x_sample"], j), f)
        m["pown"] = np.ascontiguousarray(np.stack(
            [rank_rows(inp["p_prompt"][l, g], inp["p_sample"][l], j) for l in range(DEPTH)]), f)
        c256 = lambda name, k=j: w_in[:, :, OFF[name] + k * 256: OFF[name] + (k + 1) * 256]
        mi = w_in[:, :, OFF["mi"] + j: OFF["mi"] + j + 1]
        mf = w_in[:, :, OFF["mf"] + j: OFF["mf"] + j + 1]
        m["w_tm"] = np.ascontiguousarray(np.concatenate(
            [c256("pu"), c256("mk"), c256("mv"), c256("mo"), c256("mz"), c256("aq"), c256("ak"), c256("av"), c256("az"),
             mi, mf], -1), f)
        m["w_fm"] = np.ascontiguousarray(np.concatenate([c256("pu"), c256("pz"), c256("mq")], -1), f)
        gts = w_in[:, :, OFF["gts"]:].reshape(DEPTH, D, 3, 16, 128)
        m["w_gate"] = np.ascontiguousarray(gts.transpose(0, 2, 3, 1, 4), f)
        m["w_br"] = np.ascontiguousarray(inp["w_branch"].reshape(DEPTH, 3, 1024, 16, 128).transpose(0, 1, 3, 2, 4), f)
        m["w_out"] = np.ascontiguousarray(inp["w_out"].reshape(DEPTH, D, 4, 512).transpose(0, 2, 1, 3), f)
        m["w_pg"] = np.ascontiguousarray(inp["w_ple_gate"].reshape(DEPTH, D, 4, 512).transpose(0, 2, 1, 3), f)
        m["w_pp"] = np.ascontiguousarray(inp["w_ple_proj"].reshape(DEPTH, 256, 4, 512).transpose(0, 2, 1, 3), f)
        m["gn"] = np.ascontiguousarray(inp["norm_mix"], f)
        m["gp"] = np.ascontiguousarray(inp["ple_norm"], f)
        m["wpool"] = np.ascontiguousarray(inp["w_pool_group"][:, j], f)
        m["pscale"] = np.ascontiguousarray(inp["pool_scale"][:, j * 256:(j + 1) * 256].reshape(DEPTH, 2, 128).transpose(0, 2, 1), f)
        sel = np.zeros((128, 4), f)
        sel[:, j] = 1.0
        m["psel"] = sel
        w = POOL_WINDOWS[j]
        rc = np.zeros((2, 128, 128), f)
        rc[0] = (np.float32(1.0) / np.minimum(np.arange(128) + 1, w).astype(f))[None, :]
        rc[1] = np.float32(1.0) / np.float32(w)
        m["prc"] = rc
        bg = np.zeros((DEPTH, 128, 2), f)
        bg[:, :, 0] = inp["b_ig"][:, j][:, None]
        bg[:, :, 1] = inp["b_fg"][:, j][:, None]
        m["bgate"] = bg
        m["mlg"] = np.ascontiguousarray(inp["ml_head_norm"][:, j * 256:(j + 1) * 256], f)
        m["aqg"] = np.ascontiguousarray(np.tile(inp["att_q_norm"], (1, 2)), f)
        m["akg"] = np.ascontiguousarray(np.tile(inp["att_k_norm"], (1, 2)), f)
        ab = np.zeros((DEPTH, 128, 12, 128), f)
        for l in range(DEPTH):
            for h in range(2):
                row = inp["att_rel_bias"][l, 2 * j + h]
                for kb in range(6):
                    v = np.where(bidx[kb] >= 0, row[np.maximum(bidx[kb], 0)], np.float32(NEG))
                    ab[l, :, h * 6 + kb, :] = v
        m["abias"] = ab
        m["cmask"] = cm
        sq = slice(g * NS, (g + 1) * NS)
        sp_ = inp["state_pool"][:, sq, :, j * 256:(j + 1) * 256]
        hist = np.zeros((DEPTH, 128, NS // 2, 2, 2, 16), f)
        hist[..., 1:] = sp_.reshape(DEPTH, NS // 2, 2, 15, 2, 128).transpose(0, 5, 1, 4, 2, 3)
        m["spool"] = hist
        c_ = inp["state_mlstm_c"][:, sq, j]
        m["sC"] = np.ascontiguousarray(c_.transpose(0, 1, 3, 2).reshape(DEPTH, NS, 2, 128, 256), f)
        n_ = inp["state_mlstm_n"][:, sq, j]
        m["sn"] = np.ascontiguousarray(n_.reshape(DEPTH, NS, 2, 128).transpose(0, 1, 3, 2), f)
        m_ = inp["state_mlstm_m"][:, sq, j]
        m["sm"] = np.ascontiguousarray(np.broadcast_to(m_[:, None, :], (DEPTH, 128, NS)), f)
        k_ = inp["cache_att_k"][:, sq, :, 2 * j:2 * j + 2, :]
        m["skT"] = np.ascontiguousarray(k_.transpose(0, 1, 3, 4, 2), f)
        m["sv"] = np.ascontiguousarray(inp["cache_att_v"][:, sq, :, 2 * j:2 * j + 2, :], f)
        maps.append(m)
    return maps


def assemble_outputs(cfg, res):
    f = np.float32
    NG, NPT, NSR, NS, TR = cfg.NG, cfg.NPT, cfg.NSR, cfg.NS, cfg.TR
    SEQ = cfg.SEQ
    NSAMP = NG * NS
    y_p = np.zeros((NG, SEQ, D), f)
    y_s = np.zeros((NSAMP, 64, D), f)
    pool_p = np.zeros((DEPTH, NG, 15, 1024), f)
    pool_s = np.zeros((DEPTH, NSAMP, 15, 1024), f)
    c_p = np.zeros((DEPTH, NG, 4, 256, 256), f)
    c_s = np.zeros((DEPTH, NSAMP, 4, 256, 256), f)
    n_p = np.zeros((DEPTH, NG, 4, 256), f)
    n_s = np.zeros((DEPTH, NSAMP, 4, 256), f)
    m_p = np.zeros((DEPTH, NG, 4), f)
    m_s = np.zeros((DEPTH, NSAMP, 4), f)
    kw = min(512, SEQ)
    k_p = np.zeros((DEPTH, NG, kw, 8, 128), f)
    v_p = np.zeros((DEPTH, NG, kw, 8, 128), f)
    k_s = np.zeros((DEPTH, NSAMP, 64, 8, 128), f)
    v_s = np.zeros((DEPTH, NSAMP, 64, 8, 128), f)
    for core in range(cfg.NCORES):
        g, j = divmod(core, 4)
        r = res[core]
        yo = np.asarray(r["y_own"])
        y_p[g, j * NPT * 128:(j + 1) * NPT * 128] = yo[:NPT * 128]
        y_s[g * NS + j * NSR: g * NS + (j + 1) * NSR] = yo[NPT * 128:].reshape(NSR, 64, D)
        op = np.asarray(r["o_pool"])
        pool_p[:, g, :, j * 256:(j + 1) * 256] = op[:, 0]
        pool_s[:, g * NS:(g + 1) * NS, :, j * 256:(j + 1) * 256] = op[:, 1:]
        oc = np.asarray(r["o_c"]).transpose(0, 1, 3, 2)
        c_p[:, g, j] = oc[:, 0]
        c_s[:, g * NS:(g + 1) * NS, j] = oc[:, 1:]
        on = np.asarray(r["o_n"])
        n_p[:, g, j] = on[:, 0]
        n_s[:, g * NS:(g + 1) * NS, j] = on[:, 1:]
        om = np.asarray(r["o_m"])
        m_p[:, g, j] = om[:, 0]
        m_s[:, g * NS:(g + 1) * NS, j] = om[:, 1:]
        ok = np.asarray(r["o_k"]).reshape(DEPTH, cfg.NKV, 2, 128)
        ov = np.asarray(r["o_v"]).reshape(DEPTH, cfg.NKV, 2, 128)
        k_p[:, g, :, 2 * j:2 * j + 2] = ok[:, :512][:, 512 - kw:]
        v_p[:, g, :, 2 * j:2 * j + 2] = ov[:, :512][:, 512 - kw:]
        k_s[:, g * NS:(g + 1) * NS, :, 2 * j:2 * j + 2] = ok[:, 512:].reshape(DEPTH, NS, 64, 2, 128)
        v_s[:, g * NS:(g + 1) * NS, :, 2 * j:2 * j + 2] = ov[:, 512:].reshape(DEPTH, NS, 64, 2, 128)
    return (y_p, y_s, pool_p, pool_s, c_p, c_s, n_p, n_s, m_p, m_s, k_p, k_s, v_p, v_s)


_CACHE = {}


def run_cfg(cfg, inputs, trace=False):
    key = (cfg.NG, cfg.NPT, cfg.NSR, cfg.NH)
    if key not in _CACHE:
        _CACHE[key] = build_program(cfg)
    nc, stats = _CACHE[key]
    maps = prepare_inputs(cfg, inputs)
    res = run_bass_kernel_spmd(nc, maps, core_ids=list(range(cfg.NCORES)))
    return assemble_outputs(cfg, res.results)


def kernel(**inputs):
    inputs = {k: np.asarray(v) for k, v in inputs.items()}
    cfg = Cfg(NG=2, NPT=16, NSR=4, NH=2)
    return run_cfg(cfg, inputs)
```

```python
import contextlib
import numpy as np
import concourse.bass as bass
import concourse.mybir as mybir
from concourse.bass_utils import run_bass_kernel_spmd

F32 = mybir.dt.float32
BF16 = mybir.dt.bfloat16
ALU = mybir.AluOpType
AF = mybir.ActivationFunctionType
AX = mybir.AxisListType

D = 2048
DEPTH = 2
EPS = 1e-6
NEG = -30000.0
EPOCH = 30000
NTM = 2306
NFM = 768


class Buf:
    __slots__ = ("name", "w", "r")

    def __init__(self, name=""):
        self.name = name
        self.w = None
        self.r = {}

    def reset(self):
        self.w = None
        self.r = {}


class Ctx:
    def __init__(self, nc, stack):
        self.nc = nc
        self.stack = stack
        self.nsem = 0
        self.dsems = []
        self.qs = []

    def new_sem(self, name):
        self.nsem += 1
        return self.stack.enter_context(self.nc.semaphore(f"{name}_{self.nsem}"))


class DmaSem:
    def __init__(self, ctx, name):
        self.ctx = ctx
        self.name = name
        self.sem = ctx.new_sem(name)
        self.val = 0
        self.key = (id(self), 0)
        self.ep = 0
        ctx.dsems.append(self)

    def bump(self):
        if self.val + 16 > EPOCH:
            self.sem = self.ctx.new_sem(self.name)
            self.val = 0
            self.ep += 1
            self.key = (id(self), self.ep)
        self.val += 16
        return (self.key, self.sem, self.val)

    def last(self):
        return (self.key, self.sem, self.val) if self.val else None


class Q:
    def __init__(self, ctx, name):
        self.ctx = ctx
        self.name = name
        self.sem = ctx.new_sem(name)
        self.key = (name, 0)
        self.epoch = 0
        self.count = 0
        self.pending = False
        self.waited = {}
        self.prog = []
        self.log = []
        self.n_ins = 0
        self.n_wait = 0
        ctx.qs.append(self)

    def _need(self, reads, writes):
        need = {}

        def add(tok):
            k, s, v = tok
            if self.waited.get(k, 0) >= v:
                return
            if k not in need or need[k][2] < v:
                need[k] = tok

        for b in reads:
            if b.w is not None:
                add(b.w)
        for b in writes:
            if b.w is not None and (b.w[0] != self.key or self.name != "pe"):
                add(b.w)
            for k, tok in b.r.items():
                if k != self.key:
                    add(tok)
        return list(need.values())

    def _emit_waits(self, need):
        for (k, s, v) in need:
            self.prog.append(lambda e, s=s, v=v: e.wait_ge(s, v))
            self.waited[k] = v
            self.n_wait += 1
            self.log.append(f"  wait {k} >= {v}")

    def _mark(self, tok, reads, writes):
        for b in reads:
            old = b.r.get(tok[0])
            if old is None or old[2] < tok[2]:
                b.r[tok[0]] = tok
        for b in writes:
            b.w = tok
            b.r = {}

    def op(self, meth, reads=(), writes=(), signal=True, **kw):
        self._emit_waits(self._need(reads, writes))
        if self.count + 1 > EPOCH and not self.pending:
            self.epoch += 1
            self.sem = self.ctx.new_sem(self.name)
            self.key = (self.name, self.epoch)
            self.count = 0
        self.n_ins += 1
        self.log.append(f"{meth} W={[b.name for b in writes]} R={[b.name for b in reads]} sig={signal} cnt={self.count + (1 if signal else 0)}")
        if signal:
            self.prog.append(lambda e, meth=meth, kw=kw, sem=self.sem: getattr(e, meth)(**kw).then_inc(sem, 1))
            self.count += 1
            self.pending = False
            tok = (self.key, self.sem, self.count)
        else:
            self.prog.append(lambda e, meth=meth, kw=kw: getattr(e, meth)(**kw))
            self.pending = True
            tok = (self.key, self.sem, self.count + 1)
        self._mark(tok, reads, writes)
        return tok

    def dma(self, dsem, out, in_, reads=(), writes=(), fn=None, **kw):
        need = [t for t in self._need(reads, writes) if t[0] != dsem.key]
        self._emit_waits(need)
        tok = dsem.bump()
        self.log.append(f"DMA {dsem.name} -> {tok[2]} W={[b.name for b in writes]} R={[b.name for b in reads]}")
        if fn is None:
            self.prog.append(lambda e, out=out, in_=in_, kw=kw, sem=tok[1]:
                             e.dma_start(out=out, in_=in_, **kw).then_inc(sem, 16))
        else:
            self.prog.append(lambda e, fn=fn, sem=tok[1]: fn(e).then_inc(sem, 16))
        self.n_ins += 1
        self._mark(tok, reads, writes)
        return tok

    def wait_tok(self, tok):
        if tok is None:
            return
        if self.waited.get(tok[0], 0) < tok[2]:
            self.prog.append(lambda e, s=tok[1], v=tok[2]: e.wait_ge(s, v))
            self.waited[tok[0]] = tok[2]
            self.n_wait += 1

    def last(self):
        assert not self.pending
        return (self.key, self.sem, self.count) if self.count else None


class Arena:
    def __init__(self, nc, nbytes):
        self.nc = nc
        h = nc.alloc_sbuf_tensor("arena", [128, nbytes // 2], BF16)
        self.base = nc.lookup_mloc(h).addr
        self.size = nbytes
        self.off = 0
        self.n = 0

    def alloc(self, name, shape, dt):
        nb = int(np.prod(shape[1:])) * (4 if dt == F32 else 2)
        nb = (nb + 31) // 32 * 32
        assert self.off + nb <= self.size, f"SBUF arena overflow at {name}: {self.off}+{nb} > {self.size}"
        self.n += 1
        t = self.nc.alloc_sbuf_tensor_at(f"{name}_{self.n}", list(shape), dt, offset=self.base + self.off)
        self.off += nb
        return t

    def mark(self):
        return self.off

    def release(self, m):
        self.off = m


class Cfg:
    def __init__(self, NG=2, NPT=16, NSR=4, NH=2):
        self.NG = NG
        self.NPT = NPT
        self.NSR = NSR
        self.NST = NSR // 2
        self.TPR = NPT + self.NST
        self.TR = 128 * self.TPR
        self.GT = 4 * self.TR
        self.NS = 4 * NSR
        self.NPTILES = 4 * NPT
        self.NH = NH
        assert self.TPR % NH == 0
        self.HT = self.TPR // NH
        self.HTOK = self.HT * 128
        for tg in (512, 384, 256, 128):
            if self.HTOK % tg == 0:
                self.TG = tg
                break
        self.NTG = self.HTOK // self.TG
        self.SEQ = self.NPTILES * 128
        self.NCORES = 4 * NG
        self.NKV = 512 + self.NS * 64
        self.stop_after = None
        self.no_cc = False
        self.max_tiles = None
        self.dbg = set()


def build_program(cfg):
    nc = bass.Bass("TRN2", target_bir_lowering=False)
    TR, GT, NS, TPR = cfg.TR, cfg.GT, cfg.NS, cfg.TPR

    def din(name, shape):
        return nc.dram_tensor(name, list(shape), F32, kind="ExternalInput").ap()

    def dout(name, shape):
        return nc.dram_tensor(name, list(shape), F32, kind="ExternalOutput").ap()

    xg = din("xg", [GT, D])
    xown = din("xown", [TR, D])
    pown = din("pown", [DEPTH, TR, 256])
    w_tm = din("w_tm", [DEPTH, D, NTM])
    w_fm = din("w_fm", [DEPTH, D, NFM])
    w_gate = din("w_gate", [DEPTH, 3, 16, D, 128])
    w_br = din("w_br", [DEPTH, 3, 16, 1024, 128])
    w_out = din("w_out", [DEPTH, 4, D, 512])
    w_pg = din("w_pg", [DEPTH, 4, D, 512])
    w_pp = din("w_pp", [DEPTH, 4, 256, 512])
    gn = din("gn", [DEPTH, D])
    gp = din("gp", [DEPTH, D])
    wpool = din("wpool", [DEPTH, 256, 256])
    pscale = din("pscale", [DEPTH, 128, 2])
    psel = din("psel", [128, 4])
    prc = din("prc", [2, 128, 128])
    bgate = din("bgate", [DEPTH, 128, 2])
    mlg = din("mlg", [DEPTH, 256])
    aqg = din("aqg", [DEPTH, 256])
    akg = din("akg", [DEPTH, 256])
    abias = din("abias", [DEPTH, 128, 12, 128])
    cmask = din("cmask", [128, 7, 128])
    spool = din("spool", [DEPTH, 128, NS // 2, 2, 2, 16])
    sC = din("sC", [DEPTH, NS, 2, 128, 256])
    sn = din("sn", [DEPTH, NS, 128, 2])
    sm = din("sm", [DEPTH, 128, NS])
    skT = din("skT", [DEPTH, NS, 2, 128, 512])
    sv = din("sv", [DEPTH, NS, 512, 2, 128])

    y_own = dout("y_own", [TR, D])
    o_pool = dout("o_pool", [DEPTH, 1 + NS, 15, 256])
    o_c = dout("o_c", [DEPTH, 1 + NS, 256, 256])
    o_n = dout("o_n", [DEPTH, 1 + NS, 256])
    o_m = dout("o_m", [DEPTH, 1 + NS])
    o_k = dout("o_k", [DEPTH, cfg.NKV, 256])
    o_v = dout("o_v", [DEPTH, cfg.NKV, 256])

    assert TPR % 3 == 0 and cfg.TG == 384
    NCH = GT // 384
    CPR = TPR // 3
    ysrc = nc.dram_tensor("ysrc", [NCH, 768, 384], BF16).ap()
    ydst = nc.dram_tensor("ydst", [NCH, 4 * 768, 384], BF16).ap()
    xnew = nc.dram_tensor("xnew", [TR, D], F32).ap()
    pscr = nc.dram_tensor("pscr", [2, 128, 256], F32).ap()
    xown1 = nc.dram_tensor("xown1", [TR, D], F32).ap()
    xg1 = nc.dram_tensor("xg1", [TPR, 4 * 128, D], F32).ap()
    rgroups = [[4 * g + i for i in range(4)] for g in range(cfg.NG)]

    with contextlib.ExitStack() as st:
        cx = Ctx(nc, st)
        pe, act, dve, pool, sp = Q(cx, "pe"), Q(cx, "act"), Q(cx, "dve"), Q(cx, "pool"), Q(cx, "sp")
        ccsem = cx.new_sem("cc")
        ccstate = {"n": 0}
        ar = Arena(nc, 212800)
        allbufs = []

        def B(name=""):
            b = Buf(name)
            allbufs.append(b)
            return b

        pb0 = nc.alloc_psum_tensor("pb0", [128, 1024], BF16)
        pb1 = nc.alloc_psum_tensor("pb1", [128, 1024], BF16)
        pbs = [None, None] + [nc.alloc_psum_tensor(f"pb{i}", [128, 512], F32) for i in range(2, 8)]
        b_tr = [B("tr0"), B("tr1")]
        trh = [pb0[:, 0:512].rearrange("p (a b) -> p a b", b=128),
               pb1[:, 0:512].rearrange("p (a b) -> p a b", b=128)]
        trstate = {"i": 0}

        def next_tr():
            i = trstate["i"]
            trstate["i"] = 1 - i
            return trh[i], b_tr[i]

        cm = ar.alloc("cm", [128, 7, 128], F32); b_cm = B("cm")
        identb = ar.alloc("identb", [128, 128], BF16); b_identb = B("identb")
        gnb = ar.alloc("gnb", [128, D], F32); b_gnb = B("gnb")
        xs = [ar.alloc("xs0", [128, D], F32), ar.alloc("xs1", [128, D], F32)]
        b_xs = [B("xs0"), B("xs1")]
        s_xs = [DmaSem(cx, "sxs0"), DmaSem(cx, "sxs1")]
        hb = ar.alloc("hb", [128, D], BF16); b_hb = B("hb")
        junk = hb; b_junk = b_hb
        nst = ar.alloc("nst", [128, 8], F32); b_nst = B("nst")
        s_c = DmaSem(cx, "sconst")
        IDENT, TRIL, TRILBD, ONESBD, SELLO, SELHI, ONES = range(7)

        sp.dma(s_c, cm[:], cmask, writes=[b_cm])
        dve.op("tensor_copy", out=identb[:], in_=cm[:, IDENT, :], reads=[b_cm], writes=[b_identb])

        bsem = cx.new_sem("bar")
        bstate = {"n": 0}

        def issue_collective(src, dst):
            ccstate["n"] += 1
            if not cfg.no_cc:
                pool.prog.append(lambda e, src=src, dst=dst: e.collective_compute(
                    "AllGather", ALU.bypass, replica_groups=rgroups, ins=[src], outs=[dst]).then_inc(ccsem, 1))
            else:
                pool.prog.append(lambda e: e.sem_inc(ccsem, 1))

        def barrier():
            for q in cx.qs:
                if q is not pool:
                    pool.wait_tok(q.last())
            for ds in cx.dsems:
                pool.wait_tok(ds.last())
            if ccstate["n"]:
                pool.wait_tok((("cc", 0), ccsem, ccstate["n"]))
            bstate["n"] += 1
            pool.prog.append(lambda e: e.sem_inc(bsem, 1))
            tok = (("bar", 0), bsem, bstate["n"])
            for q in cx.qs:
                q.wait_tok(tok)
            for b in allbufs:
                b.reset()

        def norm_tile(xt, b_xt, gtile, b_g):
            act.op("activation", out=junk[:], in_=xt[:], func=AF.Square, accum_out=nst[:, 0:1],
                   reads=[b_xt], writes=[b_junk, b_nst])
            act.op("activation", out=nst[:, 1:2], in_=nst[:, 0:1], func=AF.Sqrt, scale=1.0 / D, bias=EPS,
                   reads=[b_nst], writes=[b_nst])
            dve.op("reciprocal", out=nst[:, 2:3], in_=nst[:, 1:2], reads=[b_nst], writes=[b_nst])
            dve.op("scalar_tensor_tensor", out=hb[:], in0=xt[:], scalar=nst[:, 2:3], in1=gtile[:],
                   op0=ALU.mult, op1=ALU.mult, reads=[b_xt, b_nst, b_g], writes=[b_hb])

        evac_flip = {"i": 0}

        def evac_copy(out, in_, reads, writes):
            evac_flip["i"] ^= 1
            if evac_flip["i"]:
                act.op("copy", out=out, in_=in_, reads=reads, writes=writes)
            else:
                dve.op("tensor_copy", out=out, in_=in_, reads=reads, writes=writes)

        def transpose_to(dst3, b_dst, src, b_src, nblk):
            i = 0
            while i < nblk:
                n = min(4, nblk - i)
                tr, btr = next_tr()
                for k in range(n):
                    pe.op("transpose", out=tr[:, k, :], in_=src[:, (i + k) * 128:(i + k + 1) * 128],
                          identity=identb[:], reads=[b_src, b_identb], writes=[btr], signal=(k == n - 1))
                evac_copy(dst3[:, i:i + n, :], tr[:, 0:n, :], [btr], [b_dst])
                i += n

        m_persist = ar.mark()

        for layer in range(DEPTH):
            xin = xg if layer == 0 else xg1
            xsrc_own = xown if layer == 0 else xown1
            xdest = xown1 if layer == 0 else y_own

            ar.release(m_persist)
            Wtm = ar.alloc("Wtm", [128, 16, NTM], BF16); b_Wtm = B("Wtm")
            Wfm = ar.alloc("Wfm", [128, 16, NFM], BF16); b_Wfm = B("Wfm")
            wpl = ar.alloc("wpl", [128, 2, 256], BF16); b_wpl = B("wpl")
            s_w = DmaSem(cx, "sw"); s_w2 = DmaSem(cx, "sw2"); s_w3 = DmaSem(cx, "sw3")
            for kq in range(4):
                pool.dma(s_w, Wtm[:, kq * 4:(kq + 1) * 4, :],
                         w_tm[layer, kq * 512:(kq + 1) * 512, :].rearrange("(k p) c -> p k c", p=128), writes=[b_Wtm])
            for kq in range(2):
                pool.dma(s_w2, Wfm[:, kq * 8:(kq + 1) * 8, :],
                         w_fm[layer, kq * 1024:(kq + 1) * 1024, :].rearrange("(k p) c -> p k c", p=128), writes=[b_Wfm])
            pool.dma(s_w3, wpl[:], wpool[layer].rearrange("(k p) c -> p k c", p=128), writes=[b_wpl])
            s_p = DmaSem(cx, "sparam")
            sp.dma(s_p, gnb[:], gn[layer].partition_broadcast(128), writes=[b_gnb])
            psc = ar.alloc("psc", [128, 2], F32); b_psc = B()
            pse = ar.alloc("pse", [128, 4], F32); b_pse = B()
            prc_t = ar.alloc("prc", [128, 2, 128], F32); b_prc = B()
            bgt = ar.alloc("bgt", [128, 2], F32); b_bgt = B()
            mlgt = ar.alloc("mlgt", [128, 256], F32); b_mlgt = B()
            aqgt = ar.alloc("aqgt", [128, 256], F32); b_aqgt = B()
            akgt = ar.alloc("akgt", [128, 256], F32); b_akgt = B()
            smt = ar.alloc("smt", [128, NS], F32); b_smt = B()
            sp.dma(s_p, psc[:], pscale[layer], writes=[b_psc])
            sp.dma(s_p, pse[:], psel, writes=[b_pse])
            sp.dma(s_p, prc_t[:], prc.rearrange("k p t -> p k t"), writes=[b_prc])
            sp.dma(s_p, bgt[:], bgate[layer], writes=[b_bgt])
            sp.dma(s_p, mlgt[:], mlg[layer].partition_broadcast(128), writes=[b_mlgt])
            sp.dma(s_p, aqgt[:], aqg[layer].partition_broadcast(128), writes=[b_aqgt])
            sp.dma(s_p, akgt[:], akg[layer].partition_broadcast(128), writes=[b_akgt])
            sp.dma(s_p, smt[:], sm[layer], writes=[b_smt])
            for bb_ in (b_gnb, b_psc, b_pse, b_prc, b_bgt, b_mlgt, b_aqgt, b_akgt, b_smt):
                bb_.w = s_p.last()
            bhi = ar.alloc("bhi", [128, 12, 128], BF16); b_bhi = B()
            blo = ar.alloc("blo", [128, 12, 128], BF16); b_blo = B()
            stg = xs[1][:, 0:1536].rearrange("p (a b) -> p a b", b=128)
            sp.dma(s_xs[1], stg, abias[layer], writes=[b_xs[1]])
            dve.op("tensor_copy", out=bhi[:], in_=stg, reads=[b_xs[1]], writes=[b_bhi])
            dve.op("tensor_tensor", out=blo[:], in0=stg, in1=bhi[:], op=ALU.subtract,
                   reads=[b_xs[1], b_bhi], writes=[b_blo])

            hT0 = ar.alloc("hT0", [128, 16, 128], BF16)
            hT = [hT0, hT0]
            b_hT0 = B("hT")
            b_hT = [b_hT0, b_hT0]
            def dbl(name, shape, dt):
                return [ar.alloc(name + "0", shape, dt), ar.alloc(name + "1", shape, dt)], [B(name + "0"), B(name + "1")]
            gat, b_gat = dbl("gat", [128, 2], F32)
            pus, b_pus = dbl("pus", [128, 256], F32)
            puT, b_puT = dbl("puT", [128, 2, 128], F32)
            ktok, b_ktok = dbl("ktok", [128, 256], BF16)
            vsb, b_vsb = dbl("vsb", [128, 256], BF16)
            sgmo, b_sgmo = dbl("sgmo", [128, 256], BF16)
            slmz, b_slmz = dbl("slmz", [128, 256], BF16)
            aqs, b_aqs = dbl("aqs", [128, 256], F32)
            aks, b_aks = dbl("aks", [128, 256], F32)
            avs, b_avs = dbl("avs", [128, 256], F32)
            slaz, b_slaz = dbl("slaz", [128, 256], BF16)
            slpz, b_slpz = dbl("slpz", [128, 2, 128], BF16)
            qTm, b_qTm = dbl("qTm", [128, 2, 128], BF16)
            yst, b_yst = dbl("yst", [128, 2, 3, 128], BF16)
            s_yst = [DmaSem(cx, "syst0"), DmaSem(cx, "syst1")]
            khs, b_khs = dbl("khs", [128, 256], F32)
            s_out = [DmaSem(cx, "sout0"), DmaSem(cx, "sout1")]
            s_pscr = [DmaSem(cx, "spscr0"), DmaSem(cx, "spscr1")]
            qTz = [ar.alloc("qTz0", [128, 2, 128], BF16), ar.alloc("qTz1", [128, 2, 128], BF16)]
            b_qTz = [B(), B()]
            ext = ar.alloc("ext", [128, 2, 144], F32); b_ext = B("ext")
            exs = ar.alloc("exs", [128, 2, 2, 80], F32); b_exs = B("exs")
            s_exs = DmaSem(cx, "sexs")
            psA = ar.alloc("psA", [128, 2, 160], F32); psB = ar.alloc("psB", [128, 2, 160], F32)
            pacc = ar.alloc("pacc", [128, 2, 128], F32); b_pw = B("poolwork")
            hg = pacc[:].rearrange("p h t -> p (h t)"); b_hg = b_pw
            mT = ar.alloc("mT", [128, 2, 128], BF16); b_mT = B("mT")
            gw = ar.alloc("gw", [128, 16], F32); b_gw = B("gw")
            c_gw = [B(f"gw{i}") for i in range(16)]
            c_aw = [B(f"aw{i}") for i in range(16)]
            c_g0 = [B(f"g0{i}") for i in range(16)]
            g0 = ar.alloc("g0", [1, 16], F32); b_g0 = B("g0")
            edb = ar.alloc("edb", [128, 1], BF16)
            vpx = ar.alloc("vpx", [128, 257], BF16); b_vpx = B("vpx")
            kTm = ar.alloc("kTm", [128, 2, 128], BF16); b_kTm = B("kTm")
            PTm = ar.alloc("PTm", [128, 128], BF16); b_PTm = B("PTm")
            Ch = ar.alloc("Ch", [128, 2, 257], F32); b_Ch = B("Ch")
            Chb = ar.alloc("Chb", [128, 2, 257], BF16); b_Chb = B("Chb")
            Cs = [ar.alloc("Cs0", [128, 2, 257], F32), ar.alloc("Cs1", [128, 2, 257], F32)]
            b_Cs = [B(), B()]
            s_Cs = [DmaSem(cx, "scs0"), DmaSem(cx, "scs1")]
            Csb = [ar.alloc("Csb0", [128, 2, 257], BF16), ar.alloc("Csb1", [128, 2, 257], BF16)]
            b_Csb = [B(), B()]
            cst = ar.alloc("cst", [128, 2, 257], F32); b_cst = B("cst")
            s_cst = DmaSem(cx, "scst")
            yml = ar.alloc("yml", [128, 256], BF16); b_yml = B("yml")
            yat = ar.alloc("yat", [128, 256], BF16); b_yat = B("yat")
            mrow = ar.alloc("mrow", [1, 1 + NS], F32); b_mrow = B("mrow")
            s_mrow = DmaSem(cx, "smrow")
            qhb = ar.alloc("qhb", [128, 256], BF16); b_qhb = B("qhb")
            khb = ar.alloc("khb", [128, 256], BF16); b_khb = B("khb")
            qTa = ar.alloc("qTa", [128, 2, 128], BF16); b_qTa = B("qTa")
            kring = ar.alloc("kring", [128, 8, 2, 128], BF16); b_kring = [B() for _ in range(8)]
            vring = ar.alloc("vring", [128, 8, 2, 129], BF16); b_vring = [B() for _ in range(8)]
            kTc = ar.alloc("kTc", [128, 2, 2, 512], BF16); b_kTc = [B(), B()]
            Vc = ar.alloc("Vc", [128, 2, 4, 2, 129], BF16); b_Vc = [B(), B()]
            s_kc = [DmaSem(cx, "skc0"), DmaSem(cx, "skc1")]
            PTp = ar.alloc("PTp", [128, 5, 128], BF16); b_PTp = B("PTp")
            PTs = [ar.alloc("PTs0", [128, 4, 128], BF16), ar.alloc("PTs1", [128, 4, 128], BF16)]
            b_PTs = [B(), B()]
            aw = ar.alloc("aw", [128, 16], F32); b_aw = B("aw")

            ptm = [pbs[2], pbs[3], pbs[4]]; b_ptm = [B("ptm0"), B("ptm1"), B("ptm2")]
            ptm_i = {"i": 0}

            def next_tm():
                i = ptm_i["i"]
                ptm_i["i"] = (i + 1) % 3
                return ptm[i], b_ptm[i]
            b_bank5 = B("bank5")
            pN = pbs[5][:, 0:257]; b_pN = b_bank5
            pS = pbs[5][:, 257:385]; b_pS = b_bank5
            pG = pbs[5][:, 385:401]; b_pG = b_bank5
            pAS = pbs[6]; b_pAS = B("pAS")
            b_bank7 = B("bank7")
            pPV = pbs[7][:, 0:258].rearrange("p (h c) -> p h c", c=129); b_pPV = b_bank7
            pA5 = pbs[7][:, 258:386]; b_pA5 = b_bank7

            dve.op("memset", ap=ext[:], constant=0.0, writes=[b_ext])
            dve.op("memset", ap=Ch[:], constant=0.0, writes=[b_Ch])
            dve.op("memset", ap=Chb[:], constant=0.0, writes=[b_Chb])
            dve.op("memset", ap=g0[:], constant=0.0, writes=[b_g0] + c_g0)
            dve.op("memset", ap=vring[:], constant=1.0, writes=b_vring)
            dve.op("memset", ap=Vc[:], constant=1.0, writes=b_Vc)
            for s in range(2):
                dve.op("memset", ap=qTz[s][:], constant=0.0, writes=[b_qTz[s]])
                dve.op("memset", ap=PTs[s][:], constant=0.0, writes=[b_PTs[s]])

            tiles = []
            for r in range(4):
                for l in range(TPR):
                    tiles.append((r, l))

            def load_x(T):
                b = T % 2
                r_, l_ = tiles[T]
                src_ = xg[T * 128:(T + 1) * 128, :] if layer == 0 else xg1[l_, r_ * 128:(r_ + 1) * 128, :]
                sp.dma(s_xs[b], xs[b][:], src_, writes=[b_xs[b]])

            def load_sample_state(u):
                for s in range(2):
                    q = 2 * u + s
                    pool.dma(s_kc[s], kTc[:, s, :, :], skT[layer, q].rearrange("h d k -> d h k"),
                             writes=[b_kTc[s]])
                    for h in range(2):
                        pool.dma(s_kc[s], Vc[:, s, :, h, 0:128],
                                 sv[layer, q, :, h, :].rearrange("(kb p) d -> p kb d", p=128), writes=[b_Vc[s]])
                    b_kTc[s].w = s_kc[s].last(); b_Vc[s].w = s_kc[s].last()
                    sp.dma(s_Cs[s], Cs[s][:, :, 0:256], sC[layer, q].rearrange("k p v -> p k v"), writes=[b_Cs[s]])
                    sp.dma(s_Cs[s], Cs[s][:, :, 256], sn[layer, q], writes=[b_Cs[s]], allow_slow_non_contiguous=True)
                sp.dma(s_exs, exs[:, :, :, 0:16], spool[layer, :, u], writes=[b_exs])

            if "sonly" in cfg.dbg:
                tiles = [t_ for t_ in tiles if t_[1] >= cfg.NPT]
            if "ponly" in cfg.dbg:
                tiles = [t_ for t_ in tiles if t_[1] < cfg.NPT]
            ytoks = []
            ntiles_run = len(tiles) if cfg.max_tiles is None else min(len(tiles), cfg.max_tiles)

            def tinfo(T):
                r, l = tiles[T]
                is_prompt = l < cfg.NPT
                p = r * cfg.NPT + l if is_prompt else None
                u = None if is_prompt else r * cfg.NST + (l - cfg.NPT)
                out_kv = (not is_prompt) or p >= cfg.NPTILES - 4
                out_pool = (not is_prompt) or p == cfg.NPTILES - 1
                if "noout" in cfg.dbg:
                    out_kv = out_pool = False
                if "allpool" in cfg.dbg:
                    out_pool = True
                slot = (p % 6) if is_prompt else 6 + (u % 2)
                return T % 2, is_prompt, p, u, out_kv, out_pool, slot

            def front(T):
                b, is_prompt, p, u, out_kv, out_pool, slot = tinfo(T)
                if T + 1 < ntiles_run:
                    load_x(T + 1)
                norm_tile(xs[b], b_xs[b], gnb, b_gnb)
                transpose_to(hT[b], b_hT[b], hb, b_hb, 16)

                def tm_block(c0, n):
                    ps, bps = next_tm()
                    for k in range(16):
                        pe.op("matmul", out=ps[:, 0:n], lhsT=hT[b][:, k, :], rhs=Wtm[:, k, c0:c0 + n],
                              start=(k == 0), stop=(k == 15), reads=[b_hT[b], b_Wtm], writes=[bps],
                              signal=(k == 15))
                    return ps, bps
                ps, bps = tm_block(2304, 2)
                dve.op("tensor_tensor", out=gat[b][:], in0=ps[:, 0:2], in1=bgt[:], op=ALU.add,
                       reads=[bps, b_bgt], writes=[b_gat[b]])
                ps, bps = tm_block(0, 512)
                if out_pool:
                    act.op("copy", out=pus[b][:], in_=ps[:, 0:256], reads=[bps], writes=[b_pus[b]])
                act.op("activation", out=ktok[b][:], in_=ps[:, 256:512], func=AF.Copy, scale=1.0 / 16.0,
                       reads=[bps], writes=[b_ktok[b]])
                ps, bps = tm_block(512, 512)
                dve.op("tensor_copy", out=vsb[b][:], in_=ps[:, 0:256], reads=[bps], writes=[b_vsb[b]])
                act.op("activation", out=sgmo[b][:], in_=ps[:, 256:512], func=AF.Sigmoid, reads=[bps], writes=[b_sgmo[b]])
                ps, bps = tm_block(1024, 512)
                act.op("activation", out=slmz[b][:], in_=ps[:, 0:256], func=AF.Silu, reads=[bps], writes=[b_slmz[b]])
                dve.op("tensor_copy", out=aqs[b][:], in_=ps[:, 256:512], reads=[bps], writes=[b_aqs[b]])
                ps, bps = tm_block(1536, 512)
                dve.op("tensor_copy", out=aks[b][:], in_=ps[:, 0:256], reads=[bps], writes=[b_aks[b]])
                dve.op("tensor_copy", out=vring[:, slot, :, 0:128], in_=ps[:, 256:512].rearrange("p (h d) -> p h d", d=128),
                       reads=[bps], writes=[b_vring[slot]])
                if out_kv:
                    dve.op("tensor_copy", out=avs[b][:], in_=ps[:, 256:512], reads=[bps], writes=[b_avs[b]])
                ps, bps = tm_block(2048, 256)
                act.op("activation", out=slaz[b][:], in_=ps[:, 0:256], func=AF.Silu, reads=[bps], writes=[b_slaz[b]])

                nseg, Ls = (1, 128) if is_prompt else (2, 64)
                for pair in range(3):
                    psf, bpsf = next_tm()
                    pfm = psf[:, 0:256].rearrange("p (a b) -> p a b", b=128)
                    for i2 in range(2):
                        cbk = pair * 2 + i2
                        for k in range(16):
                            pe.op("matmul", out=pfm[:, i2, :], lhsT=Wfm[:, k, cbk * 128:(cbk + 1) * 128],
                                  rhs=hT[b][:, k, :], start=(k == 0), stop=(k == 15),
                                  reads=[b_hT[b], b_Wfm], writes=[bpsf], signal=(k == 15 and i2 == 1))
                    src = pfm[:, 0:2, :]
                    if pair == 0:
                        dve.op("tensor_copy", out=puT[b][:], in_=src, reads=[bpsf], writes=[b_puT[b]])
                    elif pair == 1:
                        act.op("activation", out=slpz[b][:], in_=src, func=AF.Silu, reads=[bpsf], writes=[b_slpz[b]])
                    else:
                        act.op("copy", out=qTm[b][:], in_=src, reads=[bpsf], writes=[b_qTm[b]])

            def back(T):
                b, is_prompt, p, u, out_kv, out_pool, slot = tinfo(T)
                last_prompt = is_prompt and p == cfg.NPTILES - 1
                MASK = TRIL if is_prompt else TRILBD
                if not is_prompt and "nosload" not in cfg.dbg:
                    load_sample_state(u)
                if is_prompt:
                    dve.op("tensor_copy", out=ext[:, :, 16:144], in_=puT[b][:], reads=[b_puT[b]], writes=[b_ext])
                else:
                    for h in range(2):
                        dve.op("tensor_copy", out=exs[:, h, :, 16:80],
                               in_=puT[b][:, h, :].rearrange("p (s t) -> p s t", t=64),
                               reads=[b_puT[b]], writes=[b_exs])
                    for s in range(2):
                        dve.op("tensor_copy", out=qTz[s][:, :, 64 * s:64 * s + 64],
                               in_=qTm[b][:, :, 64 * s:64 * s + 64],
                               reads=[b_qTm[b]], writes=[b_qTz[s]])

                def gen_pool():
                    nseg_, Ls_ = (1, 128) if is_prompt else (2, 64)
                    W = 16 + Ls_
                    for sg_ in range(nseg_):
                        if is_prompt:
                            def sl(t, a, c, sg_=sg_):
                                return t[:, :, a:c]
                            E, bE = ext, b_ext
                            cur = ext[:, :, 16:144]
                            accv = pacc[:]
                            mTv = mT[:]
                        else:
                            def sl(t, a, c, sg_=sg_):
                                if t is exs:
                                    return exs[:, :, sg_, a:c]
                                return t[:, :, sg_ * 80 + a:sg_ * 80 + c]
                            E, bE = exs, b_exs
                            cur = exs[:, :, sg_, 16:80]
                            accv = pacc[:, :, 64 * sg_:64 * sg_ + 64]
                            mTv = mT[:, :, 64 * sg_:64 * sg_ + 64]
                        rd = [bE, b_pw]
                        dve.op("tensor_tensor", out=sl(psA, 1, W), in0=sl(E, 1, W), in1=sl(E, 0, W - 1), op=ALU.add, reads=rd, writes=[b_pw])
                        yield
                        dve.op("tensor_scalar", out=accv, in0=sl(psA, 16, W), scalar1=pse[:, 0:1], scalar2=None, op0=ALU.mult,
                               reads=[b_pw, b_pse], writes=[b_pw])
                        yield
                        dve.op("tensor_tensor", out=sl(psB, 3, W), in0=sl(psA, 3, W), in1=sl(psA, 1, W - 2), op=ALU.add, reads=rd, writes=[b_pw])
                        yield
                        dve.op("scalar_tensor_tensor", out=accv, in0=sl(psB, 16, W), scalar=pse[:, 1:2], in1=accv,
                               op0=ALU.mult, op1=ALU.add, reads=[b_pw, b_pse], writes=[b_pw])
                        yield
                        dve.op("tensor_tensor", out=sl(psA, 7, W), in0=sl(psB, 7, W), in1=sl(psB, 3, W - 4), op=ALU.add, reads=rd, writes=[b_pw])
                        yield
                        dve.op("scalar_tensor_tensor", out=accv, in0=sl(psA, 16, W), scalar=pse[:, 2:3], in1=accv,
                               op0=ALU.mult, op1=ALU.add, reads=[b_pw, b_pse], writes=[b_pw])
                        yield
                        dve.op("tensor_tensor", out=sl(psB, 15, W), in0=sl(psA, 15, W), in1=sl(psA, 7, W - 8), op=ALU.add, reads=rd, writes=[b_pw])
                        yield
                        dve.op("scalar_tensor_tensor", out=accv, in0=sl(psB, 16, W), scalar=pse[:, 3:4], in1=accv,
                               op0=ALU.mult, op1=ALU.add, reads=[b_pw, b_pse], writes=[b_pw])
                        yield
                        if is_prompt and p == 0:
                            for h_ in range(2):
                                dve.op("tensor_tensor", out=pacc[:, h_, :], in0=pacc[:, h_, :], in1=prc_t[:, 0, :], op=ALU.mult,
                                       reads=[b_pw, b_prc], writes=[b_pw])
                                yield
                        else:
                            dve.op("tensor_scalar", out=accv, in0=accv, scalar1=prc_t[:, 1, 0:1], scalar2=None, op0=ALU.mult,
                                   reads=[b_pw, b_prc], writes=[b_pw])
                            yield
                        dve.op("tensor_tensor", out=mTv, in0=accv, in1=cur, op=ALU.subtract, reads=[b_pw, bE], writes=[b_mT])
                        yield
                    if is_prompt:
                        dve.op("tensor_copy", out=ext[:, :, 0:16], in_=ext[:, :, 128:144], reads=[b_ext, b_pw], writes=[b_ext])
                        yield
                    ps, bps = next_tm()
                    for ob in range(2):
                        for kc in range(2):
                            pe.op("matmul", out=ps[:, ob * 128:(ob + 1) * 128], lhsT=wpl[:, kc, ob * 128:(ob + 1) * 128],
                                  rhs=mT[:, kc, :], start=(kc == 0), stop=(kc == 1), reads=[b_wpl, b_mT], writes=[bps],
                                  signal=(ob == 1 and kc == 1))
                            yield
                    for ob in range(2):
                        dve.op("scalar_tensor_tensor", out=yst[b][:, ob, 0, :], in0=ps[:, ob * 128:(ob + 1) * 128],
                               scalar=psc[:, ob:ob + 1], in1=slpz[b][:, ob, :], op0=ALU.mult, op1=ALU.mult,
                               reads=[bps, b_psc, b_slpz[b]], writes=[b_yst[b]])
                        yield


                    yield
                def gen_ml():
                    act.op("activation", out=gw[:, 0:1], in_=gat[b][:, 1:2], func=AF.Exp, scale=-1.0,
                           reads=[b_gat[b]], writes=[*c_gw[0:1]])
                    yield
                    act.op("activation", out=gw[:, 1:2], in_=gw[:, 0:1], func=AF.Ln, bias=1.0, reads=[*c_gw[0:1]], writes=[*c_gw[1:2]])
                    yield
                    pe.op("matmul", out=pG[:, 0:1], lhsT=cm[:, MASK, :], rhs=gw[:, 1:2], start=True, stop=True,
                          reads=[b_cm, *c_gw[1:2]], writes=[b_pG], signal=False)
                    yield
                    pe.op("matmul", out=pG[:, 1:2], lhsT=cm[:, (ONES if is_prompt else ONESBD), :], rhs=gw[:, 1:2],
                          start=True, stop=True, reads=[b_cm, *c_gw[1:2]], writes=[b_pG], signal=is_prompt)
                    yield
                    if not is_prompt:
                        pe.op("matmul", out=pG[:, 2:3], lhsT=cm[:, SELLO, :], rhs=gw[:, 1:2], start=True, stop=True,
                              reads=[b_cm, *c_gw[1:2]], writes=[b_pG], signal=False)
                        yield
                        pe.op("matmul", out=pG[:, 3:4], lhsT=cm[:, SELHI, :], rhs=gw[:, 1:2], start=True, stop=True,
                              reads=[b_cm, *c_gw[1:2]], writes=[b_pG])
                        yield
                    dve.op("tensor_tensor", out=gw[:, 2:3], in0=gat[b][:, 0:1], in1=pG[:, 0:1], op=ALU.add,
                           reads=[b_gat[b], b_pG], writes=[*c_gw[2:3]])
                    yield
                    act.op("activation", out=gw[:, 3:4], in_=gw[:, 2:3], func=AF.Exp, reads=[*c_gw[2:3]], writes=[*c_gw[3:4]])
                    yield
                    act.op("activation", out=gw[:, 4:5], in_=pG[:, 0:1], func=AF.Exp, reads=[b_pG], writes=[*c_gw[4:5]])
                    yield
                    if is_prompt:
                        act.op("activation", out=gw[:, 5:6], in_=pG[:, 1:2], func=AF.Exp, scale=-1.0,
                               reads=[b_pG], writes=[*c_gw[5:6]])
                        yield
                    dve.op("tensor_scalar", out=vpx[:, 0:256], in0=vsb[b][:], scalar1=gw[:, 3:4], scalar2=None, op0=ALU.mult,
                           reads=[b_vsb[b], *c_gw[3:4]], writes=[b_vpx])
                    yield
                    dve.op("tensor_copy", out=vpx[:, 256:257], in_=gw[:, 3:4], reads=[*c_gw[3:4]], writes=[b_vpx])
                    yield
                    transpose_to(kTm, b_kTm, ktok[b], b_ktok[b], 2)
                    yield
                    for kc in range(2):
                        pe.op("matmul", out=pS, lhsT=kTm[:, kc, :], rhs=qTm[b][:, kc, :], start=(kc == 0), stop=(kc == 1),
                              reads=[b_kTm, b_qTm[b]], writes=[b_pS], signal=(kc == 1))
                        yield
                    dve.op("tensor_tensor", out=PTm[:], in0=pS, in1=cm[:, MASK, :], op=ALU.mult,
                           reads=[b_pS, b_cm], writes=[b_PTm])
                    yield
                    pe.op("matmul", out=pN, lhsT=PTm[:], rhs=vpx[:], start=True, stop=False,
                          reads=[b_PTm, b_vpx], writes=[b_pN], signal=False)
                    yield
                    if is_prompt:
                        for kc in range(2):
                            pe.op("matmul", out=pN, lhsT=qTm[b][:, kc, :], rhs=Chb[:, kc, :], start=False, stop=(kc == 1),
                                  reads=[b_qTm[b], b_Chb], writes=[b_pN], signal=(kc == 1))
                            yield
                    else:
                        for s in range(2):
                            q = 2 * u + s
                            act.op("activation", out=gw[:, 12 + s:13 + s], in_=smt[:, q:q + 1], func=AF.Exp,
                                   reads=[b_smt], writes=[*c_gw[12 + s:13 + s]])
                            yield
                            dve.op("tensor_scalar", out=Cs[s][:], in0=Cs[s][:], scalar1=gw[:, 12 + s:13 + s], scalar2=None,
                                   op0=ALU.mult, reads=[b_Cs[s], *c_gw[12 + s:13 + s]], writes=[b_Cs[s]])
                            yield
                            act.op("copy", out=Csb[s][:], in_=Cs[s][:], reads=[b_Cs[s]], writes=[b_Csb[s]])
                            yield
                        for s in range(2):
                            for kc in range(2):
                                last = (s == 1 and kc == 1)
                                pe.op("matmul", out=pN, lhsT=qTz[s][:, kc, :], rhs=Csb[s][:, kc, :], start=False, stop=last,
                                      reads=[b_qTz[s], b_Csb[s]], writes=[b_pN], signal=last)
                                yield
                    dve.op("tensor_copy", out=gw[:, 6:7], in_=pN[:, 256:257], reads=[b_pN], writes=[*c_gw[6:7]])
                    yield
                    dve.op("scalar_tensor_tensor", out=gw[:, 6:7], in0=gw[:, 6:7], scalar=-1.0, in1=gw[:, 6:7],
                           op0=ALU.mult, op1=ALU.max, reads=[*c_gw[6:7], *c_gw[6:7]], writes=[*c_gw[6:7]])
                    yield
                    dve.op("tensor_tensor", out=gw[:, 6:7], in0=gw[:, 6:7], in1=gw[:, 4:5], op=ALU.max, reads=[*c_gw[6:7], *c_gw[4:5]], writes=[*c_gw[6:7]])
                    yield
                    dve.op("reciprocal", out=gw[:, 7:8], in_=gw[:, 6:7], reads=[*c_gw[6:7]], writes=[*c_gw[7:8]])
                    yield
                    dve.op("scalar_tensor_tensor", out=hg[:], in0=pN[:, 0:256], scalar=gw[:, 7:8], in1=sgmo[b][:],
                           op0=ALU.mult, op1=ALU.mult, reads=[b_pN, b_sgmo[b], *c_gw[7:8]], writes=[b_hg])
                    yield
                    act.op("activation", out=junk[:, 0:256], in_=hg[:], func=AF.Square, accum_out=gw[:, 8:9],
                           reads=[b_hg], writes=[b_junk, *c_gw[8:9]])
                    yield
                    act.op("activation", out=gw[:, 9:10], in_=gw[:, 8:9], func=AF.Sqrt, scale=1.0 / 256.0, bias=EPS,
                           reads=[*c_gw[8:9]], writes=[*c_gw[9:10]])
                    yield
                    dve.op("reciprocal", out=gw[:, 9:10], in_=gw[:, 9:10], reads=[*c_gw[9:10]], writes=[*c_gw[9:10]])
                    yield
                    dve.op("scalar_tensor_tensor", out=hg[:], in0=hg[:], scalar=gw[:, 9:10], in1=mlgt[:],
                           op0=ALU.mult, op1=ALU.mult, reads=[b_hg, b_mlgt, *c_gw[9:10]], writes=[b_hg])
                    yield
                    dve.op("tensor_tensor", out=yml[:], in0=hg[:], in1=slmz[b][:], op=ALU.mult,
                           reads=[b_hg, b_slmz[b]], writes=[b_yml])
                    yield
                    psd, b_pD = next_tm()
                    pD = psd[0:1, 0:128]
                    pe.op("matmul", out=pD, lhsT=gw[:, 2:3], rhs=cm[:, IDENT, :], start=True, stop=True,
                          reads=[b_cm, *c_gw[2:3]], writes=[b_pD])
                    yield
                    if is_prompt:
                        dve.op("reduce_max", out=g0[:, 2:3], in_=pD, axis=AX.X, reads=[b_pD], writes=[*c_g0[2:3]])
                        yield
                        dve.op("tensor_tensor", out=g0[:, 4:5], in0=g0[:, 2:3], in1=g0[:, 0:1], op=ALU.add, reads=[*c_g0[2:3], *c_g0[0:1]], writes=[*c_g0[4:5]])
                        yield
                        dve.op("tensor_tensor", out=g0[:, 1:2], in0=g0[:, 1:2], in1=g0[:, 4:5], op=ALU.max, reads=[*c_g0[1:2], *c_g0[4:5]], writes=[*c_g0[1:2]])
                        yield
                        dve.op("tensor_tensor", out=g0[:, 0:1], in0=g0[:, 0:1], in1=pG[0:1, 1:2], op=ALU.add,
                               reads=[b_pG, *c_g0[0:1]], writes=[*c_g0[0:1]])
                        yield
                    elif "nosmb" in cfg.dbg:
                        dve.op("memset", ap=gw[:, 10:12], constant=1.0, writes=[])
                        yield
                    else:
                        dve.op("reduce_max", out=g0[:, 2:4], in_=pD.rearrange("p (s t) -> p s t", t=64), axis=AX.X,
                               reads=[b_pD], writes=[*c_g0[2:4]])
                        yield
                        dve.op("tensor_tensor", out=g0[:, 4:6], in0=g0[:, 2:4], in1=smt[0:1, 2 * u:2 * u + 2], op=ALU.max,
                               reads=[b_smt, *c_g0[2:4]], writes=[*c_g0[4:6]])
                        yield
                        dve.op("tensor_tensor", out=mrow[:, 1 + 2 * u:3 + 2 * u], in0=g0[:, 4:6], in1=pG[0:1, 2:4], op=ALU.subtract,
                               reads=[b_pG, *c_g0[4:6]], writes=[b_mrow])
                        yield
                        act.op("activation", out=g0[:, 8:10], in_=g0[:, 4:6], func=AF.Exp, scale=-1.0, reads=[*c_g0[4:6]], writes=[*c_g0[8:10]])
                        yield
                        psb, b_pBc = next_tm()
                        pBc = psb[:, 0:8]
                        if "nopbc" in cfg.dbg:
                            dve.op("memset", ap=gw[:, 10:12], constant=1.0, writes=[])
                            yield
                        else:
                            pe.op("matmul", out=pBc[:, 0:2], lhsT=cm[0:1, ONES, :], rhs=g0[:, 8:10], start=True, stop=True,
                                  reads=[b_cm, *c_g0[8:10]], writes=[b_pBc])
                            yield
                            dve.op("tensor_copy", out=gw[:, 10:12], in_=pBc[:, 0:2], reads=[b_pBc], writes=[*c_gw[10:12]])
                            yield
                    if is_prompt:
                        for kc in range(2):
                            ps, bps = next_tm()
                            pe.op("matmul", out=ps[:, 0:257], lhsT=ktok[b][:, kc * 128:(kc + 1) * 128], rhs=vpx[:],
                                  start=True, stop=True, reads=[b_ktok[b], b_vpx], writes=[bps])
                            yield
                            dve.op("tensor_scalar", out=Ch[:, kc, :], in0=Ch[:, kc, :], scalar1=gw[:, 5:6], scalar2=None,
                                   op0=ALU.mult, reads=[b_Ch, *c_gw[5:6]], writes=[b_Ch])
                            yield
                            dve.op("scalar_tensor_tensor", out=Ch[:, kc, :], in0=ps[:, 0:257], scalar=gw[:, 5:6], in1=Ch[:, kc, :],
                                   op0=ALU.mult, op1=ALU.add, reads=[bps, b_Ch, *c_gw[5:6]], writes=[b_Ch])
                            yield
                            act.op("copy", out=Chb[:, kc, :], in_=Ch[:, kc, :], reads=[b_Ch], writes=[b_Chb])
                            yield
                        if last_prompt:
                            dve.op("tensor_tensor", out=mrow[:, 0:1], in0=g0[:, 1:2], in1=g0[:, 0:1], op=ALU.subtract,
                                   reads=[*c_g0[1:2], *c_g0[0:1]], writes=[b_mrow])
                            yield
                            act.op("activation", out=g0[:, 8:9], in_=mrow[:, 0:1], func=AF.Exp, scale=-1.0,
                                   reads=[b_mrow], writes=[*c_g0[8:9]])
                            yield
                            psb, b_pBc = next_tm()
                            pBc = psb[:, 0:8]
                            pe.op("matmul", out=pBc[:, 0:1], lhsT=cm[0:1, ONES, :], rhs=g0[:, 8:9], start=True, stop=True,
                                  reads=[b_cm, *c_g0[8:9]], writes=[b_pBc])
                            yield
                            dve.op("tensor_copy", out=gw[:, 10:11], in_=pBc[:, 0:1], reads=[b_pBc], writes=[*c_gw[10:11]])
                            yield
                            dve.op("tensor_scalar", out=cst[:], in0=Ch[:], scalar1=gw[:, 10:11], scalar2=None, op0=ALU.mult,
                                   reads=[b_Ch, *c_gw[10:11]], writes=[b_cst])
                            yield
                            sp.dma(s_cst, o_c[layer, 0].rearrange("(k p) v -> p k v", p=128), cst[:, :, 0:256], reads=[b_cst])
                            yield
                            sp.dma(s_cst, o_n[layer, 0].rearrange("(k p) -> p k", p=128), cst[:, :, 256], reads=[b_cst], allow_slow_non_contiguous=True)
                            yield
                    elif "noK64" not in cfg.dbg:
                        for s in range(2):
                            q = 2 * u + s
                            for kc in range(2):
                                ps, bps = next_tm()
                                pe.op("matmul", out=ps[:, 0:257], lhsT=ktok[b][64 * s:64 * s + 64, kc * 128:(kc + 1) * 128],
                                      rhs=vpx[64 * s:64 * s + 64, :], start=True, stop=True,
                                      reads=[b_ktok[b], b_vpx], writes=[bps])
                                yield
                                dve.op("tensor_tensor", out=cst[:, kc, :], in0=Cs[s][:, kc, :], in1=ps[:, 0:257], op=ALU.add,
                                       reads=[b_Cs[s], bps], writes=[b_cst])
                                yield
                            dve.op("tensor_scalar", out=cst[:], in0=cst[:], scalar1=gw[:, 10 + s:11 + s], scalar2=None,
                                   op0=ALU.mult, reads=[b_cst, *c_gw[10 + s:11 + s]], writes=[b_cst])
                            yield
                            sp.dma(s_cst, o_c[layer, 1 + q].rearrange("(k p) v -> p k v", p=128), cst[:, :, 0:256], reads=[b_cst])
                            yield
                            sp.dma(s_cst, o_n[layer, 1 + q].rearrange("(k p) -> p k", p=128), cst[:, :, 256], reads=[b_cst], allow_slow_non_contiguous=True)
                            yield


                    yield
                def gen_att():
                    for h in range(2):
                        act.op("activation", out=junk[:, 0:128], in_=aqs[b][:, h * 128:(h + 1) * 128], func=AF.Square,
                               accum_out=aw[:, h:h + 1], reads=[b_aqs[b]], writes=[b_junk, *c_aw[h:h + 1]])
                        yield
                        act.op("activation", out=junk[:, 0:128], in_=aks[b][:, h * 128:(h + 1) * 128], func=AF.Square,
                               accum_out=aw[:, 2 + h:3 + h], reads=[b_aks[b]], writes=[b_junk, *c_aw[2 + h:3 + h]])
                        yield
                    act.op("activation", out=aw[:, 4:6], in_=aw[:, 0:2], func=AF.Sqrt, scale=1.0, bias=128.0 * EPS,
                           reads=[*c_aw[0:2]], writes=[*c_aw[4:6]])
                    yield
                    act.op("activation", out=aw[:, 6:8], in_=aw[:, 2:4], func=AF.Sqrt, scale=1.0 / 128.0, bias=EPS,
                           reads=[*c_aw[2:4]], writes=[*c_aw[6:8]])
                    yield
                    dve.op("reciprocal", out=aw[:, 4:8], in_=aw[:, 4:8], reads=[*c_aw[4:8]], writes=[*c_aw[4:8]])
                    yield
                    for h in range(2):
                        hs = slice(h * 128, (h + 1) * 128)
                        dve.op("scalar_tensor_tensor", out=qhb[:, hs], in0=aqs[b][:, hs], scalar=aw[:, 4 + h:5 + h], in1=aqgt[:, hs],
                               op0=ALU.mult, op1=ALU.mult, reads=[b_aqs[b], b_aqgt, *c_aw[4 + h:5 + h]], writes=[b_qhb])
                        yield
                        dve.op("scalar_tensor_tensor", out=khs[b][:, hs], in0=aks[b][:, hs], scalar=aw[:, 6 + h:7 + h], in1=akgt[:, hs],
                               op0=ALU.mult, op1=ALU.mult, reads=[b_aks[b], b_akgt, *c_aw[6 + h:7 + h]], writes=[b_khs[b]])
                        yield
                    act.op("copy", out=khb[:], in_=khs[b][:], reads=[b_khs[b]], writes=[b_khb])
                    yield
                    transpose_to(qTa, b_qTa, qhb, b_qhb, 2)
                    yield
                    transpose_to(kring[:, slot, :, :], b_kring[slot], khb, b_khb, 2)
                    yield
                    for h in range(2):
                        if is_prompt:
                            kbs = list(range(max(0, 4 - p), 5))
                            nA = len([kb for kb in kbs if kb < 4])
                            pAv = pAS[:, 0:512].rearrange("p (a b) -> p a b", b=128)
                            for kb in kbs:
                                sl_k = (p - 4 + kb) % 6
                                dst = pAv[:, kb, :] if kb < 4 else pA5
                                bd = b_pAS if kb < 4 else b_pA5
                                pe.op("matmul", out=dst, lhsT=kring[:, sl_k, h, :], rhs=qTa[:, h, :], start=True, stop=False,
                                      reads=[b_kring[sl_k], b_qTa], writes=[bd], signal=False)
                                yield
                                pe.op("matmul", out=dst, lhsT=identb[:], rhs=bhi[:, h * 6 + kb, :], start=False, stop=False,
                                      reads=[b_identb, b_bhi], writes=[bd], signal=False)
                                yield
                                pe.op("matmul", out=dst, lhsT=identb[:], rhs=blo[:, h * 6 + kb, :], start=False, stop=True,
                                      reads=[b_identb, b_blo], writes=[bd], signal=True)
                                yield
                            k0 = kbs[0]
                            if k0 < 4:
                                act.op("activation", out=PTp[:, k0:4, :], in_=pAv[:, k0:4, :], func=AF.Exp,
                                       reads=[b_pAS], writes=[b_PTp])
                                yield
                            act.op("activation", out=PTp[:, 4, :], in_=pA5, func=AF.Exp, reads=[b_pA5], writes=[b_PTp])
                            yield
                            for i, kb in enumerate(kbs):
                                sl_k = (p - 4 + kb) % 6
                                pe.op("matmul", out=pPV[:, h, :], lhsT=PTp[:, kb, :], rhs=vring[:, sl_k, h, :],
                                      start=(i == 0), stop=(i == len(kbs) - 1), reads=[b_PTp, b_vring[sl_k]], writes=[b_pPV],
                                      signal=(i == len(kbs) - 1))
                                yield
                        elif "nosatt" in cfg.dbg:
                            pe.op("matmul", out=pPV[:, h, :], lhsT=PTp[:, 4, :], rhs=vring[:, slot, h, :], start=True, stop=True,
                                  reads=[b_PTp, b_vring[slot]], writes=[b_pPV])
                            yield
                        else:
                            pAv = pAS[:, 0:512].rearrange("p (a b) -> p a b", b=64)
                            for s in range(2):
                                for kb in range(4):
                                    dst = pAv[:, s * 4 + kb, :]
                                    qs = slice(64 * s, 64 * s + 64)
                                    pe.op("matmul", out=dst, lhsT=kTc[:, s, h, kb * 128:(kb + 1) * 128], rhs=qTa[:, h, qs],
                                          start=True, stop=False, reads=[b_kTc[s], b_qTa], writes=[b_pAS], signal=False)
                                    yield
                                    pe.op("matmul", out=dst, lhsT=identb[:], rhs=bhi[:, h * 6 + kb, 0:64], start=False, stop=False,
                                          reads=[b_identb, b_bhi], writes=[b_pAS], signal=False)
                                    yield
                                    pe.op("matmul", out=dst, lhsT=identb[:], rhs=blo[:, h * 6 + kb, 0:64], start=False, stop=True,
                                          reads=[b_identb, b_blo], writes=[b_pAS], signal=(s == 1 and kb == 3))
                                    yield
                            pe.op("matmul", out=pA5, lhsT=kring[:, slot, h, :], rhs=qTa[:, h, :], start=True, stop=False,
                                  reads=[b_kring[slot], b_qTa], writes=[b_pA5], signal=False)
                            yield
                            pe.op("matmul", out=pA5, lhsT=identb[:], rhs=bhi[:, h * 6 + 5, :], start=False, stop=False,
                                  reads=[b_identb, b_bhi], writes=[b_pA5], signal=False)
                            yield
                            pe.op("matmul", out=pA5, lhsT=identb[:], rhs=blo[:, h * 6 + 5, :], start=False, stop=True,
                                  reads=[b_identb, b_blo], writes=[b_pA5])
                            yield
                            for s in range(2):
                                act.op("activation", out=PTs[s][:, :, 64 * s:64 * s + 64], in_=pAv[:, s * 4:s * 4 + 4, :], func=AF.Exp,
                                       reads=[b_pAS], writes=[b_PTs[s]])
                                yield
                            act.op("activation", out=PTp[:, 4, :], in_=pA5, func=AF.Exp, reads=[b_pA5], writes=[b_PTp])
                            yield
                            for s in range(2):
                                for kb in range(4):
                                    pe.op("matmul", out=pPV[:, h, :], lhsT=PTs[s][:, kb, :], rhs=Vc[:, s, kb, h, :],
                                          start=(s == 0 and kb == 0), stop=False, reads=[b_PTs[s], b_Vc[s]], writes=[b_pPV],
                                          signal=False)
                                    yield
                            pe.op("matmul", out=pPV[:, h, :], lhsT=PTp[:, 4, :], rhs=vring[:, slot, h, :], start=False, stop=True,
                                  reads=[b_PTp, b_vring[slot]], writes=[b_pPV])
                            yield
                    dve.op("reciprocal", out=aw[:, 8:10], in_=pPV[:, :, 128], reads=[b_pPV], writes=[*c_aw[8:10]])
                    yield
                    for h in range(2):
                        hs = slice(h * 128, (h + 1) * 128)
                        dve.op("scalar_tensor_tensor", out=yat[:, hs], in0=pPV[:, h, 0:128], scalar=aw[:, 8 + h:9 + h],
                               in1=slaz[b][:, hs], op0=ALU.mult, op1=ALU.mult, reads=[b_pPV, b_slaz[b], *c_aw[8 + h:9 + h]], writes=[b_yat])
                        yield


                    yield
                gens = [gen_pool(), gen_ml(), gen_att()]
                while gens:
                    for g_ in list(gens):
                        try:
                            next(g_)
                        except StopIteration:
                            gens.remove(g_)

                transpose_to(yst[b][:, :, 1, :], b_yst[b], yml, b_yml, 2)
                transpose_to(yst[b][:, :, 2, :], b_yst[b], yat, b_yat, 2)
                ytok = sp.dma(s_yst[b], ysrc[T // 3].rearrange("(c p) t -> p c t", p=128)[:, :, (T % 3) * 128:(T % 3 + 1) * 128],
                              yst[b][:].rearrange("p h b t -> p (h b) t"), reads=[b_yst[b]])
                ytoks.append(ytok)
                if T % 3 == 2 and cfg.max_tiles is None:
                    pool.wait_tok(ytoks[-1])
                    pool.wait_tok(ytoks[-2])
                    issue_collective(ysrc[T // 3], ydst[T // 3])

                if out_pool and "nosout" not in cfg.dbg:
                    tokp = sp.dma(s_pscr[b], pscr[b], pus[b][:], reads=[b_pus[b]])
                    sp.wait_tok(tokp)
                    if is_prompt:
                        sp.dma(s_out[b], o_pool[layer, 0], pscr[b, 113:128, :])
                    else:
                        for s in range(2):
                            sp.dma(s_out[b], o_pool[layer, 1 + 2 * u + s], pscr[b, 64 * s + 49:64 * s + 64, :])
                    sp.wait_tok(s_out[b].last())
                if out_kv:
                    row0 = (p - (cfg.NPTILES - 4)) * 128 if is_prompt else 512 + u * 128
                    if "nook" not in cfg.dbg:
                        sp.dma(s_out[b], o_k[layer, row0:row0 + 128, :], khs[b][:], reads=[b_khs[b]])
                    if "noov" not in cfg.dbg:
                        sp.dma(s_out[b], o_v[layer, row0:row0 + 128, :], avs[b][:], reads=[b_avs[b]])
                if out_pool or out_kv:
                    for bb_ in (b_pus[b], b_khs[b], b_avs[b]):
                        if s_out[b].key in bb_.r:
                            bb_.r[s_out[b].key] = s_out[b].last()
            load_x(0)
            front(0)
            for T in range(ntiles_run):
                if T + 1 < ntiles_run:
                    front(T + 1)
                back(T)
            sp.dma(s_mrow, o_m[layer:layer + 1, :], mrow[:], reads=[b_mrow])

            barrier()
            if cfg.stop_after == f"A{layer}":
                break

            ar.release(m_persist)
            HT, HTOK, TG, NTG = cfg.HT, cfg.HTOK, cfg.TG, cfg.NTG
            hTB = ar.alloc("hTB", [128, 16, HTOK], BF16); b_hTB = B("hTB")
            mgT = ar.alloc("mgT", [128, 16, HTOK], BF16); b_mgT = B("mgT")
            yTb = [ar.alloc("yTb0", [128, NTG, 8, TG], BF16), ar.alloc("yTb1", [128, NTG, 8, TG], BF16)]
            b_yTb = [B(), B()]
            s_yTb = [DmaSem(cx, "sytb0"), DmaSem(cx, "sytb1")]
            NW = 2
            wslot = [ar.alloc(f"wsl{i}", [128, 16 * 512 + 2 * 512], BF16) for i in range(NW)]
            b_wslot = [B() for _ in range(NW)]
            s_wslot = [DmaSem(cx, f"swsl{i}") for i in range(NW)]
            wsl_i = {"i": 0}
            NWS = 6
            b_wsm = [B() for _ in range(NWS)]
            s_wsm = [DmaSem(cx, f"swsm{i}") for i in range(NWS)]
            wsm_i = {"i": 0}
            sgs = [ar.alloc("sgs0", [128, 512], F32), ar.alloc("sgs1", [128, 512], F32)]
            b_sgs = [B(), B()]
            tmpB = [ar.alloc("tmpB0", [128, 512], F32), ar.alloc("tmpB1", [128, 512], F32)]
            b_tmpB = [B(), B()]
            xsl = [ar.alloc(f"xsl{i}", [128, 512], F32) for i in range(3)]
            b_xsl = [B() for _ in range(3)]
            s_xsl = [DmaSem(cx, f"sxsl{i}") for i in range(3)]
            xos = [ar.alloc(f"xos{i}", [128, 512], F32) for i in range(3)]
            b_xos = [B() for _ in range(3)]
            s_xos = [DmaSem(cx, f"sxos{i}") for i in range(3)]
            pT = ar.alloc("pT", [128, 2, HTOK], BF16); b_pT = B("pT")
            pst = ar.alloc("pst", [128, 256], F32); b_pst = B("pst")
            s_pst = DmaSem(cx, "spst")
            pbf = ar.alloc("pbf", [128, 256], BF16); b_pbf = B("pbf")
            gpb = gnb; b_gpb = b_gnb
            s_gn = DmaSem(cx, "sgn")
            PA = [pbs[2], pbs[3]]; b_PA = [B(), B()]
            PB = [pbs[4], pbs[5]]; b_PB = [B(), B()]
            rank_state = {}

            for hh in range(cfg.NH):
                t0 = hh * HTOK
                sp.dma(s_gn, gnb[:], gn[layer].partition_broadcast(128), writes=[b_gnb])
                for i in range(HT):
                    b = i % 2
                    sp.dma(s_xs[b], xs[b][:], xsrc_own[t0 + i * 128:t0 + (i + 1) * 128, :], writes=[b_xs[b]])
                    norm_tile(xs[b], b_xs[b], gnb, b_gnb)
                    transpose_to(hTB[:, :, i * 128:(i + 1) * 128], b_hTB, hb, b_hb, 16)
                par = 0
                for br in range(3):
                    yb = br % 2
                    for tg in range(NTG):
                        def fn(e, br=br, tg=tg, hh=hh, dstt=yTb[yb], NTG=NTG, CPR=CPR):
                            if "nodyn" in cfg.dbg:
                                rank_state["c"] = 0
                            if "c" not in rank_state:
                                rank_state["c"] = e.partition_id() % 4
                            c = rank_state["c"]
                            y4 = ydst.rearrange("(c k) r t -> c k r t", k=CPR)
                            src = y4[bass.ds(c, 1), hh * NTG + tg, :, :].rearrange("o (rh b p) t -> p (o rh) b t", b=3, p=128)
                            return e.dma_start(out=dstt[:, tg, :, :], in_=src[:, :, br, :])
                        sp.dma(s_yTb[yb], None, None, writes=[b_yTb[yb]], fn=fn)
                    for eb in range(16):
                        wi = wsm_i["i"]
                        wsm_i["i"] = (wi + 1) % NWS
                        wbase = wslot[wi // 3][:, (wi % 3) * 3072:(wi % 3 + 1) * 3072]
                        wg = wbase[:, 0:2048].rearrange("p (k c) -> p k c", c=128)
                        wb = wbase[:, 2048:3072].rearrange("p (k c) -> p k c", c=128)
                        bw_ = b_wsm[wi]
                        pool.dma(s_wsm[wi], wg, w_gate[layer, br, eb].rearrange("(k p) c -> p k c", p=128), writes=[bw_, b_wslot[wi // 3]])
                        pool.dma(s_wsm[wi], wb, w_br[layer, br, eb].rearrange("(k p) c -> p k c", p=128), writes=[bw_, b_wslot[wi // 3]])
                        for tg in range(NTG):
                            ts = slice(tg * TG, (tg + 1) * TG)
                            par ^= 1
                            for k in range(16):
                                pe.op("matmul", out=PA[par][:, 0:TG], lhsT=wg[:, k, :], rhs=hTB[:, k, ts], start=(k == 0), stop=(k == 15),
                                      reads=[bw_, b_hTB], writes=[b_PA[par]], signal=(k == 15))
                            for k in range(8):
                                pe.op("matmul", out=PB[par][:, 0:TG], lhsT=wb[:, k, :], rhs=yTb[yb][:, tg, k, :], start=(k == 0), stop=(k == 7),
                                      reads=[bw_, b_yTb[yb]], writes=[b_PB[par]], signal=(k == 7))
                            act.op("activation", out=sgs[par][:, 0:TG], in_=PA[par][:, 0:TG], func=AF.Sigmoid,
                                   reads=[b_PA[par]], writes=[b_sgs[par]])
                            if br == 0:
                                dve.op("tensor_tensor", out=mgT[:, eb, ts], in0=sgs[par][:, 0:TG], in1=PB[par][:, 0:TG], op=ALU.mult,
                                       reads=[b_sgs[par], b_PB[par]], writes=[b_mgT])
                            else:
                                dve.op("tensor_tensor", out=tmpB[par][:, 0:TG], in0=sgs[par][:, 0:TG], in1=PB[par][:, 0:TG], op=ALU.mult,
                                       reads=[b_sgs[par], b_PB[par]], writes=[b_tmpB[par]])
                                dve.op("tensor_tensor", out=mgT[:, eb, ts], in0=mgT[:, eb, ts], in1=tmpB[par][:, 0:TG], op=ALU.add,
                                       reads=[b_tmpB[par], b_mgT], writes=[b_mgT])
                xi = 0
                for cb in range(4):
                    wi = wsl_i["i"]
                    wsl_i["i"] = (wi + 1) % NW
                    wo = wslot[wi][:, 0:8192].rearrange("p (k c) -> p k c", c=512)
                    for kq in range(2):
                        pool.dma(s_wslot[wi], wo[:, kq * 8:(kq + 1) * 8, :],
                                 w_out[layer, cb, kq * 1024:(kq + 1) * 1024, :].rearrange("(k p) c -> p k c", p=128),
                                 writes=[b_wslot[wi]] + b_wsm[wi * 3:wi * 3 + 3])
                    for i in range(HT):
                        rows = slice(t0 + i * 128, t0 + (i + 1) * 128)
                        cols = slice(cb * 512, (cb + 1) * 512)
                        xi = (xi + 1) % 3
                        par ^= 1
                        sp.dma(s_xsl[xi], xsl[xi][:], xsrc_own[rows, cols], writes=[b_xsl[xi]])
                        for k in range(16):
                            pe.op("matmul", out=PA[par][:], lhsT=mgT[:, k, i * 128:(i + 1) * 128], rhs=wo[:, k, :], start=(k == 0), stop=(k == 15),
                                  reads=[b_mgT, b_wslot[wi]], writes=[b_PA[par]], signal=(k == 15))
                        dve.op("tensor_tensor", out=xos[xi][:], in0=PA[par][:], in1=xsl[xi][:], op=ALU.add,
                               reads=[b_PA[par], b_xsl[xi]], writes=[b_xos[xi]])
                        sp.dma(s_xos[xi], xnew[rows, cols], xos[xi][:], reads=[b_xos[xi]])
                for i3 in range(3):
                    sp.wait_tok(s_xos[i3].last())
                sp.dma(s_gn, gnb[:], gp[layer].partition_broadcast(128), writes=[b_gnb])
                for i in range(HT):
                    b = i % 2
                    rows = slice(t0 + i * 128, t0 + (i + 1) * 128)
                    sp.dma(s_xs[b], xs[b][:], xnew[rows, :], writes=[b_xs[b]])
                    sp.dma(s_pst, pst[:], pown[layer, rows, :], writes=[b_pst])
                    norm_tile(xs[b], b_xs[b], gpb, b_gpb)
                    transpose_to(hTB[:, :, i * 128:(i + 1) * 128], b_hTB, hb, b_hb, 16)
                    dve.op("tensor_copy", out=pbf[:], in_=pst[:], reads=[b_pst], writes=[b_pbf])
                    transpose_to(pT[:, :, i * 128:(i + 1) * 128], b_pT, pbf, b_pbf, 2)
                for cb in range(4):
                    wi = wsl_i["i"]
                    wsl_i["i"] = (wi + 1) % NW
                    wo = wslot[wi][:, 0:8192].rearrange("p (k c) -> p k c", c=512)
                    wp = wslot[wi][:, 8192:9216].rearrange("p (k c) -> p k c", c=512)
                    for kq in range(2):
                        pool.dma(s_wslot[wi], wo[:, kq * 8:(kq + 1) * 8, :],
                                 w_pg[layer, cb, kq * 1024:(kq + 1) * 1024, :].rearrange("(k p) c -> p k c", p=128),
                                 writes=[b_wslot[wi]] + b_wsm[wi * 3:wi * 3 + 3])
                    pool.dma(s_wslot[wi], wp, w_pp[layer, cb].rearrange("(k p) c -> p k c", p=128), writes=[b_wslot[wi]])
                    for i in range(HT):
                        rows = slice(t0 + i * 128, t0 + (i + 1) * 128)
                        cols = slice(cb * 512, (cb + 1) * 512)
                        xi = (xi + 1) % 3
                        par ^= 1
                        sp.dma(s_xsl[xi], xsl[xi][:], xnew[rows, cols], writes=[b_xsl[xi]])
                        for k in range(16):
                            pe.op("matmul", out=PA[par][:], lhsT=hTB[:, k, i * 128:(i + 1) * 128], rhs=wo[:, k, :], start=(k == 0), stop=(k == 15),
                                  reads=[b_hTB, b_wslot[wi]], writes=[b_PA[par]], signal=(k == 15))
                        for k in range(2):
                            pe.op("matmul", out=PB[par][:], lhsT=pT[:, k, i * 128:(i + 1) * 128], rhs=wp[:, k, :], start=(k == 0), stop=(k == 1),
                                  reads=[b_pT, b_wslot[wi]], writes=[b_PB[par]], signal=(k == 1))
                        act.op("activation", out=sgs[par][:], in_=PA[par][:], func=AF.Sigmoid, reads=[b_PA[par]], writes=[b_sgs[par]])
                        dve.op("tensor_tensor", out=tmpB[par][:], in0=sgs[par][:], in1=PB[par][:], op=ALU.mult,
                               reads=[b_sgs[par], b_PB[par]], writes=[b_tmpB[par]])
                        dve.op("tensor_tensor", out=xos[xi][:], in0=tmpB[par][:], in1=xsl[xi][:], op=ALU.add,
                               reads=[b_tmpB[par], b_xsl[xi]], writes=[b_xos[xi]])
                        sp.dma(s_xos[xi], xdest[rows, cols], xos[xi][:], reads=[b_xos[xi]])
                for i3 in range(3):
                    sp.wait_tok(s_xos[i3].last())

            if layer == 0:
                barrier()
                xo3 = xown1.rearrange("(i p) d -> i p d", p=128)
                for i in range(TPR):
                    issue_collective(xo3[i], xg1[i])
            barrier()

        with nc.Block() as block:
            @block.tensor
            def _(e):
                for f in pe.prog:
                    f(e)

            @block.scalar
            def _(e):
                for f in act.prog:
                    f(e)

            @block.vector
            def _(e):
                for f in dve.prog:
                    f(e)

            @block.gpsimd
            def _(e):
                for f in pool.prog:
                    f(e)

            @block.sync
            def _(e):
                for f in sp.prog:
                    f(e)
        stats = {q.name: (q.n_ins, q.n_wait) for q in cx.qs}
        stats["nsem"] = cx.nsem
        stats["logs"] = {q.name: q.log for q in cx.qs}
        stats["sbuf_peak"] = ar.off
    return nc, stats


POOL_WINDOWS = (2, 4, 8, 16)
OFF = {}
_acc = 0
for _n, _s in (("pu", 1024), ("pz", 1024), ("mq", 1024), ("mk", 1024), ("mv", 1024), ("mo", 1024), ("mz", 1024),
               ("mi", 4), ("mf", 4), ("aq", 1024), ("ak", 1024), ("av", 1024), ("az", 1024), ("gts", 3 * D)):
    OFF[_n] = _acc
    _acc += _s


def _consts():
    cm = np.zeros((128, 7, 128), np.float32)
    s = np.arange(128)[:, None]
    t = np.arange(128)[None, :]
    same = (s // 64) == (t // 64)
    cm[:, 0, :] = np.eye(128)
    cm[:, 1, :] = (s <= t)
    cm[:, 2, :] = (s <= t) & same
    cm[:, 3, :] = same
    cm[:, 4, :] = (s < 64) & (t >= 0)
    cm[:, 5, :] = (s >= 64) & (t >= 0)
    cm[:, 6, :] = 1.0
    return cm


def _bias_index():
    idx = np.full((6, 128, 128), -1, np.int64)
    jj = np.arange(128)[:, None]
    ii = np.arange(128)[None, :]
    ci = ii // 64
    for kb in range(5):
        relk = 128 * (kb - 4) + jj
        ok = (relk >= 64 * ci - 512) & (relk <= 64 * ci + 63)
        dist = np.clip(ii - relk, -256, 256) + 256
        idx[kb] = np.where(ok, dist, -1)
    same = (jj // 64) == (ii // 64)
    dist = np.clip((ii % 64) - (jj % 64), -256, 256) + 256
    idx[5] = np.where(same, dist, -1)
    return idx


def prepare_inputs(cfg, inp):
    f = np.float32
    NPT, NSR, NS, TR = cfg.NPT, cfg.NSR, cfg.NS, cfg.TR
    w_in = inp["w_in"]
    cm = _consts()
    bidx = _bias_index()
    maps = []
    for core in range(cfg.NCORES):
        g, j = divmod(core, 4)

        def rank_rows(arr_p, arr_s, r):
            a = arr_p[r * NPT * 128:(r + 1) * NPT * 128]
            bq = arr_s[g * NS + r * NSR: g * NS + (r + 1) * NSR]
            return np.concatenate([a, bq.reshape((-1,) + bq.shape[2:])], 0)

        m = {}
        m["xg"] = np.ascontiguousarray(np.concatenate(
            [rank_rows(inp["x_prompt"][g], inp["x_sample"], r) for r in range(4)], 0), f)
        m["xown"] = np.ascontiguousarray(rank_rows(inp["x_prompt"][g], inp["x_sample"], j), f)
        m["pown"] = np.ascontiguousarray(np.stack(
            [rank_rows(inp["p_prompt"][l, g], inp["p_sample"][l], j) for l in range(DEPTH)]), f)
        c256 = lambda name, k=j: w_in[:, :, OFF[name] + k * 256: OFF[name] + (k + 1) * 256]
        mi = w_in[:, :, OFF["mi"] + j: OFF["mi"] + j + 1]
        mf = w_in[:, :, OFF["mf"] + j: OFF["mf"] + j + 1]
        m["w_tm"] = np.ascontiguousarray(np.concatenate(
            [c256("pu"), c256("mk"), c256("mv"), c256("mo"), c256("mz"), c256("aq"), c256("ak"), c256("av"), c256("az"),
             mi, mf], -1), f)
        m["w_fm"] = np.ascontiguousarray(np.concatenate([c256("pu"), c256("pz"), c256("mq")], -1), f)
        gts = w_in[:, :, OFF["gts"]:].reshape(DEPTH, D, 3, 16, 128)
        m["w_gate"] = np.ascontiguousarray(gts.transpose(0, 2, 3, 1, 4), f)
        m["w_br"] = np.ascontiguousarray(inp["w_branch"].reshape(DEPTH, 3, 1024, 16, 128).transpose(0, 1, 3, 2, 4), f)
        m["w_out"] = np.ascontiguousarray(inp["w_out"].reshape(DEPTH, D, 4, 512).transpose(0, 2, 1, 3), f)
        m["w_pg"] = np.ascontiguousarray(inp["w_ple_gate"].reshape(DEPTH, D, 4, 512).transpose(0, 2, 1, 3), f)
        m["w_pp"] = np.ascontiguousarray(inp["w_ple_proj"].reshape(DEPTH, 256, 4, 512).transpose(0, 2, 1, 3), f)
        m["gn"] = np.ascontiguousarray(inp["norm_mix"], f)
        m["gp"] = np.ascontiguousarray(inp["ple_norm"], f)
        m["wpool"] = np.ascontiguousarray(inp["w_pool_group"][:, j], f)
        m["pscale"] = np.ascontiguousarray(inp["pool_scale"][:, j * 256:(j + 1) * 256].reshape(DEPTH, 2, 128).transpose(0, 2, 1), f)
        sel = np.zeros((128, 4), f)
        sel[:, j] = 1.0
        m["psel"] = sel
        w = POOL_WINDOWS[j]
        rc = np.zeros((2, 128, 128), f)
        rc[0] = (np.float32(1.0) / np.minimum(np.arange(128) + 1, w).astype(f))[None, :]
        rc[1] = np.float32(1.0) / np.float32(w)
        m["prc"] = rc
        bg = np.zeros((DEPTH, 128, 2), f)
        bg[:, :, 0] = inp["b_ig"][:, j][:, None]
        bg[:, :, 1] = inp["b_fg"][:, j][:, None]
        m["bgate"] = bg
        m["mlg"] = np.ascontiguousarray(inp["ml_head_norm"][:, j * 256:(j + 1) * 256], f)
        m["aqg"] = np.ascontiguousarray(np.tile(inp["att_q_norm"], (1, 2)), f)
        m["akg"] = np.ascontiguousarray(np.tile(inp["att_k_norm"], (1, 2)), f)
        ab = np.zeros((DEPTH, 128, 12, 128), f)
        for l in range(DEPTH):
            for h in range(2):
                row = inp["att_rel_bias"][l, 2 * j + h]
                for kb in range(6):
                    v = np.where(bidx[kb] >= 0, row[np.maximum(bidx[kb], 0)], np.float32(NEG))
                    ab[l, :, h * 6 + kb, :] = v
        m["abias"] = ab
        m["cmask"] = cm
        sq = slice(g * NS, (g + 1) * NS)
        sp_ = inp["state_pool"][:, sq, :, j * 256:(j + 1) * 256]
        hist = np.zeros((DEPTH, 128, NS // 2, 2, 2, 16), f)
        hist[..., 1:] = sp_.reshape(DEPTH, NS // 2, 2, 15, 2, 128).transpose(0, 5, 1, 4, 2, 3)
        m["spool"] = hist
        c_ = inp["state_mlstm_c"][:, sq, j]
        m["sC"] = np.ascontiguousarray(c_.transpose(0, 1, 3, 2).reshape(DEPTH, NS, 2, 128, 256), f)
        n_ = inp["state_mlstm_n"][:, sq, j]
        m["sn"] = np.ascontiguousarray(n_.reshape(DEPTH, NS, 2, 128).transpose(0, 1, 3, 2), f)
        m_ = inp["state_mlstm_m"][:, sq, j]
        m["sm"] = np.ascontiguousarray(np.broadcast_to(m_[:, None, :], (DEPTH, 128, NS)), f)
        k_ = inp["cache_att_k"][:, sq, :, 2 * j:2 * j + 2, :]
        m["skT"] = np.ascontiguousarray(k_.transpose(0, 1, 3, 4, 2), f)
        m["sv"] = np.ascontiguousarray(inp["cache_att_v"][:, sq, :, 2 * j:2 * j + 2, :], f)
        maps.append(m)
    return maps


def assemble_outputs(cfg, res):
    f = np.float32
    NG, NPT, NSR, NS, TR = cfg.NG, cfg.NPT, cfg.NSR, cfg.NS, cfg.TR
    SEQ = cfg.SEQ
    NSAMP = NG * NS
    y_p = np.zeros((NG, SEQ, D), f)
    y_s = np.zeros((NSAMP, 64, D), f)
    pool_p = np.zeros((DEPTH, NG, 15, 1024), f)
    pool_s = np.zeros((DEPTH, NSAMP, 15, 1024), f)
    c_p = np.zeros((DEPTH, NG, 4, 256, 256), f)
    c_s = np.zeros((DEPTH, NSAMP, 4, 256, 256), f)
    n_p = np.zeros((DEPTH, NG, 4, 256), f)
    n_s = np.zeros((DEPTH, NSAMP, 4, 256), f)
    m_p = np.zeros((DEPTH, NG, 4), f)
    m_s = np.zeros((DEPTH, NSAMP, 4), f)
    kw = min(512, SEQ)
    k_p = np.zeros((DEPTH, NG, kw, 8, 128), f)
    v_p = np.zeros((DEPTH, NG, kw, 8, 128), f)
    k_s = np.zeros((DEPTH, NSAMP, 64, 8, 128), f)
    v_s = np.zeros((DEPTH, NSAMP, 64, 8, 128), f)
    for core in range(cfg.NCORES):
        g, j = divmod(core, 4)
        r = res[core]
        yo = np.asarray(r["y_own"])
        y_p[g, j * NPT * 128:(j + 1) * NPT * 128] = yo[:NPT * 128]
        y_s[g * NS + j * NSR: g * NS + (j + 1) * NSR] = yo[NPT * 128:].reshape(NSR, 64, D)
        op = np.asarray(r["o_pool"])
        pool_p[:, g, :, j * 256:(j + 1) * 256] = op[:, 0]
        pool_s[:, g * NS:(g + 1) * NS, :, j * 256:(j + 1) * 256] = op[:, 1:]
        oc = np.asarray(r["o_c"]).transpose(0, 1, 3, 2)
        c_p[:, g, j] = oc[:, 0]
        c_s[:, g * NS:(g + 1) * NS, j] = oc[:, 1:]
        on = np.asarray(r["o_n"])
        n_p[:, g, j] = on[:, 0]
        n_s[:, g * NS:(g + 1) * NS, j] = on[:, 1:]
        om = np.asarray(r["o_m"])
        m_p[:, g, j] = om[:, 0]
        m_s[:, g * NS:(g + 1) * NS, j] = om[:, 1:]
        ok = np.asarray(r["o_k"]).reshape(DEPTH, cfg.NKV, 2, 128)
        ov = np.asarray(r["o_v"]).reshape(DEPTH, cfg.NKV, 2, 128)
        k_p[:, g, :, 2 * j:2 * j + 2] = ok[:, :512][:, 512 - kw:]
        v_p[:, g, :, 2 * j:2 * j + 2] = ov[:, :512][:, 512 - kw:]
        k_s[:, g * NS:(g + 1) * NS, :, 2 * j:2 * j + 2] = ok[:, 512:].reshape(DEPTH, NS, 64, 2, 128)
        v_s[:, g * NS:(g + 1) * NS, :, 2 * j:2 * j + 2] = ov[:, 512:].reshape(DEPTH, NS, 64, 2, 128)
    return (y_p, y_s, pool_p, pool_s, c_p, c_s, n_p, n_s, m_p, m_s, k_p, k_s, v_p, v_s)


_CACHE = {}


def run_cfg(cfg, inputs, trace=False):
    key = (cfg.NG, cfg.NPT, cfg.NSR, cfg.NH)
    if key not in _CACHE:
        _CACHE[key] = build_program(cfg)
    nc, stats = _CACHE[key]
    maps = prepare_inputs(cfg, inputs)
    res = run_bass_kernel_spmd(nc, maps, core_ids=list(range(cfg.NCORES)))
    return assemble_outputs(cfg, res.results)


def kernel(**inputs):
    inputs = {k: np.asarray(v) for k, v in inputs.items()}
    cfg = Cfg(NG=2, NPT=16, NSR=4, NH=2)
    return run_cfg(cfg, inputs)
```

```python
import contextlib
import numpy as np
import concourse.bass as bass
import concourse.mybir as mybir
from concourse.bass_utils import run_bass_kernel_spmd

F32 = mybir.dt.float32
BF16 = mybir.dt.bfloat16
ALU = mybir.AluOpType
AF = mybir.ActivationFunctionType
AX = mybir.AxisListType

D = 2048
DEPTH = 2
EPS = 1e-6
NEG = -30000.0
EPOCH = 30000
NTM = 2306
NFM = 768


class Buf:
    __slots__ = ("name", "w", "r")

    def __init__(self, name=""):
        self.name = name
        self.w = None
        self.r = {}

    def reset(self):
        self.w = None
        self.r = {}


class Ctx:
    def __init__(self, nc, stack):
        self.nc = nc
        self.stack = stack
        self.nsem = 0
        self.dsems = []
        self.qs = []

    def new_sem(self, name):
        self.nsem += 1
        return self.stack.enter_context(self.nc.semaphore(f"{name}_{self.nsem}"))


class DmaSem:
    def __init__(self, ctx, name):
        self.ctx = ctx
        self.name = name
        self.sem = ctx.new_sem(name)
        self.val = 0
        self.key = (id(self), 0)
        self.ep = 0
        ctx.dsems.append(self)

    def bump(self):
        if self.val + 16 > EPOCH:
            self.sem = self.ctx.new_sem(self.name)
            self.val = 0
            self.ep += 1
            self.key = (id(self), self.ep)
        self.val += 16
        return (self.key, self.sem, self.val)

    def last(self):
        return (self.key, self.sem, self.val) if self.val else None


class Q:
    def __init__(self, ctx, name):
        self.ctx = ctx
        self.name = name
        self.sem = ctx.new_sem(name)
        self.key = (name, 0)
        self.epoch = 0
        self.count = 0
        self.pending = False
        self.waited = {}
        self.prog = []
        self.log = []
        self.n_ins = 0
        self.n_wait = 0
        ctx.qs.append(self)

    def _need(self, reads, writes):
        need = {}

        def add(tok):
            k, s, v = tok
            if self.waited.get(k, 0) >= v:
                return
            if k not in need or need[k][2] < v:
                need[k] = tok

        for b in reads:
            if b.w is not None:
                add(b.w)
        for b in writes:
            if b.w is not None and (b.w[0] != self.key or self.name != "pe"):
                add(b.w)
            for k, tok in b.r.items():
                if k != self.key:
                    add(tok)
        return list(need.values())

    def _emit_waits(self, need):
        for (k, s, v) in need:
            self.prog.append(lambda e, s=s, v=v: e.wait_ge(s, v))
            self.waited[k] = v
            self.n_wait += 1
            self.log.append(f"  wait {k} >= {v}")

    def _mark(self, tok, reads, writes):
        for b in reads:
            old = b.r.get(tok[0])
            if old is None or old[2] < tok[2]:
                b.r[tok[0]] = tok
        for b in writes:
            b.w = tok
            b.r = {}

    def op(self, meth, reads=(), writes=(), signal=True, **kw):
        self._emit_waits(self._need(reads, writes))
        if self.count + 1 > EPOCH and not self.pending:
            self.epoch += 1
            self.sem = self.ctx.new_sem(self.name)
            self.key = (self.name, self.epoch)
            self.count = 0
        self.n_ins += 1
        self.log.append(f"{meth} W={[b.name for b in writes]} R={[b.name for b in reads]} sig={signal} cnt={self.count + (1 if signal else 0)}")
        if signal:
            self.prog.append(lambda e, meth=meth, kw=kw, sem=self.sem: getattr(e, meth)(**kw).then_inc(sem, 1))
            self.count += 1
            self.pending = False
            tok = (self.key, self.sem, self.count)
        else:
            self.prog.append(lambda e, meth=meth, kw=kw: getattr(e, meth)(**kw))
            self.pending = True
            tok = (self.key, self.sem, self.count + 1)
        self._mark(tok, reads, writes)
        return tok

    def dma(self, dsem, out, in_, reads=(), writes=(), fn=None, **kw):
        need = [t for t in self._need(reads, writes) if t[0] != dsem.key]
        self._emit_waits(need)
        tok = dsem.bump()
        self.log.append(f"DMA {dsem.name} -> {tok[2]} W={[b.name for b in writes]} R={[b.name for b in reads]}")
        if fn is None:
            self.prog.append(lambda e, out=out, in_=in_, kw=kw, sem=tok[1]:
                             e.dma_start(out=out, in_=in_, **kw).then_inc(sem, 16))
        else:
            self.prog.append(lambda e, fn=fn, sem=tok[1]: fn(e).then_inc(sem, 16))
        self.n_ins += 1
        self._mark(tok, reads, writes)
        return tok

    def wait_tok(self, tok):
        if tok is None:
            return
        if self.waited.get(tok[0], 0) < tok[2]:
            self.prog.append(lambda e, s=tok[1], v=tok[2]: e.wait_ge(s, v))
            self.waited[tok[0]] = tok[2]
            self.n_wait += 1

    def last(self):
        assert not self.pending
        return (self.key, self.sem, self.count) if self.count else None


class Arena:
    def __init__(self, nc, nbytes):
        self.nc = nc
        h = nc.alloc_sbuf_tensor("arena", [128, nbytes // 2], BF16)
        self.base = nc.lookup_mloc(h).addr
        self.size = nbytes
        self.off = 0
        self.n = 0

    def alloc(self, name, shape, dt):
        nb = int(np.prod(shape[1:])) * (4 if dt == F32 else 2)
        nb = (nb + 31) // 32 * 32
        assert self.off + nb <= self.size, f"SBUF arena overflow at {name}: {self.off}+{nb} > {self.size}"
        self.n += 1
        t = self.nc.alloc_sbuf_tensor_at(f"{name}_{self.n}", list(shape), dt, offset=self.base + self.off)
        self.off += nb
        return t

    def mark(self):
        return self.off

    def release(self, m):
        self.off = m


class Cfg:
    def __init__(self, NG=2, NPT=16, NSR=4, NH=2):
        self.NG = NG
        self.NPT = NPT
        self.NSR = NSR
        self.NST = NSR // 2
        self.TPR = NPT + self.NST
        self.TR = 128 * self.TPR
        self.GT = 4 * self.TR
        self.NS = 4 * NSR
        self.NPTILES = 4 * NPT
        self.NH = NH
        assert self.TPR % NH == 0
        self.HT = self.TPR // NH
        self.HTOK = self.HT * 128
        for tg in (512, 384, 256, 128):
            if self.HTOK % tg == 0:
                self.TG = tg
                break
        self.NTG = self.HTOK // self.TG
        self.SEQ = self.NPTILES * 128
        self.NCORES = 4 * NG
        self.NKV = 512 + self.NS * 64
        self.stop_after = None
        self.no_cc = False
        self.max_tiles = None
        self.dbg = set()


def build_program(cfg):
    nc = bass.Bass("TRN2", target_bir_lowering=False)
    TR, GT, NS, TPR = cfg.TR, cfg.GT, cfg.NS, cfg.TPR

    def din(name, shape):
        return nc.dram_tensor(name, list(shape), F32, kind="ExternalInput").ap()

    def dout(name, shape):
        return nc.dram_tensor(name, list(shape), F32, kind="ExternalOutput").ap()

    xg = din("xg", [GT, D])
    xown = din("xown", [TR, D])
    pown = din("pown", [DEPTH, TR, 256])
    w_tm = din("w_tm", [DEPTH, D, NTM])
    w_fm = din("w_fm", [DEPTH, D, NFM])
    w_gate = din("w_gate", [DEPTH, 3, 16, D, 128])
    w_br = din("w_br", [DEPTH, 3, 16, 1024, 128])
    w_out = din("w_out", [DEPTH, 4, D, 512])
    w_pg = din("w_pg", [DEPTH, 4, D, 512])
    w_pp = din("w_pp", [DEPTH, 4, 256, 512])
    gn = din("gn", [DEPTH, D])
    gp = din("gp", [DEPTH, D])
    wpool = din("wpool", [DEPTH, 256, 256])
    pscale = din("pscale", [DEPTH, 128, 2])
    psel = din("psel", [128, 4])
    prc = din("prc", [2, 128, 128])
    bgate = din("bgate", [DEPTH, 128, 2])
    mlg = din("mlg", [DEPTH, 256])
    aqg = din("aqg", [DEPTH, 256])
    akg = din("akg", [DEPTH, 256])
    abias = din("abias", [DEPTH, 128, 12, 128])
    cmask = din("cmask", [128, 7, 128])
    spool = din("spool", [DEPTH, 128, NS // 2, 2, 2, 16])
    sC = din("sC", [DEPTH, NS, 2, 128, 256])
    sn = din("sn", [DEPTH, NS, 128, 2])
    sm = din("sm", [DEPTH, 128, NS])
    skT = din("skT", [DEPTH, NS, 2, 128, 512])
    sv = din("sv", [DEPTH, NS, 512, 2, 128])

    y_own = dout("y_own", [TR, D])
    o_pool = dout("o_pool", [DEPTH, 1 + NS, 15, 256])
    o_c = dout("o_c", [DEPTH, 1 + NS, 256, 256])
    o_n = dout("o_n", [DEPTH, 1 + NS, 256])
    o_m = dout("o_m", [DEPTH, 1 + NS])
    o_k = dout("o_k", [DEPTH, cfg.NKV, 256])
    o_v = dout("o_v", [DEPTH, cfg.NKV, 256])

    assert TPR % 3 == 0 and cfg.TG == 384
    NCH = GT // 384
    CPR = TPR // 3
    ysrc = nc.dram_tensor("ysrc", [NCH, 768, 384], BF16).ap()
    ydst = nc.dram_tensor("ydst", [NCH, 4 * 768, 384], BF16).ap()
    xnew = nc.dram_tensor("xnew", [TR, D], F32).ap()
    pscr = nc.dram_tensor("pscr", [2, 128, 256], F32).ap()
    xown1 = nc.dram_tensor("xown1", [TR, D], F32).ap()
    xg1 = nc.dram_tensor("xg1", [TPR, 4 * 128, D], F32).ap()
    rgroups = [[4 * g + i for i in range(4)] for g in range(cfg.NG)]

    with contextlib.ExitStack() as st:
        cx = Ctx(nc, st)
        pe, act, dve, pool, sp = Q(cx, "pe"), Q(cx, "act"), Q(cx, "dve"), Q(cx, "pool"), Q(cx, "sp")
        ccsem = cx.new_sem("cc")
        ccstate = {"n": 0}
        ar = Arena(nc, 212800)
        allbufs = []

        def B(name=""):
            b = Buf(name)
            allbufs.append(b)
            return b

        pb0 = nc.alloc_psum_tensor("pb0", [128, 1024], BF16)
        pb1 = nc.alloc_psum_tensor("pb1", [128, 1024], BF16)
        pbs = [None, None] + [nc.alloc_psum_tensor(f"pb{i}", [128, 512], F32) for i in range(2, 8)]
        b_tr = [B("tr0"), B("tr1")]
        trh = [pb0[:, 0:512].rearrange("p (a b) -> p a b", b=128),
               pb1[:, 0:512].rearrange("p (a b) -> p a b", b=128)]
        trstate = {"i": 0}

        def next_tr():
            i = trstate["i"]
            trstate["i"] = 1 - i
            return trh[i], b_tr[i]

        cm = ar.alloc("cm", [128, 7, 128], F32); b_cm = B("cm")
        identb = ar.alloc("identb", [128, 128], BF16); b_identb = B("identb")
        gnb = ar.alloc("gnb", [128, D], F32); b_gnb = B("gnb")
        xs = [ar.alloc("xs0", [128, D], F32), ar.alloc("xs1", [128, D], F32)]
        b_xs = [B("xs0"), B("xs1")]
        s_xs = [DmaSem(cx, "sxs0"), DmaSem(cx, "sxs1")]
        hb = ar.alloc("hb", [128, D], BF16); b_hb = B("hb")
        junk = hb; b_junk = b_hb
        nst = ar.alloc("nst", [128, 8], F32); b_nst = B("nst")
        s_c = DmaSem(cx, "sconst")
        IDENT, TRIL, TRILBD, ONESBD, SELLO, SELHI, ONES = range(7)

        sp.dma(s_c, cm[:], cmask, writes=[b_cm])
        dve.op("tensor_copy", out=identb[:], in_=cm[:, IDENT, :], reads=[b_cm], writes=[b_identb])

        bsem = cx.new_sem("bar")
        bstate = {"n": 0}

        def issue_collective(src, dst):
            ccstate["n"] += 1
            if not cfg.no_cc:
                pool.prog.append(lambda e, src=src, dst=dst: e.collective_compute(
                    "AllGather", ALU.bypass, replica_groups=rgroups, ins=[src], outs=[dst]).then_inc(ccsem, 1))
            else:
                pool.prog.append(lambda e: e.sem_inc(ccsem, 1))

        def barrier():
            for q in cx.qs:
                if q is not pool:
                    pool.wait_tok(q.last())
            for ds in cx.dsems:
                pool.wait_tok(ds.last())
            if ccstate["n"]:
                pool.wait_tok((("cc", 0), ccsem, ccstate["n"]))
            bstate["n"] += 1
            pool.prog.append(lambda e: e.sem_inc(bsem, 1))
            tok = (("bar", 0), bsem, bstate["n"])
            for q in cx.qs:
                q.wait_tok(tok)
            for b in allbufs:
                b.reset()

        def norm_tile(xt, b_xt, gtile, b_g):
            act.op("activation", out=junk[:], in_=xt[:], func=AF.Square, accum_out=nst[:, 0:1],
                   reads=[b_xt], writes=[b_junk, b_nst])
            act.op("activation", out=nst[:, 1:2], in_=nst[:, 0:1], func=AF.Sqrt, scale=1.0 / D, bias=EPS,
                   reads=[b_nst], writes=[b_nst])
            dve.op("reciprocal", out=nst[:, 2:3], in_=nst[:, 1:2], reads=[b_nst], writes=[b_nst])
            dve.op("scalar_tensor_tensor", out=hb[:], in0=xt[:], scalar=nst[:, 2:3], in1=gtile[:],
                   op0=ALU.mult, op1=ALU.mult, reads=[b_xt, b_nst, b_g], writes=[b_hb])

        evac_flip = {"i": 0}

        def evac_copy(out, in_, reads, writes):
            evac_flip["i"] ^= 1
            if evac_flip["i"]:
                act.op("copy", out=out, in_=in_, reads=reads, writes=writes)
            else:
                dve.op("tensor_copy", out=out, in_=in_, reads=reads, writes=writes)

        def transpose_to(dst3, b_dst, src, b_src, nblk):
            i = 0
            while i < nblk:
                n = min(4, nblk - i)
                tr, btr = next_tr()
                for k in range(n):
                    pe.op("transpose", out=tr[:, k, :], in_=src[:, (i + k) * 128:(i + k + 1) * 128],
                          identity=identb[:], reads=[b_src, b_identb], writes=[btr], signal=(k == n - 1))
                evac_copy(dst3[:, i:i + n, :], tr[:, 0:n, :], [btr], [b_dst])
                i += n

        m_persist = ar.mark()

        for layer in range(DEPTH):
            xin = xg if layer == 0 else xg1
            xsrc_own = xown if layer == 0 else xown1
            xdest = xown1 if layer == 0 else y_own

            ar.release(m_persist)
            Wtm = ar.alloc("Wtm", [128, 16, NTM], BF16); b_Wtm = B("Wtm")
            Wfm = ar.alloc("Wfm", [128, 16, NFM], BF16); b_Wfm = B("Wfm")
            wpl = ar.alloc("wpl", [128, 2, 256], BF16); b_wpl = B("wpl")
            s_w = DmaSem(cx, "sw"); s_w2 = DmaSem(cx, "sw2"); s_w3 = DmaSem(cx, "sw3")
            for kq in range(4):
                pool.dma(s_w, Wtm[:, kq * 4:(kq + 1) * 4, :],
                         w_tm[layer, kq * 512:(kq + 1) * 512, :].rearrange("(k p) c -> p k c", p=128), writes=[b_Wtm])
            for kq in range(2):
                pool.dma(s_w2, Wfm[:, kq * 8:(kq + 1) * 8, :],
                         w_fm[layer, kq * 1024:(kq + 1) * 1024, :].rearrange("(k p) c -> p k c", p=128), writes=[b_Wfm])
            pool.dma(s_w3, wpl[:], wpool[layer].rearrange("(k p) c -> p k c", p=128), writes=[b_wpl])
            s_p = DmaSem(cx, "sparam")
            sp.dma(s_p, gnb[:], gn[layer].partition_broadcast(128), writes=[b_gnb])
            psc = ar.alloc("psc", [128, 2], F32); b_psc = B()
            pse = ar.alloc("pse", [128, 4], F32); b_pse = B()
            prc_t = ar.alloc("prc", [128, 2, 128], F32); b_prc = B()
            bgt = ar.alloc("bgt", [128, 2], F32); b_bgt = B()
            mlgt = ar.alloc("mlgt", [128, 256], F32); b_mlgt = B()
            aqgt = ar.alloc("aqgt", [128, 256], F32); b_aqgt = B()
            akgt = ar.alloc("akgt", [128, 256], F32); b_akgt = B()
            smt = ar.alloc("smt", [128, NS], F32); b_smt = B()
            sp.dma(s_p, psc[:], pscale[layer], writes=[b_psc])
            sp.dma(s_p, pse[:], psel, writes=[b_pse])
            sp.dma(s_p, prc_t[:], prc.rearrange("k p t -> p k t"), writes=[b_prc])
            sp.dma(s_p, bgt[:], bgate[layer], writes=[b_bgt])
            sp.dma(s_p, mlgt[:], mlg[layer].partition_broadcast(128), writes=[b_mlgt])
            sp.dma(s_p, aqgt[:], aqg[layer].partition_broadcast(128), writes=[b_aqgt])
            sp.dma(s_p, akgt[:], akg[layer].partition_broadcast(128), writes=[b_akgt])
            sp.dma(s_p, smt[:], sm[layer], writes=[b_smt])
            for bb_ in (b_gnb, b_psc, b_pse, b_prc, b_bgt, b_mlgt, b_aqgt, b_akgt, b_smt):
                bb_.w = s_p.last()
            bhi = ar.alloc("bhi", [128, 12, 128], BF16); b_bhi = B()
            blo = ar.alloc("blo", [128, 12, 128], BF16); b_blo = B()
            stg = xs[1][:, 0:1536].rearrange("p (a b) -> p a b", b=128)
            sp.dma(s_xs[1], stg, abias[layer], writes=[b_xs[1]])
            dve.op("tensor_copy", out=bhi[:], in_=stg, reads=[b_xs[1]], writes=[b_bhi])
            dve.op("tensor_tensor", out=blo[:], in0=stg, in1=bhi[:], op=ALU.subtract,
                   reads=[b_xs[1], b_bhi], writes=[b_blo])

            hT0 = ar.alloc("hT0", [128, 16, 128], BF16)
            hT = [hT0, hT0]
            b_hT0 = B("hT")
            b_hT = [b_hT0, b_hT0]
            def dbl(name, shape, dt):
                return [ar.alloc(name + "0", shape, dt), ar.alloc(name + "1", shape, dt)], [B(name + "0"), B(name + "1")]
            gat, b_gat = dbl("gat", [128, 2], F32)
            pus, b_pus = dbl("pus", [128, 256], F32)
            puT, b_puT = dbl("puT", [128, 2, 128], F32)
            ktok, b_ktok = dbl("ktok", [128, 256], BF16)
            vsb, b_vsb = dbl("vsb", [128, 256], BF16)
            sgmo, b_sgmo = dbl("sgmo", [128, 256], BF16)
            slmz, b_slmz = dbl("slmz", [128, 256], BF16)
            aqs, b_aqs = dbl("aqs", [128, 256], F32)
            aks, b_aks = dbl("aks", [128, 256], F32)
            avs, b_avs = dbl("avs", [128, 256], F32)
            slaz, b_slaz = dbl("slaz", [128, 256], BF16)
            slpz, b_slpz = dbl("slpz", [128, 2, 128], BF16)
            qTm, b_qTm = dbl("qTm", [128, 2, 128], BF16)
            yst, b_yst = dbl("yst", [128, 2, 3, 128], BF16)
            s_yst = [DmaSem(cx, "syst0"), DmaSem(cx, "syst1")]
            khs, b_khs = dbl("khs", [128, 256], F32)
            s_out = [DmaSem(cx, "sout0"), DmaSem(cx, "sout1")]
            s_pscr = [DmaSem(cx, "spscr0"), DmaSem(cx, "spscr1")]
            qTz = [ar.alloc("qTz0", [128, 2, 128], BF16), ar.alloc("qTz1", [128, 2, 128], BF16)]
            b_qTz = [B(), B()]
            ext = ar.alloc("ext", [128, 2, 144], F32); b_ext = B("ext")
            exs = ar.alloc("exs", [128, 2, 2, 80], F32); b_exs = B("exs")
            s_exs = DmaSem(cx, "sexs")
            psA = ar.alloc("psA", [128, 2, 160], F32); psB = ar.alloc("psB", [128, 2, 160], F32)
            pacc = ar.alloc("pacc", [128, 2, 128], F32); b_pw = B("poolwork")
            hg = pacc[:].rearrange("p h t -> p (h t)"); b_hg = b_pw
            mT = ar.alloc("mT", [128, 2, 128], BF16); b_mT = B("mT")
            gw = ar.alloc("gw", [128, 16], F32); b_gw = B("gw")
            c_gw = [B(f"gw{i}") for i in range(16)]
            c_aw = [B(f"aw{i}") for i in range(16)]
            c_g0 = [B(f"g0{i}") for i in range(16)]
            g0 = ar.alloc("g0", [1, 16], F32); b_g0 = B("g0")
            edb = ar.alloc("edb", [128, 1], BF16)
            vpx = ar.alloc("vpx", [128, 257], BF16); b_vpx = B("vpx")
            kTm = ar.alloc("kTm", [128, 2, 128], BF16); b_kTm = B("kTm")
            PTm = ar.alloc("PTm", [128, 128], BF16); b_PTm = B("PTm")
            Ch = ar.alloc("Ch", [128, 2, 257], F32); b_Ch = B("Ch")
            Chb = ar.alloc("Chb", [128, 2, 257], BF16); b_Chb = B("Chb")
            Cs = [ar.alloc("Cs0", [128, 2, 257], F32), ar.alloc("Cs1", [128, 2, 257], F32)]
            b_Cs = [B(), B()]
            s_Cs = [DmaSem(cx, "scs0"), DmaSem(cx, "scs1")]
            Csb = [ar.alloc("Csb0", [128, 2, 257], BF16), ar.alloc("Csb1", [128, 2, 257], BF16)]
            b_Csb = [B(), B()]
            cst = ar.alloc("cst", [128, 2, 257], F32); b_cst = B("cst")
            s_cst = DmaSem(cx, "scst")
            yml = ar.alloc("yml", [128, 256], BF16); b_yml = B("yml")
            yat = ar.alloc("yat", [128, 256], BF16); b_yat = B("yat")
            mrow = ar.alloc("mrow", [1, 1 + NS], F32); b_mrow = B("mrow")
            s_mrow = DmaSem(cx, "smrow")
            qhb = ar.alloc("qhb", [128, 256], BF16); b_qhb = B("qhb")
            khb = ar.alloc("khb", [128, 256], BF16); b_khb = B("khb")
            qTa = ar.alloc("qTa", [128, 2, 128], BF16); b_qTa = B("qTa")
            kring = ar.alloc("kring", [128, 8, 2, 128], BF16); b_kring = [B() for _ in range(8)]
            vring = ar.alloc("vring", [128, 8, 2, 129], BF16); b_vring = [B() for _ in range(8)]
            kTc = ar.alloc("kTc", [128, 2, 2, 512], BF16); b_kTc = [B(), B()]
            Vc = ar.alloc("Vc", [128, 2, 4, 2, 129], BF16); b_Vc = [B(), B()]
            s_kc = [DmaSem(cx, "skc0"), DmaSem(cx, "skc1")]
            PTp = ar.alloc("PTp", [128, 5, 128], BF16); b_PTp = B("PTp")
            PTs = [ar.alloc("PTs0", [128, 4, 128], BF16), ar.alloc("PTs1", [128, 4, 128], BF16)]
            b_PTs = [B(), B()]
            aw = ar.alloc("aw", [128, 16], F32); b_aw = B("aw")
            junk2 = ar.alloc("junk2", [128, 256], BF16); b_junk2 = B("junk2")

            ptm = [pbs[2], pbs[3], pbs[4]]; b_ptm = [B("ptm0"), B("ptm1"), B("ptm2")]
            ptm_i = {"i": 0}

            def next_fr():
                i = ptm_i["i"]
                ptm_i["i"] = 1 - i
                return ptm[i], b_ptm[i]

            def next_tm():
                return ptm[2], b_ptm[2]
            ml_done = [False]
            b_bank5 = B("bank5")
            pN = pbs[5][:, 0:257]; b_pN = b_bank5
            pS = pbs[5][:, 257:385]; b_pS = b_bank5
            pG = pbs[5][:, 385:401]; b_pG = b_bank5
            pAS = pbs[6]; b_pAS = B("pAS")
            b_bank7 = B("bank7")
            pPV = pbs[7][:, 0:258].rearrange("p (h c) -> p h c", c=129); b_pPV = b_bank7
            pA5 = pbs[7][:, 258:386]; b_pA5 = b_bank7

            dve.op("memset", ap=ext[:], constant=0.0, writes=[b_ext])
            dve.op("memset", ap=Ch[:], constant=0.0, writes=[b_Ch])
            dve.op("memset", ap=Chb[:], constant=0.0, writes=[b_Chb])
            dve.op("memset", ap=g0[:], constant=0.0, writes=[b_g0] + c_g0)
            dve.op("memset", ap=vring[:], constant=1.0, writes=b_vring)
            dve.op("memset", ap=Vc[:], constant=1.0, writes=b_Vc)
            for s in range(2):
                dve.op("memset", ap=qTz[s][:], constant=0.0, writes=[b_qTz[s]])
                dve.op("memset", ap=PTs[s][:], constant=0.0, writes=[b_PTs[s]])

            tiles = []
            for r in range(4):
                for l in range(TPR):
                    tiles.append((r, l))

            def load_x(T):
                b = T % 2
                r_, l_ = tiles[T]
                src_ = xg[T * 128:(T + 1) * 128, :] if layer == 0 else xg1[l_, r_ * 128:(r_ + 1) * 128, :]
                sp.dma(s_xs[b], xs[b][:], src_, writes=[b_xs[b]])

            def load_sample_state(u):
                for s in range(2):
                    q = 2 * u + s
                    pool.dma(s_kc[s], kTc[:, s, :, :], skT[layer, q].rearrange("h d k -> d h k"),
                             writes=[b_kTc[s]])
                    for h in range(2):
                        pool.dma(s_kc[s], Vc[:, s, :, h, 0:128],
                                 sv[layer, q, :, h, :].rearrange("(kb p) d -> p kb d", p=128), writes=[b_Vc[s]])
                    b_kTc[s].w = s_kc[s].last(); b_Vc[s].w = s_kc[s].last()
                    sp.dma(s_Cs[s], Cs[s][:, :, 0:256], sC[layer, q].rearrange("k p v -> p k v"), writes=[b_Cs[s]])
                    sp.dma(s_Cs[s], Cs[s][:, :, 256], sn[layer, q], writes=[b_Cs[s]], allow_slow_non_contiguous=True)
                sp.dma(s_exs, exs[:, :, :, 0:16], spool[layer, :, u], writes=[b_exs])

            if "sonly" in cfg.dbg:
                tiles = [t_ for t_ in tiles if t_[1] >= cfg.NPT]
            if "ponly" in cfg.dbg:
                tiles = [t_ for t_ in tiles if t_[1] < cfg.NPT]
            ytoks = []
            ntiles_run = len(tiles) if cfg.max_tiles is None else min(len(tiles), cfg.max_tiles)

            def tinfo(T):
                r, l = tiles[T]
                is_prompt = l < cfg.NPT
                p = r * cfg.NPT + l if is_prompt else None
                u = None if is_prompt else r * cfg.NST + (l - cfg.NPT)
                out_kv = (not is_prompt) or p >= cfg.NPTILES - 4
                out_pool = (not is_prompt) or p == cfg.NPTILES - 1
                if "noout" in cfg.dbg:
                    out_kv = out_pool = False
                if "allpool" in cfg.dbg:
                    out_pool = True
                slot = (p % 6) if is_prompt else 6 + (u % 2)
                return T % 2, is_prompt, p, u, out_kv, out_pool, slot

            def front(T):
                b, is_prompt, p, u, out_kv, out_pool, slot = tinfo(T)
                if T + 1 < ntiles_run:
                    load_x(T + 1)
                    yield
                norm_tile(xs[b], b_xs[b], gnb, b_gnb)
                yield
                transpose_to(hT[b], b_hT[b], hb, b_hb, 16)
                yield

                def tm_block(c0, n):
                    ps, bps = next_fr()
                    for k in range(16):
                        pe.op("matmul", out=ps[:, 0:n], lhsT=hT[b][:, k, :], rhs=Wtm[:, k, c0:c0 + n],
                              start=(k == 0), stop=(k == 15), reads=[b_hT[b], b_Wtm], writes=[bps],
                              signal=(k == 15))
                    return ps, bps
                ps, bps = tm_block(2304, 2)
                yield
                dve.op("tensor_tensor", out=gat[b][:], in0=ps[:, 0:2], in1=bgt[:], op=ALU.add,
                       reads=[bps, b_bgt], writes=[b_gat[b]])
                yield
                ps, bps = tm_block(0, 512)
                yield
                if out_pool:
                    act.op("copy", out=pus[b][:], in_=ps[:, 0:256], reads=[bps], writes=[b_pus[b]])
                    yield
                act.op("activation", out=ktok[b][:], in_=ps[:, 256:512], func=AF.Copy, scale=1.0 / 16.0,
                       reads=[bps], writes=[b_ktok[b]])
                yield
                ps, bps = tm_block(512, 512)
                yield
                dve.op("tensor_copy", out=vsb[b][:], in_=ps[:, 0:256], reads=[bps], writes=[b_vsb[b]])
                yield
                act.op("activation", out=sgmo[b][:], in_=ps[:, 256:512], func=AF.Sigmoid, reads=[bps], writes=[b_sgmo[b]])
                yield
                ps, bps = tm_block(1024, 512)
                yield
                act.op("activation", out=slmz[b][:], in_=ps[:, 0:256], func=AF.Silu, reads=[bps], writes=[b_slmz[b]])
                yield
                dve.op("tensor_copy", out=aqs[b][:], in_=ps[:, 256:512], reads=[bps], writes=[b_aqs[b]])
                yield
                ps, bps = tm_block(1536, 512)
                yield
                dve.op("tensor_copy", out=aks[b][:], in_=ps[:, 0:256], reads=[bps], writes=[b_aks[b]])
                yield
                dve.op("tensor_copy", out=vring[:, slot, :, 0:128], in_=ps[:, 256:512].rearrange("p (h d) -> p h d", d=128),
                       reads=[bps], writes=[b_vring[slot]])
                yield
                if out_kv:
                    dve.op("tensor_copy", out=avs[b][:], in_=ps[:, 256:512], reads=[bps], writes=[b_avs[b]])
                    yield
                ps, bps = tm_block(2048, 256)
                yield
                act.op("activation", out=slaz[b][:], in_=ps[:, 0:256], func=AF.Silu, reads=[bps], writes=[b_slaz[b]])
                yield

                nseg, Ls = (1, 128) if is_prompt else (2, 64)
                for pair in range(3):
                    psf, bpsf = next_fr()
                    pfm = psf[:, 0:256].rearrange("p (a b) -> p a b", b=128)
                    for i2 in range(2):
                        cbk = pair * 2 + i2
                        for k in range(16):
                            pe.op("matmul", out=pfm[:, i2, :], lhsT=Wfm[:, k, cbk * 128:(cbk + 1) * 128],
                                  rhs=hT[b][:, k, :], start=(k == 0), stop=(k == 15),
                                  reads=[b_hT[b], b_Wfm], writes=[bpsf], signal=(k == 15 and i2 == 1))
                            yield
                    src = pfm[:, 0:2, :]
                    if pair == 0:
                        dve.op("tensor_copy", out=puT[b][:], in_=src, reads=[bpsf], writes=[b_puT[b]])
                        yield
                    elif pair == 1:
                        act.op("activation", out=slpz[b][:], in_=src, func=AF.Silu, reads=[bpsf], writes=[b_slpz[b]])
                        yield
                    else:
                        act.op("copy", out=qTm[b][:], in_=src, reads=[bpsf], writes=[b_qTm[b]])
                        yield


            def back(T, extra=()):
                b, is_prompt, p, u, out_kv, out_pool, slot = tinfo(T)
                last_prompt = is_prompt and p == cfg.NPTILES - 1
                MASK = TRIL if is_prompt else TRILBD
                if not is_prompt and "nosload" not in cfg.dbg:
                    load_sample_state(u)
                if is_prompt:
                    dve.op("tensor_copy", out=ext[:, :, 16:144], in_=puT[b][:], reads=[b_puT[b]], writes=[b_ext])
                else:
                    for h in range(2):
                        dve.op("tensor_copy", out=exs[:, h, :, 16:80],
                               in_=puT[b][:, h, :].rearrange("p (s t) -> p s t", t=64),
                               reads=[b_puT[b]], writes=[b_exs])
                    for s in range(2):
                        dve.op("tensor_copy", out=qTz[s][:, :, 64 * s:64 * s + 64],
                               in_=qTm[b][:, :, 64 * s:64 * s + 64],
                               reads=[b_qTm[b]], writes=[b_qTz[s]])

                def gen_pool():
                    nseg_, Ls_ = (1, 128) if is_prompt else (2, 64)
                    W = 16 + Ls_
                    for sg_ in range(nseg_):
                        if is_prompt:
                            def sl(t, a, c, sg_=sg_):
                                return t[:, :, a:c]
                            E, bE = ext, b_ext
                            cur = ext[:, :, 16:144]
                            accv = pacc[:]
                            mTv = mT[:]
                        else:
                            def sl(t, a, c, sg_=sg_):
                                if t is exs:
                                    return exs[:, :, sg_, a:c]
                                return t[:, :, sg_ * 80 + a:sg_ * 80 + c]
                            E, bE = exs, b_exs
                            cur = exs[:, :, sg_, 16:80]
                            accv = pacc[:, :, 64 * sg_:64 * sg_ + 64]
                            mTv = mT[:, :, 64 * sg_:64 * sg_ + 64]
                        rd = [bE, b_pw]
                        dve.op("tensor_tensor", out=sl(psA, 1, W), in0=sl(E, 1, W), in1=sl(E, 0, W - 1), op=ALU.add, reads=rd, writes=[b_pw])
                        yield
                        dve.op("tensor_scalar", out=accv, in0=sl(psA, 16, W), scalar1=pse[:, 0:1], scalar2=None, op0=ALU.mult,
                               reads=[b_pw, b_pse], writes=[b_pw])
                        yield
                        dve.op("tensor_tensor", out=sl(psB, 3, W), in0=sl(psA, 3, W), in1=sl(psA, 1, W - 2), op=ALU.add, reads=rd, writes=[b_pw])
                        yield
                        dve.op("scalar_tensor_tensor", out=accv, in0=sl(psB, 16, W), scalar=pse[:, 1:2], in1=accv,
                               op0=ALU.mult, op1=ALU.add, reads=[b_pw, b_pse], writes=[b_pw])
                        yield
                        dve.op("tensor_tensor", out=sl(psA, 7, W), in0=sl(psB, 7, W), in1=sl(psB, 3, W - 4), op=ALU.add, reads=rd, writes=[b_pw])
                        yield
                        dve.op("scalar_tensor_tensor", out=accv, in0=sl(psA, 16, W), scalar=pse[:, 2:3], in1=accv,
                               op0=ALU.mult, op1=ALU.add, reads=[b_pw, b_pse], writes=[b_pw])
                        yield
                        dve.op("tensor_tensor", out=sl(psB, 15, W), in0=sl(psA, 15, W), in1=sl(psA, 7, W - 8), op=ALU.add, reads=rd, writes=[b_pw])
                        yield
                        dve.op("scalar_tensor_tensor", out=accv, in0=sl(psB, 16, W), scalar=pse[:, 3:4], in1=accv,
                               op0=ALU.mult, op1=ALU.add, reads=[b_pw, b_pse], writes=[b_pw])
                        yield
                        if is_prompt and p == 0:
                            for h_ in range(2):
                                dve.op("tensor_tensor", out=pacc[:, h_, :], in0=pacc[:, h_, :], in1=prc_t[:, 0, :], op=ALU.mult,
                                       reads=[b_pw, b_prc], writes=[b_pw])
                                yield
                        else:
                            dve.op("tensor_scalar", out=accv, in0=accv, scalar1=prc_t[:, 1, 0:1], scalar2=None, op0=ALU.mult,
                                   reads=[b_pw, b_prc], writes=[b_pw])
                            yield
                        dve.op("tensor_tensor", out=mTv, in0=accv, in1=cur, op=ALU.subtract, reads=[b_pw, bE], writes=[b_mT])
                        yield
                    if is_prompt:
                        dve.op("tensor_copy", out=ext[:, :, 0:16], in_=ext[:, :, 128:144], reads=[b_ext, b_pw], writes=[b_ext])
                        yield
                    while not ml_done[0]:
                        yield
                    ps, bps = next_tm()
                    for ob in range(2):
                        for kc in range(2):
                            pe.op("matmul", out=ps[:, ob * 128:(ob + 1) * 128], lhsT=wpl[:, kc, ob * 128:(ob + 1) * 128],
                                  rhs=mT[:, kc, :], start=(kc == 0), stop=(kc == 1), reads=[b_wpl, b_mT], writes=[bps],
                                  signal=(ob == 1 and kc == 1))
                            yield
                    for ob in range(2):
                        dve.op("scalar_tensor_tensor", out=yst[b][:, ob, 0, :], in0=ps[:, ob * 128:(ob + 1) * 128],
                               scalar=psc[:, ob:ob + 1], in1=slpz[b][:, ob, :], op0=ALU.mult, op1=ALU.mult,
                               reads=[bps, b_psc, b_slpz[b]], writes=[b_yst[b]])
                        yield


                    yield
                def gen_ml():
                    act.op("activation", out=gw[:, 0:1], in_=gat[b][:, 1:2], func=AF.Exp, scale=-1.0,
                           reads=[b_gat[b]], writes=[*c_gw[0:1]])
                    yield
                    act.op("activation", out=gw[:, 1:2], in_=gw[:, 0:1], func=AF.Ln, bias=1.0, reads=[*c_gw[0:1]], writes=[*c_gw[1:2]])
                    yield
                    pe.op("matmul", out=pG[:, 0:1], lhsT=cm[:, MASK, :], rhs=gw[:, 1:2], start=True, stop=True,
                          reads=[b_cm, *c_gw[1:2]], writes=[b_pG], signal=False)
                    yield
                    pe.op("matmul", out=pG[:, 1:2], lhsT=cm[:, (ONES if is_prompt else ONESBD), :], rhs=gw[:, 1:2],
                          start=True, stop=True, reads=[b_cm, *c_gw[1:2]], writes=[b_pG], signal=is_prompt)
                    yield
                    if not is_prompt:
                        pe.op("matmul", out=pG[:, 2:3], lhsT=cm[:, SELLO, :], rhs=gw[:, 1:2], start=True, stop=True,
                              reads=[b_cm, *c_gw[1:2]], writes=[b_pG], signal=False)
                        yield
                        pe.op("matmul", out=pG[:, 3:4], lhsT=cm[:, SELHI, :], rhs=gw[:, 1:2], start=True, stop=True,
                              reads=[b_cm, *c_gw[1:2]], writes=[b_pG])
                        yield
                    dve.op("tensor_tensor", out=gw[:, 2:3], in0=gat[b][:, 0:1], in1=pG[:, 0:1], op=ALU.add,
                           reads=[b_gat[b], b_pG], writes=[*c_gw[2:3]])
                    yield
                    act.op("activation", out=gw[:, 3:4], in_=gw[:, 2:3], func=AF.Exp, reads=[*c_gw[2:3]], writes=[*c_gw[3:4]])
                    yield
                    act.op("activation", out=gw[:, 4:5], in_=pG[:, 0:1], func=AF.Exp, reads=[b_pG], writes=[*c_gw[4:5]])
                    yield
                    if is_prompt:
                        act.op("activation", out=gw[:, 5:6], in_=pG[:, 1:2], func=AF.Exp, scale=-1.0,
                               reads=[b_pG], writes=[*c_gw[5:6]])
                        yield
                    dve.op("tensor_scalar", out=vpx[:, 0:256], in0=vsb[b][:], scalar1=gw[:, 3:4], scalar2=None, op0=ALU.mult,
                           reads=[b_vsb[b], *c_gw[3:4]], writes=[b_vpx])
                    yield
                    dve.op("tensor_copy", out=vpx[:, 256:257], in_=gw[:, 3:4], reads=[*c_gw[3:4]], writes=[b_vpx])
                    yield
                    transpose_to(kTm, b_kTm, ktok[b], b_ktok[b], 2)
                    yield
                    for kc in range(2):
                        pe.op("matmul", out=pS, lhsT=kTm[:, kc, :], rhs=qTm[b][:, kc, :], start=(kc == 0), stop=(kc == 1),
                              reads=[b_kTm, b_qTm[b]], writes=[b_pS], signal=(kc == 1))
                        yield
                    dve.op("tensor_tensor", out=PTm[:], in0=pS, in1=cm[:, MASK, :], op=ALU.mult,
                           reads=[b_pS, b_cm], writes=[b_PTm])
                    yield
                    pe.op("matmul", out=pN, lhsT=PTm[:], rhs=vpx[:], start=True, stop=False,
                          reads=[b_PTm, b_vpx], writes=[b_pN], signal=False)
                    yield
                    if is_prompt:
                        for kc in range(2):
                            pe.op("matmul", out=pN, lhsT=qTm[b][:, kc, :], rhs=Chb[:, kc, :], start=False, stop=(kc == 1),
                                  reads=[b_qTm[b], b_Chb], writes=[b_pN], signal=(kc == 1))
                            yield
                    else:
                        for s in range(2):
                            q = 2 * u + s
                            act.op("activation", out=gw[:, 12 + s:13 + s], in_=smt[:, q:q + 1], func=AF.Exp,
                                   reads=[b_smt], writes=[*c_gw[12 + s:13 + s]])
                            yield
                            dve.op("tensor_scalar", out=Cs[s][:], in0=Cs[s][:], scalar1=gw[:, 12 + s:13 + s], scalar2=None,
                                   op0=ALU.mult, reads=[b_Cs[s], *c_gw[12 + s:13 + s]], writes=[b_Cs[s]])
                            yield
                            act.op("copy", out=Csb[s][:], in_=Cs[s][:], reads=[b_Cs[s]], writes=[b_Csb[s]])
                            yield
                        for s in range(2):
                            for kc in range(2):
                                last = (s == 1 and kc == 1)
                                pe.op("matmul", out=pN, lhsT=qTz[s][:, kc, :], rhs=Csb[s][:, kc, :], start=False, stop=last,
                                      reads=[b_qTz[s], b_Csb[s]], writes=[b_pN], signal=last)
                                yield
                    dve.op("tensor_copy", out=gw[:, 6:7], in_=pN[:, 256:257], reads=[b_pN], writes=[*c_gw[6:7]])
                    yield
                    dve.op("scalar_tensor_tensor", out=gw[:, 6:7], in0=gw[:, 6:7], scalar=-1.0, in1=gw[:, 6:7],
                           op0=ALU.mult, op1=ALU.max, reads=[*c_gw[6:7], *c_gw[6:7]], writes=[*c_gw[6:7]])
                    yield
                    dve.op("tensor_tensor", out=gw[:, 6:7], in0=gw[:, 6:7], in1=gw[:, 4:5], op=ALU.max, reads=[*c_gw[6:7], *c_gw[4:5]], writes=[*c_gw[6:7]])
                    yield
                    dve.op("reciprocal", out=gw[:, 7:8], in_=gw[:, 6:7], reads=[*c_gw[6:7]], writes=[*c_gw[7:8]])
                    yield
                    dve.op("scalar_tensor_tensor", out=hg[:], in0=pN[:, 0:256], scalar=gw[:, 7:8], in1=sgmo[b][:],
                           op0=ALU.mult, op1=ALU.mult, reads=[b_pN, b_sgmo[b], *c_gw[7:8]], writes=[b_hg])
                    yield
                    act.op("activation", out=junk2[:, 0:256], in_=hg[:], func=AF.Square, accum_out=gw[:, 8:9],
                           reads=[b_hg], writes=[b_junk2, *c_gw[8:9]])
                    yield
                    act.op("activation", out=gw[:, 9:10], in_=gw[:, 8:9], func=AF.Sqrt, scale=1.0 / 256.0, bias=EPS,
                           reads=[*c_gw[8:9]], writes=[*c_gw[9:10]])
                    yield
                    dve.op("reciprocal", out=gw[:, 9:10], in_=gw[:, 9:10], reads=[*c_gw[9:10]], writes=[*c_gw[9:10]])
                    yield
                    dve.op("scalar_tensor_tensor", out=hg[:], in0=hg[:], scalar=gw[:, 9:10], in1=mlgt[:],
                           op0=ALU.mult, op1=ALU.mult, reads=[b_hg, b_mlgt, *c_gw[9:10]], writes=[b_hg])
                    yield
                    dve.op("tensor_tensor", out=yml[:], in0=hg[:], in1=slmz[b][:], op=ALU.mult,
                           reads=[b_hg, b_slmz[b]], writes=[b_yml])
                    yield
                    psd, b_pD = next_tm()
                    pD = psd[0:1, 0:128]
                    pe.op("matmul", out=pD, lhsT=gw[:, 2:3], rhs=cm[:, IDENT, :], start=True, stop=True,
                          reads=[b_cm, *c_gw[2:3]], writes=[b_pD])
                    yield
                    if is_prompt:
                        dve.op("reduce_max", out=g0[:, 2:3], in_=pD, axis=AX.X, reads=[b_pD], writes=[*c_g0[2:3]])
                        yield
                        dve.op("tensor_tensor", out=g0[:, 4:5], in0=g0[:, 2:3], in1=g0[:, 0:1], op=ALU.add, reads=[*c_g0[2:3], *c_g0[0:1]], writes=[*c_g0[4:5]])
                        yield
                        dve.op("tensor_tensor", out=g0[:, 1:2], in0=g0[:, 1:2], in1=g0[:, 4:5], op=ALU.max, reads=[*c_g0[1:2], *c_g0[4:5]], writes=[*c_g0[1:2]])
                        yield
                        dve.op("tensor_tensor", out=g0[:, 0:1], in0=g0[:, 0:1], in1=pG[0:1, 1:2], op=ALU.add,
                               reads=[b_pG, *c_g0[0:1]], writes=[*c_g0[0:1]])
                        yield
                    elif "nosmb" in cfg.dbg:
                        dve.op("memset", ap=gw[:, 10:12], constant=1.0, writes=[])
                        yield
                    else:
                        dve.op("reduce_max", out=g0[:, 2:4], in_=pD.rearrange("p (s t) -> p s t", t=64), axis=AX.X,
                               reads=[b_pD], writes=[*c_g0[2:4]])
                        yield
                        dve.op("tensor_tensor", out=g0[:, 4:6], in0=g0[:, 2:4], in1=smt[0:1, 2 * u:2 * u + 2], op=ALU.max,
                               reads=[b_smt, *c_g0[2:4]], writes=[*c_g0[4:6]])
                        yield
                        dve.op("tensor_tensor", out=mrow[:, 1 + 2 * u:3 + 2 * u], in0=g0[:, 4:6], in1=pG[0:1, 2:4], op=ALU.subtract,
                               reads=[b_pG, *c_g0[4:6]], writes=[b_mrow])
                        yield
                        act.op("activation", out=g0[:, 8:10], in_=g0[:, 4:6], func=AF.Exp, scale=-1.0, reads=[*c_g0[4:6]], writes=[*c_g0[8:10]])
                        yield
                        psb, b_pBc = next_tm()
                        pBc = psb[:, 0:8]
                        if "nopbc" in cfg.dbg:
                            dve.op("memset", ap=gw[:, 10:12], constant=1.0, writes=[])
                            yield
                        else:
                            pe.op("matmul", out=pBc[:, 0:2], lhsT=cm[0:1, ONES, :], rhs=g0[:, 8:10], start=True, stop=True,
                                  reads=[b_cm, *c_g0[8:10]], writes=[b_pBc])
                            yield
                            dve.op("tensor_copy", out=gw[:, 10:12], in_=pBc[:, 0:2], reads=[b_pBc], writes=[*c_gw[10:12]])
                            yield
                    if is_prompt:
                        for kc in range(2):
                            ps, bps = next_tm()
                            pe.op("matmul", out=ps[:, 0:257], lhsT=ktok[b][:, kc * 128:(kc + 1) * 128], rhs=vpx[:],
                                  start=True, stop=True, reads=[b_ktok[b], b_vpx], writes=[bps])
                            yield
                            dve.op("tensor_scalar", out=Ch[:, kc, :], in0=Ch[:, kc, :], scalar1=gw[:, 5:6], scalar2=None,
                                   op0=ALU.mult, reads=[b_Ch, *c_gw[5:6]], writes=[b_Ch])
                            yield
                            dve.op("scalar_tensor_tensor", out=Ch[:, kc, :], in0=ps[:, 0:257], scalar=gw[:, 5:6], in1=Ch[:, kc, :],
                                   op0=ALU.mult, op1=ALU.add, reads=[bps, b_Ch, *c_gw[5:6]], writes=[b_Ch])
                            yield
                            act.op("copy", out=Chb[:, kc, :], in_=Ch[:, kc, :], reads=[b_Ch], writes=[b_Chb])
                            yield
                        if last_prompt:
                            dve.op("tensor_tensor", out=mrow[:, 0:1], in0=g0[:, 1:2], in1=g0[:, 0:1], op=ALU.subtract,
                                   reads=[*c_g0[1:2], *c_g0[0:1]], writes=[b_mrow])
                            yield
                            act.op("activation", out=g0[:, 8:9], in_=mrow[:, 0:1], func=AF.Exp, scale=-1.0,
                                   reads=[b_mrow], writes=[*c_g0[8:9]])
                            yield
                            psb, b_pBc = next_tm()
                            pBc = psb[:, 0:8]
                            pe.op("matmul", out=pBc[:, 0:1], lhsT=cm[0:1, ONES, :], rhs=g0[:, 8:9], start=True, stop=True,
                                  reads=[b_cm, *c_g0[8:9]], writes=[b_pBc])
                            yield
                            dve.op("tensor_copy", out=gw[:, 10:11], in_=pBc[:, 0:1], reads=[b_pBc], writes=[*c_gw[10:11]])
                            yield
                            dve.op("tensor_scalar", out=cst[:], in0=Ch[:], scalar1=gw[:, 10:11], scalar2=None, op0=ALU.mult,
                                   reads=[b_Ch, *c_gw[10:11]], writes=[b_cst])
                            yield
                            sp.dma(s_cst, o_c[layer, 0].rearrange("(k p) v -> p k v", p=128), cst[:, :, 0:256], reads=[b_cst])
                            yield
                            sp.dma(s_cst, o_n[layer, 0].rearrange("(k p) -> p k", p=128), cst[:, :, 256], reads=[b_cst], allow_slow_non_contiguous=True)
                            yield
                    elif "noK64" not in cfg.dbg:
                        for s in range(2):
                            q = 2 * u + s
                            for kc in range(2):
                                ps, bps = next_tm()
                                pe.op("matmul", out=ps[:, 0:257], lhsT=ktok[b][64 * s:64 * s + 64, kc * 128:(kc + 1) * 128],
                                      rhs=vpx[64 * s:64 * s + 64, :], start=True, stop=True,
                                      reads=[b_ktok[b], b_vpx], writes=[bps])
                                yield
                                dve.op("tensor_tensor", out=cst[:, kc, :], in0=Cs[s][:, kc, :], in1=ps[:, 0:257], op=ALU.add,
                                       reads=[b_Cs[s], bps], writes=[b_cst])
                                yield
                            dve.op("tensor_scalar", out=cst[:], in0=cst[:], scalar1=gw[:, 10 + s:11 + s], scalar2=None,
                                   op0=ALU.mult, reads=[b_cst, *c_gw[10 + s:11 + s]], writes=[b_cst])
                            yield
                            sp.dma(s_cst, o_c[layer, 1 + q].rearrange("(k p) v -> p k v", p=128), cst[:, :, 0:256], reads=[b_cst])
                            yield
                            sp.dma(s_cst, o_n[layer, 1 + q].rearrange("(k p) -> p k", p=128), cst[:, :, 256], reads=[b_cst], allow_slow_non_contiguous=True)
                            yield


                    yield
                    ml_done[0] = True
                    yield
                def gen_att():
                    for h in range(2):
                        act.op("activation", out=junk2[:, 0:128], in_=aqs[b][:, h * 128:(h + 1) * 128], func=AF.Square,
                               accum_out=aw[:, h:h + 1], reads=[b_aqs[b]], writes=[b_junk2, *c_aw[h:h + 1]])
                        yield
                        act.op("activation", out=junk2[:, 0:128], in_=aks[b][:, h * 128:(h + 1) * 128], func=AF.Square,
                               accum_out=aw[:, 2 + h:3 + h], reads=[b_aks[b]], writes=[b_junk2, *c_aw[2 + h:3 + h]])
                        yield
                    act.op("activation", out=aw[:, 4:6], in_=aw[:, 0:2], func=AF.Sqrt, scale=1.0, bias=128.0 * EPS,
                           reads=[*c_aw[0:2]], writes=[*c_aw[4:6]])
                    yield
                    act.op("activation", out=aw[:, 6:8], in_=aw[:, 2:4], func=AF.Sqrt, scale=1.0 / 128.0, bias=EPS,
                           reads=[*c_aw[2:4]], writes=[*c_aw[6:8]])
                    yield
                    dve.op("reciprocal", out=aw[:, 4:8], in_=aw[:, 4:8], reads=[*c_aw[4:8]], writes=[*c_aw[4:8]])
                    yield
                    for h in range(2):
                        hs = slice(h * 128, (h + 1) * 128)
                        dve.op("scalar_tensor_tensor", out=qhb[:, hs], in0=aqs[b][:, hs], scalar=aw[:, 4 + h:5 + h], in1=aqgt[:, hs],
                               op0=ALU.mult, op1=ALU.mult, reads=[b_aqs[b], b_aqgt, *c_aw[4 + h:5 + h]], writes=[b_qhb])
                        yield
                        dve.op("scalar_tensor_tensor", out=khs[b][:, hs], in0=aks[b][:, hs], scalar=aw[:, 6 + h:7 + h], in1=akgt[:, hs],
                               op0=ALU.mult, op1=ALU.mult, reads=[b_aks[b], b_akgt, *c_aw[6 + h:7 + h]], writes=[b_khs[b]])
                        yield
                    act.op("copy", out=khb[:], in_=khs[b][:], reads=[b_khs[b]], writes=[b_khb])
                    yield
                    transpose_to(qTa, b_qTa, qhb, b_qhb, 2)
                    yield
                    transpose_to(kring[:, slot, :, :], b_kring[slot], khb, b_khb, 2)
                    yield
                    for h in range(2):
                        if is_prompt:
                            kbs = list(range(max(0, 4 - p), 5))
                            nA = len([kb for kb in kbs if kb < 4])
                            pAv = pAS[:, 0:512].rearrange("p (a b) -> p a b", b=128)
                            for kb in kbs:
                                sl_k = (p - 4 + kb) % 6
                                dst = pAv[:, kb, :] if kb < 4 else pA5
                                bd = b_pAS if kb < 4 else b_pA5
                                pe.op("matmul", out=dst, lhsT=kring[:, sl_k, h, :], rhs=qTa[:, h, :], start=True, stop=False,
                                      reads=[b_kring[sl_k], b_qTa], writes=[bd], signal=False)
                                yield
                                pe.op("matmul", out=dst, lhsT=identb[:], rhs=bhi[:, h * 6 + kb, :], start=False, stop=False,
                                      reads=[b_identb, b_bhi], writes=[bd], signal=False)
                                yield
                                pe.op("matmul", out=dst, lhsT=identb[:], rhs=blo[:, h * 6 + kb, :], start=False, stop=True,
                                      reads=[b_identb, b_blo], writes=[bd], signal=True)
                                yield
                            k0 = kbs[0]
                            if k0 < 4:
                                act.op("activation", out=PTp[:, k0:4, :], in_=pAv[:, k0:4, :], func=AF.Exp,
                                       reads=[b_pAS], writes=[b_PTp])
                                yield
                            act.op("activation", out=PTp[:, 4, :], in_=pA5, func=AF.Exp, reads=[b_pA5], writes=[b_PTp])
                            yield
                            for i, kb in enumerate(kbs):
                                sl_k = (p - 4 + kb) % 6
                                pe.op("matmul", out=pPV[:, h, :], lhsT=PTp[:, kb, :], rhs=vring[:, sl_k, h, :],
                                      start=(i == 0), stop=(i == len(kbs) - 1), reads=[b_PTp, b_vring[sl_k]], writes=[b_pPV],
                                      signal=(i == len(kbs) - 1))
                                yield
                        elif "nosatt" in cfg.dbg:
                            pe.op("matmul", out=pPV[:, h, :], lhsT=PTp[:, 4, :], rhs=vring[:, slot, h, :], start=True, stop=True,
                                  reads=[b_PTp, b_vring[slot]], writes=[b_pPV])
                            yield
                        else:
                            pAv = pAS[:, 0:512].rearrange("p (a b) -> p a b", b=64)
                            for s in range(2):
                                for kb in range(4):
                                    dst = pAv[:, s * 4 + kb, :]
                                    qs = slice(64 * s, 64 * s + 64)
                                    pe.op("matmul", out=dst, lhsT=kTc[:, s, h, kb * 128:(kb + 1) * 128], rhs=qTa[:, h, qs],
                                          start=True, stop=False, reads=[b_kTc[s], b_qTa], writes=[b_pAS], signal=False)
                                    yield
                                    pe.op("matmul", out=dst, lhsT=identb[:], rhs=bhi[:, h * 6 + kb, 0:64], start=False, stop=False,
                                          reads=[b_identb, b_bhi], writes=[b_pAS], signal=False)
                                    yield
                                    pe.op("matmul", out=dst, lhsT=identb[:], rhs=blo[:, h * 6 + kb, 0:64], start=False, stop=True,
                                          reads=[b_identb, b_blo], writes=[b_pAS], signal=(s == 1 and kb == 3))
                                    yield
                            pe.op("matmul", out=pA5, lhsT=kring[:, slot, h, :], rhs=qTa[:, h, :], start=True, stop=False,
                                  reads=[b_kring[slot], b_qTa], writes=[b_pA5], signal=False)
                            yield
                            pe.op("matmul", out=pA5, lhsT=identb[:], rhs=bhi[:, h * 6 + 5, :], start=False, stop=False,
                                  reads=[b_identb, b_bhi], writes=[b_pA5], signal=False)
                            yield
                            pe.op("matmul", out=pA5, lhsT=identb[:], rhs=blo[:, h * 6 + 5, :], start=False, stop=True,
                                  reads=[b_identb, b_blo], writes=[b_pA5])
                            yield
                            for s in range(2):
                                act.op("activation", out=PTs[s][:, :, 64 * s:64 * s + 64], in_=pAv[:, s * 4:s * 4 + 4, :], func=AF.Exp,
                                       reads=[b_pAS], writes=[b_PTs[s]])
                                yield
                            act.op("activation", out=PTp[:, 4, :], in_=pA5, func=AF.Exp, reads=[b_pA5], writes=[b_PTp])
                            yield
                            for s in range(2):
                                for kb in range(4):
                                    pe.op("matmul", out=pPV[:, h, :], lhsT=PTs[s][:, kb, :], rhs=Vc[:, s, kb, h, :],
                                          start=(s == 0 and kb == 0), stop=False, reads=[b_PTs[s], b_Vc[s]], writes=[b_pPV],
                                          signal=False)
                                    yield
                            pe.op("matmul", out=pPV[:, h, :], lhsT=PTp[:, 4, :], rhs=vring[:, slot, h, :], start=False, stop=True,
                                  reads=[b_PTp, b_vring[slot]], writes=[b_pPV])
                            yield
                    dve.op("reciprocal", out=aw[:, 8:10], in_=pPV[:, :, 128], reads=[b_pPV], writes=[*c_aw[8:10]])
                    yield
                    for h in range(2):
                        hs = slice(h * 128, (h + 1) * 128)
                        dve.op("scalar_tensor_tensor", out=yat[:, hs], in0=pPV[:, h, 0:128], scalar=aw[:, 8 + h:9 + h],
                               in1=slaz[b][:, hs], op0=ALU.mult, op1=ALU.mult, reads=[b_pPV, b_slaz[b], *c_aw[8 + h:9 + h]], writes=[b_yat])
                        yield


                    yield
                ml_done[0] = False
                gens = list(extra) + [gen_pool(), gen_ml(), gen_att()]
                while gens:
                    for g_ in list(gens):
                        try:
                            next(g_)
                        except StopIteration:
                            gens.remove(g_)

                transpose_to(yst[b][:, :, 1, :], b_yst[b], yml, b_yml, 2)
                transpose_to(yst[b][:, :, 2, :], b_yst[b], yat, b_yat, 2)
                ytok = sp.dma(s_yst[b], ysrc[T // 3].rearrange("(c p) t -> p c t", p=128)[:, :, (T % 3) * 128:(T % 3 + 1) * 128],
                              yst[b][:].rearrange("p h b t -> p (h b) t"), reads=[b_yst[b]])
                ytoks.append(ytok)
                if T % 3 == 2 and cfg.max_tiles is None:
                    pool.wait_tok(ytoks[-1])
                    pool.wait_tok(ytoks[-2])
                    issue_collective(ysrc[T // 3], ydst[T // 3])

                if out_pool and "nosout" not in cfg.dbg:
                    tokp = sp.dma(s_pscr[b], pscr[b], pus[b][:], reads=[b_pus[b]])
                    sp.wait_tok(tokp)
                    if is_prompt:
                        sp.dma(s_out[b], o_pool[layer, 0], pscr[b, 113:128, :])
                    else:
                        for s in range(2):
                            sp.dma(s_out[b], o_pool[layer, 1 + 2 * u + s], pscr[b, 64 * s + 49:64 * s + 64, :])
                    sp.wait_tok(s_out[b].last())
                if out_kv:
                    row0 = (p - (cfg.NPTILES - 4)) * 128 if is_prompt else 512 + u * 128
                    if "nook" not in cfg.dbg:
                        sp.dma(s_out[b], o_k[layer, row0:row0 + 128, :], khs[b][:], reads=[b_khs[b]])
                    if "noov" not in cfg.dbg:
                        sp.dma(s_out[b], o_v[layer, row0:row0 + 128, :], avs[b][:], reads=[b_avs[b]])
                if out_pool or out_kv:
                    for bb_ in (b_pus[b], b_khs[b], b_avs[b]):
                        if s_out[b].key in bb_.r:
                            bb_.r[s_out[b].key] = s_out[b].last()
            load_x(0)
            for _ in front(0):
                pass
            for T in range(ntiles_run):
                back(T, [front(T + 1)] if T + 1 < ntiles_run else [])
            sp.dma(s_mrow, o_m[layer:layer + 1, :], mrow[:], reads=[b_mrow])

            barrier()
            if cfg.stop_after == f"A{layer}":
                break

            ar.release(m_persist)
            HT, HTOK, TG, NTG = cfg.HT, cfg.HTOK, cfg.TG, cfg.NTG
            hTB = ar.alloc("hTB", [128, 16, HTOK], BF16); b_hTB = B("hTB")
            mgT = ar.alloc("mgT", [128, 16, HTOK], BF16); b_mgT = B("mgT")
            yTb = [ar.alloc("yTb0", [128, NTG, 8, TG], BF16), ar.alloc("yTb1", [128, NTG, 8, TG], BF16)]
            b_yTb = [B(), B()]
            s_yTb = [DmaSem(cx, "sytb0"), DmaSem(cx, "sytb1")]
            NW = 2
            wslot = [ar.alloc(f"wsl{i}", [128, 16 * 512 + 2 * 512], BF16) for i in range(NW)]
            b_wslot = [B() for _ in range(NW)]
            s_wslot = [DmaSem(cx, f"swsl{i}") for i in range(NW)]
            wsl_i = {"i": 0}
            NWS = 6
            b_wsm = [B() for _ in range(NWS)]
            s_wsm = [DmaSem(cx, f"swsm{i}") for i in range(NWS)]
            wsm_i = {"i": 0}
            sgs = [ar.alloc("sgs0", [128, 512], F32), ar.alloc("sgs1", [128, 512], F32)]
            b_sgs = [B(), B()]
            tmpB = [ar.alloc("tmpB0", [128, 512], F32), ar.alloc("tmpB1", [128, 512], F32)]
            b_tmpB = [B(), B()]
            xsl = [ar.alloc(f"xsl{i}", [128, 512], F32) for i in range(3)]
            b_xsl = [B() for _ in range(3)]
            s_xsl = [DmaSem(cx, f"sxsl{i}") for i in range(3)]
            xos = [ar.alloc(f"xos{i}", [128, 512], F32) for i in range(3)]
            b_xos = [B() for _ in range(3)]
            s_xos = [DmaSem(cx, f"sxos{i}") for i in range(3)]
            pT = ar.alloc("pT", [128, 2, HTOK], BF16); b_pT = B("pT")
            pst = ar.alloc("pst", [128, 256], F32); b_pst = B("pst")
            s_pst = DmaSem(cx, "spst")
            pbf = ar.alloc("pbf", [128, 256], BF16); b_pbf = B("pbf")
            gpb = gnb; b_gpb = b_gnb
            s_gn = DmaSem(cx, "sgn")
            PA = [pbs[2], pbs[3]]; b_PA = [B(), B()]
            PB = [pbs[4], pbs[5]]; b_PB = [B(), B()]
            rank_state = {}

            for hh in range(cfg.NH):
                t0 = hh * HTOK
                sp.dma(s_gn, gnb[:], gn[layer].partition_broadcast(128), writes=[b_gnb])
                for i in range(HT):
                    b = i % 2
                    sp.dma(s_xs[b], xs[b][:], xsrc_own[t0 + i * 128:t0 + (i + 1) * 128, :], writes=[b_xs[b]])
                    norm_tile(xs[b], b_xs[b], gnb, b_gnb)
                    transpose_to(hTB[:, :, i * 128:(i + 1) * 128], b_hTB, hb, b_hb, 16)
                par = 0
                for br in range(3):
                    yb = br % 2
                    for tg in range(NTG):
                        def fn(e, br=br, tg=tg, hh=hh, dstt=yTb[yb], NTG=NTG, CPR=CPR):
                            if "nodyn" in cfg.dbg:
                                rank_state["c"] = 0
                            if "c" not in rank_state:
                                rank_state["c"] = e.partition_id() % 4
                            c = rank_state["c"]
                            y4 = ydst.rearrange("(c k) r t -> c k r t", k=CPR)
                            src = y4[bass.ds(c, 1), hh * NTG + tg, :, :].rearrange("o (rh b p) t -> p (o rh) b t", b=3, p=128)
                            return e.dma_start(out=dstt[:, tg, :, :], in_=src[:, :, br, :])
                        sp.dma(s_yTb[yb], None, None, writes=[b_yTb[yb]], fn=fn)
                    for eb in range(16):
                        wi = wsm_i["i"]
                        wsm_i["i"] = (wi + 1) % NWS
                        wbase = wslot[wi // 3][:, (wi % 3) * 3072:(wi % 3 + 1) * 3072]
                        wg = wbase[:, 0:2048].rearrange("p (k c) -> p k c", c=128)
                        wb = wbase[:, 2048:3072].rearrange("p (k c) -> p k c", c=128)
                        bw_ = b_wsm[wi]
                        pool.dma(s_wsm[wi], wg, w_gate[layer, br, eb].rearrange("(k p) c -> p k c", p=128), writes=[bw_, b_wslot[wi // 3]])
                        pool.dma(s_wsm[wi], wb, w_br[layer, br, eb].rearrange("(k p) c -> p k c", p=128), writes=[bw_, b_wslot[wi // 3]])
                        for tg in range(NTG):
                            ts = slice(tg * TG, (tg + 1) * TG)
                            par ^= 1
                            for k in range(16):
                                pe.op("matmul", out=PA[par][:, 0:TG], lhsT=wg[:, k, :], rhs=hTB[:, k, ts], start=(k == 0), stop=(k == 15),
                                      reads=[bw_, b_hTB], writes=[b_PA[par]], signal=(k == 15))
                            for k in range(8):
                                pe.op("matmul", out=PB[par][:, 0:TG], lhsT=wb[:, k, :], rhs=yTb[yb][:, tg, k, :], start=(k == 0), stop=(k == 7),
                                      reads=[bw_, b_yTb[yb]], writes=[b_PB[par]], signal=(k == 7))
                            act.op("activation", out=sgs[par][:, 0:TG], in_=PA[par][:, 0:TG], func=AF.Sigmoid,
                                   reads=[b_PA[par]], writes=[b_sgs[par]])
                            if br == 0:
                                dve.op("tensor_tensor", out=mgT[:, eb, ts], in0=sgs[par][:, 0:TG], in1=PB[par][:, 0:TG], op=ALU.mult,
                                       reads=[b_sgs[par], b_PB[par]], writes=[b_mgT])
                            else:
                                dve.op("tensor_tensor", out=tmpB[par][:, 0:TG], in0=sgs[par][:, 0:TG], in1=PB[par][:, 0:TG], op=ALU.mult,
                                       reads=[b_sgs[par], b_PB[par]], writes=[b_tmpB[par]])
                                dve.op("tensor_tensor", out=mgT[:, eb, ts], in0=mgT[:, eb, ts], in1=tmpB[par][:, 0:TG], op=ALU.add,
                                       reads=[b_tmpB[par], b_mgT], writes=[b_mgT])
                xi = 0
                for cb in range(4):
                    wi = wsl_i["i"]
                    wsl_i["i"] = (wi + 1) % NW
                    wo = wslot[wi][:, 0:8192].rearrange("p (k c) -> p k c", c=512)
                    for kq in range(2):
                        pool.dma(s_wslot[wi], wo[:, kq * 8:(kq + 1) * 8, :],
                                 w_out[layer, cb, kq * 1024:(kq + 1) * 1024, :].rearrange("(k p) c -> p k c", p=128),
                                 writes=[b_wslot[wi]] + b_wsm[wi * 3:wi * 3 + 3])
                    for i in range(HT):
                        rows = slice(t0 + i * 128, t0 + (i + 1) * 128)
                        cols = slice(cb * 512, (cb + 1) * 512)
                        xi = (xi + 1) % 3
                        par ^= 1
                        sp.dma(s_xsl[xi], xsl[xi][:], xsrc_own[rows, cols], writes=[b_xsl[xi]])
                        for k in range(16):
                            pe.op("matmul", out=PA[par][:], lhsT=mgT[:, k, i * 128:(i + 1) * 128], rhs=wo[:, k, :], start=(k == 0), stop=(k == 15),
                                  reads=[b_mgT, b_wslot[wi]], writes=[b_PA[par]], signal=(k == 15))
                        dve.op("tensor_tensor", out=xos[xi][:], in0=PA[par][:], in1=xsl[xi][:], op=ALU.add,
                               reads=[b_PA[par], b_xsl[xi]], writes=[b_xos[xi]])
                        sp.dma(s_xos[xi], xnew[rows, cols], xos[xi][:], reads=[b_xos[xi]])
                for i3 in range(3):
                    sp.wait_tok(s_xos[i3].last())
                sp.dma(s_gn, gnb[:], gp[layer].partition_broadcast(128), writes=[b_gnb])
                for i in range(HT):
                    b = i % 2
                    rows = slice(t0 + i * 128, t0 + (i + 1) * 128)
                    sp.dma(s_xs[b], xs[b][:], xnew[rows, :], writes=[b_xs[b]])
                    sp.dma(s_pst, pst[:], pown[layer, rows, :], writes=[b_pst])
                    norm_tile(xs[b], b_xs[b], gpb, b_gpb)
                    transpose_to(hTB[:, :, i * 128:(i + 1) * 128], b_hTB, hb, b_hb, 16)
                    dve.op("tensor_copy", out=pbf[:], in_=pst[:], reads=[b_pst], writes=[b_pbf])
                    transpose_to(pT[:, :, i * 128:(i + 1) * 128], b_pT, pbf, b_pbf, 2)
                for cb in range(4):
                    wi = wsl_i["i"]
                    wsl_i["i"] = (wi + 1) % NW
                    wo = wslot[wi][:, 0:8192].rearrange("p (k c) -> p k c", c=512)
                    wp = wslot[wi][:, 8192:9216].rearrange("p (k c) -> p k c", c=512)
                    for kq in range(2):
                        pool.dma(s_wslot[wi], wo[:, kq * 8:(kq + 1) * 8, :],
                                 w_pg[layer, cb, kq * 1024:(kq + 1) * 1024, :].rearrange("(k p) c -> p k c", p=128),
                                 writes=[b_wslot[wi]] + b_wsm[wi * 3:wi * 3 + 3])
                    pool.dma(s_wslot[wi], wp, w_pp[layer, cb].rearrange("(k p) c -> p k c", p=128), writes=[b_wslot[wi]])
                    for i in range(HT):
                        rows = slice(t0 + i * 128, t0 + (i + 1) * 128)
                        cols = slice(cb * 512, (cb + 1) * 512)
                        xi = (xi + 1) % 3
                        par ^= 1
                        sp.dma(s_xsl[xi], xsl[xi][:], xnew[rows, cols], writes=[b_xsl[xi]])
                        for k in range(16):
                            pe.op("matmul", out=PA[par][:], lhsT=hTB[:, k, i * 128:(i + 1) * 128], rhs=wo[:, k, :], start=(k == 0), stop=(k == 15),
                                  reads=[b_hTB, b_wslot[wi]], writes=[b_PA[par]], signal=(k == 15))
                        for k in range(2):
                            pe.op("matmul", out=PB[par][:], lhsT=pT[:, k, i * 128:(i + 1) * 128], rhs=wp[:, k, :], start=(k == 0), stop=(k == 1),
                                  reads=[b_pT, b_wslot[wi]], writes=[b_PB[par]], signal=(k == 1))
                        act.op("activation", out=sgs[par][:], in_=PA[par][:], func=AF.Sigmoid, reads=[b_PA[par]], writes=[b_sgs[par]])
                        dve.op("tensor_tensor", out=tmpB[par][:], in0=sgs[par][:], in1=PB[par][:], op=ALU.mult,
                               reads=[b_sgs[par], b_PB[par]], writes=[b_tmpB[par]])
                        dve.op("tensor_tensor", out=xos[xi][:], in0=tmpB[par][:], in1=xsl[xi][:], op=ALU.add,
                               reads=[b_tmpB[par], b_xsl[xi]], writes=[b_xos[xi]])
                        sp.dma(s_xos[xi], xdest[rows, cols], xos[xi][:], reads=[b_xos[xi]])
                for i3 in range(3):
                    sp.wait_tok(s_xos[i3].last())
                if layer == 0:
                    xo3 = xown1.rearrange("(i p) d -> i p d", p=128)
                    for i3 in range(3):
                        pool.wait_tok(s_xos[i3].last())
                    for i in range(hh * HT, (hh + 1) * HT):
                        issue_collective(xo3[i], xg1[i])

            barrier()

        with nc.Block() as block:
            @block.tensor
            def _(e):
                for f in pe.prog:
                    f(e)

            @block.scalar
            def _(e):
                for f in act.prog:
                    f(e)

            @block.vector
            def _(e):
                for f in dve.prog:
                    f(e)

            @block.gpsimd
            def _(e):
                for f in pool.prog:
                    f(e)

            @block.sync
            def _(e):
                for f in sp.prog:
                    f(e)
        stats = {q.name: (q.n_ins, q.n_wait) for q in cx.qs}
        stats["nsem"] = cx.nsem
        stats["logs"] = {q.name: q.log for q in cx.qs}
        stats["sbuf_peak"] = ar.off
    return nc, stats


POOL_WINDOWS = (2, 4, 8, 16)
OFF = {}
_acc = 0
for _n, _s in (("pu", 1024), ("pz", 1024), ("mq", 1024), ("mk", 1024), ("mv", 1024), ("mo", 1024), ("mz", 1024),
               ("mi", 4), ("mf", 4), ("aq", 1024), ("ak", 1024), ("av", 1024), ("az", 1024), ("gts", 3 * D)):
    OFF[_n] = _acc
    _acc += _s


def _consts():
    cm = np.zeros((128, 7, 128), np.float32)
    s = np.arange(128)[:, None]
    t = np.arange(128)[None, :]
    same = (s // 64) == (t // 64)
    cm[:, 0, :] = np.eye(128)
    cm[:, 1, :] = (s <= t)
    cm[:, 2, :] = (s <= t) & same
    cm[:, 3, :] = same
    cm[:, 4, :] = (s < 64) & (t >= 0)
    cm[:, 5, :] = (s >= 64) & (t >= 0)
    cm[:, 6, :] = 1.0
    return cm


def _bias_index():
    idx = np.full((6, 128, 128), -1, np.int64)
    jj = np.arange(128)[:, None]
    ii = np.arange(128)[None, :]
    ci = ii // 64
    for kb in range(5):
        relk = 128 * (kb - 4) + jj
        ok = (relk >= 64 * ci - 512) & (relk <= 64 * ci + 63)
        dist = np.clip(ii - relk, -256, 256) + 256
        idx[kb] = np.where(ok, dist, -1)
    same = (jj // 64) == (ii // 64)
    dist = np.clip((ii % 64) - (jj % 64), -256, 256) + 256
    idx[5] = np.where(same, dist, -1)
    return idx


def prepare_inputs(cfg, inp):
    f = np.float32
    NPT, NSR, NS, TR = cfg.NPT, cfg.NSR, cfg.NS, cfg.TR
    w_in = inp["w_in"]
    cm = _consts()
    bidx = _bias_index()
    maps = []
    for core in range(cfg.NCORES):
        g, j = divmod(core, 4)

        def rank_rows(arr_p, arr_s, r):
            a = arr_p[r * NPT * 128:(r + 1) * NPT * 128]
            bq = arr_s[g * NS + r * NSR: g * NS + (r + 1) * NSR]
            return np.concatenate([a, bq.reshape((-1,) + bq.shape[2:])], 0)

        m = {}
        m["xg"] = np.ascontiguousarray(np.concatenate(
            [rank_rows(inp["x_prompt"][g], inp["x_sample"], r) for r in range(4)], 0), f)
        m["xown"] = np.ascontiguousarray(rank_rows(inp["x_prompt"][g], inp["x_sample"], j), f)
        m["pown"] = np.ascontiguousarray(np.stack(
            [rank_rows(inp["p_prompt"][l, g], inp["p_sample"][l], j) for l in range(DEPTH)]), f)
        c256 = lambda name, k=j: w_in[:, :, OFF[name] + k * 256: OFF[name] + (k + 1) * 256]
        mi = w_in[:, :, OFF["mi"] + j: OFF["mi"] + j + 1]
        mf = w_in[:, :, OFF["mf"] + j: OFF["mf"] + j + 1]
        m["w_tm"] = np.ascontiguousarray(np.concatenate(
            [c256("pu"), c256("mk"), c256("mv"), c256("mo"), c256("mz"), c256("aq"), c256("ak"), c256("av"), c256("az"),
             mi, mf], -1), f)
        m["w_fm"] = np.ascontiguousarray(np.concatenate([c256("pu"), c256("pz"), c256("mq")], -1), f)
        gts = w_in[:, :, OFF["gts"]:].reshape(DEPTH, D, 3, 16, 128)
        m["w_gate"] = np.ascontiguousarray(gts.transpose(0, 2, 3, 1, 4), f)
        m["w_br"] = np.ascontiguousarray(inp["w_branch"].reshape(DEPTH, 3, 1024, 16, 128).transpose(0, 1, 3, 2, 4), f)
        m["w_out"] = np.ascontiguousarray(inp["w_out"].reshape(DEPTH, D, 4, 512).transpose(0, 2, 1, 3), f)
        m["w_pg"] = np.ascontiguousarray(inp["w_ple_gate"].reshape(DEPTH, D, 4, 512).transpose(0, 2, 1, 3), f)
        m["w_pp"] = np.ascontiguousarray(inp["w_ple_proj"].reshape(DEPTH, 256, 4, 512).transpose(0, 2, 1, 3), f)
        m["gn"] = np.ascontiguousarray(inp["norm_mix"], f)
        m["gp"] = np.ascontiguousarray(inp["ple_norm"], f)
        m["wpool"] = np.ascontiguousarray(inp["w_pool_group"][:, j], f)
        m["pscale"] = np.ascontiguousarray(inp["pool_scale"][:, j * 256:(j + 1) * 256].reshape(DEPTH, 2, 128).transpose(0, 2, 1), f)
        sel = np.zeros((128, 4), f)
        sel[:, j] = 1.0
        m["psel"] = sel
        w = POOL_WINDOWS[j]
        rc = np.zeros((2, 128, 128), f)
        rc[0] = (np.float32(1.0) / np.minimum(np.arange(128) + 1, w).astype(f))[None, :]
        rc[1] = np.float32(1.0) / np.float32(w)
        m["prc"] = rc
        bg = np.zeros((DEPTH, 128, 2), f)
        bg[:, :, 0] = inp["b_ig"][:, j][:, None]
        bg[:, :, 1] = inp["b_fg"][:, j][:, None]
        m["bgate"] = bg
        m["mlg"] = np.ascontiguousarray(inp["ml_head_norm"][:, j * 256:(j + 1) * 256], f)
        m["aqg"] = np.ascontiguousarray(np.tile(inp["att_q_norm"], (1, 2)), f)
        m["akg"] = np.ascontiguousarray(np.tile(inp["att_k_norm"], (1, 2)), f)
        ab = np.zeros((DEPTH, 128, 12, 128), f)
        for l in range(DEPTH):
            for h in range(2):
                row = inp["att_rel_bias"][l, 2 * j + h]
                for kb in range(6):
                    v = np.where(bidx[kb] >= 0, row[np.maximum(bidx[kb], 0)], np.float32(NEG))
                    ab[l, :, h * 6 + kb, :] = v
        m["abias"] = ab
        m["cmask"] = cm
        sq = slice(g * NS, (g + 1) * NS)
        sp_ = inp["state_pool"][:, sq, :, j * 256:(j + 1) * 256]
        hist = np.zeros((DEPTH, 128, NS // 2, 2, 2, 16), f)
        hist[..., 1:] = sp_.reshape(DEPTH, NS // 2, 2, 15, 2, 128).transpose(0, 5, 1, 4, 2, 3)
        m["spool"] = hist
        c_ = inp["state_mlstm_c"][:, sq, j]
        m["sC"] = np.ascontiguousarray(c_.transpose(0, 1, 3, 2).reshape(DEPTH, NS, 2, 128, 256), f)
        n_ = inp["state_mlstm_n"][:, sq, j]
        m["sn"] = np.ascontiguousarray(n_.reshape(DEPTH, NS, 2, 128).transpose(0, 1, 3, 2), f)
        m_ = inp["state_mlstm_m"][:, sq, j]
        m["sm"] = np.ascontiguousarray(np.broadcast_to(m_[:, None, :], (DEPTH, 128, NS)), f)
        k_ = inp["cache_att_k"][:, sq, :, 2 * j:2 * j + 2, :]
        m["skT"] = np.ascontiguousarray(k_.transpose(0, 1, 3, 4, 2), f)
        m["sv"] = np.ascontiguousarray(inp["cache_att_v"][:, sq, :, 2 * j:2 * j + 2, :], f)
        maps.append(m)
    return maps


def assemble_outputs(cfg, res):
    f = np.float32
    NG, NPT, NSR, NS, TR = cfg.NG, cfg.NPT, cfg.NSR, cfg.NS, cfg.TR
    SEQ = cfg.SEQ
    NSAMP = NG * NS
    y_p = np.zeros((NG, SEQ, D), f)
    y_s = np.zeros((NSAMP, 64, D), f)
    pool_p = np.zeros((DEPTH, NG, 15, 1024), f)
    pool_s = np.zeros((DEPTH, NSAMP, 15, 1024), f)
    c_p = np.zeros((DEPTH, NG, 4, 256, 256), f)
    c_s = np.zeros((DEPTH, NSAMP, 4, 256, 256), f)
    n_p = np.zeros((DEPTH, NG, 4, 256), f)
    n_s = np.zeros((DEPTH, NSAMP, 4, 256), f)
    m_p = np.zeros((DEPTH, NG, 4), f)
    m_s = np.zeros((DEPTH, NSAMP, 4), f)
    kw = min(512, SEQ)
    k_p = np.zeros((DEPTH, NG, kw, 8, 128), f)
    v_p = np.zeros((DEPTH, NG, kw, 8, 128), f)
    k_s = np.zeros((DEPTH, NSAMP, 64, 8, 128), f)
    v_s = np.zeros((DEPTH, NSAMP, 64, 8, 128), f)
    for core in range(cfg.NCORES):
        g, j = divmod(core, 4)
        r = res[core]
        yo = np.asarray(r["y_own"])
        y_p[g, j * NPT * 128:(j + 1) * NPT * 128] = yo[:NPT * 128]
        y_s[g * NS + j * NSR: g * NS + (j + 1) * NSR] = yo[NPT * 128:].reshape(NSR, 64, D)
        op = np.asarray(r["o_pool"])
        pool_p[:, g, :, j * 256:(j + 1) * 256] = op[:, 0]
        pool_s[:, g * NS:(g + 1) * NS, :, j * 256:(j + 1) * 256] = op[:, 1:]
        oc = np.asarray(r["o_c"]).transpose(0, 1, 3, 2)
        c_p[:, g, j] = oc[:, 0]
        c_s[:, g * NS:(g + 1) * NS, j] = oc[:, 1:]
        on = np.asarray(r["o_n"])
        n_p[:, g, j] = on[:, 0]
        n_s[:, g * NS:(g + 1) * NS, j] = on[:, 1:]
        om = np.asarray(r["o_m"])
        m_p[:, g, j] = om[:, 0]
        m_s[:, g * NS:(g + 1) * NS, j] = om[:, 1:]
        ok = np.asarray(r["o_k"]).reshape(DEPTH, cfg.NKV, 2, 128)
        ov = np.asarray(r["o_v"]).reshape(DEPTH, cfg.NKV, 2, 128)
        k_p[:, g, :, 2 * j:2 * j + 2] = ok[:, :512][:, 512 - kw:]
        v_p[:, g, :, 2 * j:2 * j + 2] = ov[:, :512][:, 512 - kw:]
        k_s[:, g * NS:(g + 1) * NS, :, 2 * j:2 * j + 2] = ok[:, 512:].reshape(DEPTH, NS, 64, 2, 128)
        v_s[:, g * NS:(g + 1) * NS, :, 2 * j:2 * j + 2] = ov[:, 512:].reshape(DEPTH, NS, 64, 2, 128)
    return (y_p, y_s, pool_p, pool_s, c_p, c_s, n_p, n_s, m_p, m_s, k_p, k_s, v_p, v_s)


_CACHE = {}


def run_cfg(cfg, inputs, trace=False):
    key = (cfg.NG, cfg.NPT, cfg.NSR, cfg.NH)
    if key not in _CACHE:
        _CACHE[key] = build_program(cfg)
    nc, stats = _CACHE[key]
    maps = prepare_inputs(cfg, inputs)
    res = run_bass_kernel_spmd(nc, maps, core_ids=list(range(cfg.NCORES)))
    return assemble_outputs(cfg, res.results)


def kernel(**inputs):
    inputs = {k: np.asarray(v) for k, v in inputs.items()}
    cfg = Cfg(NG=2, NPT=16, NSR=4, NH=2)
    return run_cfg(cfg, inputs)
```
